# Optimizing a Trainium2 kernel written in Bass

```python
import math
import jax
import jax.numpy as jnp
from jax import lax
import numpy as np

D_MODEL = 1024
BATCH = 32
SEQ = 2048
DEPTH = 4
DEC_BATCH = 4
DEC_SEQ = 4096
PAST_LEN = 128

N_MIXERS = 4
N_HY = (DEPTH + 3) // 4
N_GDN = (DEPTH + 2) // 4
N_SWA = (DEPTH + 1) // 4
N_MLA = DEPTH // 4
EPS = 1e-6
D_FF = ((8 * D_MODEL + 767) // 768) * 256

HY_SHORT = 3
HY_EMB = 33
HY_BANDS = (HY_EMB - 1) // 2
HY_FILT = 64
HY_TARGET = 1e-2
HY_FAST = 0.3
HY_SLOW = 1.5

GDN_HK = 8
GDN_HV = 16
GDN_DK = 128
GDN_DV = 128
GDN_CONV = 3
GDN_CHUNK = 64
GDN_QKV = 2 * GDN_HK * GDN_DK + GDN_HV * GDN_DV
GDN_IN = GDN_QKV + GDN_HV * GDN_DV

SWA_HQ = 16
SWA_HKV = 4
SWA_DH = 64
SWA_WINDOW = 128
SWA_BLOCK = 128

MLA_H = 16
MLA_NOPE = 64
MLA_ROPE = 32
MLA_DV = 64
MLA_QRANK = 256
MLA_KVRANK = 256
MLA_QBLOCK = 128
ROPE_THETA = 10000.0

kernel_name = 'hybrid_bidir_encoder_hyena_gdn_swa_mla'


def rmsnorm(x, w):
    xf = x.astype(jnp.float32)
    y = xf * lax.rsqrt(jnp.mean(xf * xf, axis=-1, keepdims=True) + EPS)
    return (y * w.astype(jnp.float32)).astype(x.dtype)


def centred_dwconv(x, w, b):
    K = w.shape[0]
    pad = K // 2
    L = x.shape[1]
    xp = jnp.pad(x, ((0, 0), (pad, pad), (0, 0)))
    y = b + w[0] * xp[:, 0:L]
    for j in range(1, K):
        y = y + w[j] * xp[:, j:j + L]
    return y


def swiglu(h, w_gu, w_down):
    gate, up = jnp.split(h @ w_gu, 2, axis=-1)
    return (jax.nn.silu(gate) * up) @ w_down


def hyena_filter(L, fw1, fb1, ff1, fw2, fb2, ff2, fw3):
    f32 = jnp.float32
    pos = jnp.arange(L, dtype=f32)[:, None]
    t = pos / max(L - 1, 1)
    freqs = jnp.linspace(1e-4, HY_BANDS - 1, HY_BANDS, dtype=f32)[None, :]
    ang = freqs * (2.0 * math.pi / L) * pos
    feats = jnp.concatenate([t, jnp.cos(ang), -jnp.sin(ang)], axis=-1)
    z = jnp.sin(ff1.astype(f32) * (feats @ fw1.astype(f32) + fb1.astype(f32)))
    z = jnp.sin(ff2.astype(f32) * (z @ fw2.astype(f32) + fb2.astype(f32)))
    z = z @ fw3.astype(f32)
    rates = jnp.abs(jnp.linspace(math.log(HY_TARGET) / HY_SLOW, math.log(HY_TARGET) / HY_FAST, D_MODEL, dtype=f32))
    window = jnp.exp(-t * rates)
    h_fwd = z[:, :D_MODEL] * window
    h_bwd = z[:, D_MODEL:] * window
    return jnp.concatenate([h_fwd, jnp.zeros((1, D_MODEL), f32), h_bwd[:0:-1]], axis=0)


def hyena_mixer(h, w_in, conv_w, conv_b, fw1, fb1, ff1, fw2, fb2, ff2, fw3, skip, w_out):
    B, L, _ = h.shape
    u = centred_dwconv(h @ w_in, conv_w, conv_b)
    x0, x1, v = jnp.split(u, 3, axis=-1)
    v = (v * x1).astype(jnp.float32)
    kern_f = jnp.fft.rfft(hyena_filter(L, fw1, fb1, ff1, fw2, fb2, ff2, fw3), axis=0)
    v_f = jnp.fft.rfft(v, n=2 * L, axis=1)
    y = jnp.fft.irfft(v_f * kern_f, n=2 * L, axis=1)[:, :L] + v * skip.astype(jnp.float32)
    return (y.astype(h.dtype) * x0) @ w_out


def l2norm(x):
    xf = x.astype(jnp.float32)
    return xf * lax.rsqrt(jnp.sum(xf * xf, axis=-1, keepdims=True) + EPS)


def chunk_gated_delta(q, k, v, g, beta):
    f32 = jnp.float32
    B, L, H, DK = q.shape
    DV = v.shape[-1]
    C = GDN_CHUNK
    N = L // C

    def blocks(t):
        t = t.astype(f32).reshape((B, N, C, H) + t.shape[3:])
        return jnp.swapaxes(t, 2, 3)

    q, k, v, g, beta = blocks(q), blocks(k), blocks(v), blocks(g), blocks(beta)
    gc = jnp.cumsum(g, axis=-1)
    incl = jnp.tril(jnp.ones((C, C), dtype=bool))
    strict = jnp.tril(jnp.ones((C, C), dtype=bool), -1)
    decay = jnp.exp(jnp.where(incl, gc[..., :, None] - gc[..., None, :], -jnp.inf))
    k_beta = k * beta[..., None]
    a_mat = jnp.where(strict, jnp.einsum('bnhik,bnhjk->bnhij', k_beta, k) * decay, 0.0)
    rhs = jnp.concatenate([v * beta[..., None], k_beta * jnp.exp(gc)[..., None]], axis=-1)
    sol = lax.linalg.triangular_solve(a_mat, rhs, left_side=True, lower=True, unit_diagonal=True)
    u, w = sol[..., :DV], sol[..., DV:]
    intra = jnp.einsum('bnhik,bnhjk->bnhij', q, k) * decay
    q_dec = q * jnp.exp(gc)[..., None]
    k_dec = k * jnp.exp(gc[..., -1:] - gc)[..., None]
    g_end = jnp.exp(gc[..., -1])

    def step(S, xs):
        qd, kd, u_c, w_c, a_c, ge = xs
        v_new = u_c - jnp.einsum('bhck,bhkv->bhcv', w_c, S)
        o = jnp.einsum('bhck,bhkv->bhcv', qd, S) + jnp.einsum('bhij,bhjv->bhiv', a_c, v_new)
        S = S * ge[..., None, None] + jnp.einsum('bhck,bhcv->bhkv', kd, v_new)
        return S, o

    xs = tuple(jnp.moveaxis(t, 1, 0) for t in (q_dec, k_dec, u, w, intra, g_end))
    _, o = lax.scan(step, jnp.zeros((B, H, DK, DV), f32), xs)
    return jnp.swapaxes(jnp.moveaxis(o, 0, 1), 2, 3).reshape(B, L, H, DV)


def gdn_mixer(h, w_in, conv_w, conv_b, w_ab, a_log, dt_bias, norm_w, w_out):
    B, L, _ = h.shape
    proj = h @ w_in
    qkv = jax.nn.silu(centred_dwconv(proj[..., :GDN_QKV], conv_w, conv_b))
    z = proj[..., GDN_QKV:].reshape(B, L, GDN_HV, GDN_DV)
    nq = GDN_HK * GDN_DK
    q = l2norm(qkv[..., :nq].reshape(B, L, GDN_HK, GDN_DK)) * (GDN_DK ** -0.5)
    k = l2norm(qkv[..., nq:2 * nq].reshape(B, L, GDN_HK, GDN_DK))
    v = qkv[..., 2 * nq:].reshape(B, L, GDN_HV, GDN_DV)
    rep = GDN_HV // GDN_HK
    q = jnp.repeat(q, rep, axis=2)
    k = jnp.repeat(k, rep, axis=2)
    ab = (h @ w_ab).astype(jnp.float32)
    a = ab[..., :2 * GDN_HV].reshape(B, L, 2, GDN_HV)
    bb = ab[..., 2 * GDN_HV:].reshape(B, L, 2, GDN_HV)
    g = -jnp.exp(a_log.astype(jnp.float32)) * jax.nn.softplus(a + dt_bias.astype(jnp.float32))
    beta = jax.nn.sigmoid(bb)
    o_fwd = chunk_gated_delta(q, k, v, g[:, :, 0], beta[:, :, 0])
    flip = lambda t: jnp.flip(t, axis=1)
    o_bwd = flip(chunk_gated_delta(flip(q), flip(k), flip(v), flip(g[:, :, 1]), flip(beta[:, :, 1])))
    o = (o_fwd + o_bwd).astype(h.dtype)
    o = rmsnorm(o, norm_w) * jax.nn.silu(z)
    return o.reshape(B, L, GDN_HV * GDN_DV) @ w_out


def swa_mixer(h, w_qkv, sink, w_out):
    f32 = jnp.float32
    B, L, _ = h.shape
    W = SWA_BLOCK
    NB = L // W
    G = SWA_HQ // SWA_HKV
    qkv = h @ w_qkv
    nq = SWA_HQ * SWA_DH
    nk = SWA_HKV * SWA_DH
    q = qkv[..., :nq].reshape(B, NB, W, SWA_HKV, G, SWA_DH)
    k = qkv[..., nq:nq + nk].reshape(B, L, SWA_HKV, SWA_DH)
    v = qkv[..., nq + nk:].reshape(B, L, SWA_HKV, SWA_DH)

    def band(t):
        tp = jnp.pad(t, ((0, 0), (W, W), (0, 0), (0, 0))).reshape(B, NB + 2, W, SWA_HKV, SWA_DH)
        return jnp.concatenate([tp[:, :-2], tp[:, 1:-1], tp[:, 2:]], axis=2)

    kb, vb = band(k), band(v)
    rel = jnp.arange(3 * W)[None, :] - W - jnp.arange(W)[:, None]
    key_pos = (jnp.arange(NB)[:, None] - 1) * W + jnp.arange(3 * W)[None, :]
    valid = (jnp.abs(rel) <= SWA_WINDOW)[None] & ((key_pos >= 0) & (key_pos < L))[:, None, :]
    slopes = (2.0 ** (-8.0 * jnp.arange(1, SWA_HQ + 1, dtype=f32) / SWA_HQ)).reshape(SWA_HKV, G)
    dist = jnp.abs(rel).astype(f32)
    s = jnp.einsum('bnqhgd,bnkhd->bnhgqk', q, kb).astype(f32) * (SWA_DH ** -0.5)
    s = s - slopes[:, :, None, None] * dist
    s = jnp.where(valid[:, None, None], s, -jnp.inf)
    sink_l = sink.astype(f32).reshape(SWA_HKV, G)[:, :, None, None]
    m = jnp.maximum(jnp.max(s, axis=-1, keepdims=True), sink_l)
    e = jnp.exp(s - m)
    p = e / (jnp.sum(e, axis=-1, keepdims=True) + jnp.exp(sink_l - m))
    o = jnp.einsum('bnhgqk,bnkhd->bnqhgd', p.astype(vb.dtype), vb).reshape(B, L, nq)
    return o @ w_out


def rope_tables(L):
    inv = ROPE_THETA ** (-jnp.arange(0, MLA_ROPE, 2, dtype=jnp.float32) / MLA_ROPE)
    ang = jnp.arange(L, dtype=jnp.float32)[:, None] * inv[None, :]
    return jnp.cos(ang), jnp.sin(ang)


def apply_rope(x, cos, sin):
    x1, x2 = jnp.split(x, 2, axis=-1)
    cos = cos.astype(x.dtype)
    sin = sin.astype(x.dtype)
    return jnp.concatenate([x1 * cos - x2 * sin, x2 * cos + x1 * sin], axis=-1)


def mla_mixer(h, w_down, q_norm_w, w_uq, kv_norm_w, w_ukv, w_out):
    B, L, _ = h.shape
    d = h @ w_down
    cq = rmsnorm(d[..., :MLA_QRANK], q_norm_w)
    ckv = rmsnorm(d[..., MLA_QRANK:MLA_QRANK + MLA_KVRANK], kv_norm_w)
    k_rope = d[..., MLA_QRANK + MLA_KVRANK:]
    q = (cq @ w_uq).reshape(B, L, MLA_H, MLA_NOPE + MLA_ROPE)
    kv = (ckv @ w_ukv).reshape(B, L, MLA_H, MLA_NOPE + MLA_DV)
    k_nope, v = kv[..., :MLA_NOPE], kv[..., MLA_NOPE:]
    cos, sin = rope_tables(L)
    q_nope = q[..., :MLA_NOPE]
    q_rope = apply_rope(q[..., MLA_NOPE:], cos[:, None, :], sin[:, None, :])
    k_rope = apply_rope(k_rope, cos, sin)
    scale = (MLA_NOPE + MLA_ROPE) ** -0.5
    NB = L // MLA_QBLOCK
    qn = jnp.moveaxis(q_nope.reshape(B, NB, MLA_QBLOCK, MLA_H, MLA_NOPE), 1, 0)
    qr = jnp.moveaxis(q_rope.reshape(B, NB, MLA_QBLOCK, MLA_H, MLA_ROPE), 1, 0)

    def block(args):
        qn_b, qr_b = args
        s = (jnp.einsum('bqhd,bkhd->bhqk', qn_b, k_nope) + jnp.einsum('bqhd,bkd->bhqk', qr_b, k_rope)).astype(jnp.float32) * scale
        p = jax.nn.softmax(s, axis=-1)
        return jnp.einsum('bhqk,bkhd->bqhd', p.astype(v.dtype), v)

    o = lax.map(block, (qn, qr))
    o = jnp.moveaxis(o, 0, 1).reshape(B, L, MLA_H * MLA_DV)
    return o @ w_out


def encoder_trunk(x, c, ada_w, ada_b, norm_w, hy, gdn, swa, mla, ffn_w_gu, ffn_w_down, final_norm_w):
    c_act = jax.nn.silu(c)
    for i in range(DEPTH):
        kind, j = i % N_MIXERS, i // N_MIXERS
        mod = (c_act @ ada_w[i] + ada_b[i])[:, None, :]
        sh1, sc1, g1, sh2, sc2, g2 = jnp.split(mod, 6, axis=-1)
        h = rmsnorm(x, norm_w[i, 0]) * (1.0 + sc1) + sh1
        if kind == 0:
            out = hyena_mixer(h, *[p[j] for p in hy])
        elif kind == 1:
            out = gdn_mixer(h, *[p[j] for p in gdn])
        elif kind == 2:
            out = swa_mixer(h, *[p[j] for p in swa])
        else:
            out = mla_mixer(h, *[p[j] for p in mla])
        x = x + g1 * out
        h = rmsnorm(x, norm_w[i, 1]) * (1.0 + sc2) + sh2
        x = x + g2 * swiglu(h, ffn_w_gu[i], ffn_w_down[i])
    return rmsnorm(x, final_norm_w)


def setup_inputs(seed: int = 0) -> dict:
    key = jax.random.key(seed)
    keys = iter(jax.random.split(key, 64))
    f32 = jnp.float32

    def nrm(shape, scale):
        return jax.random.normal(next(keys), shape, f32) * scale

    def gain(shape):
        return 1.0 + nrm(shape, 0.01)

    D = D_MODEL
    dt = jnp.exp(jax.random.uniform(next(keys), (N_GDN, 2, GDN_HV), f32, math.log(1e-3), math.log(1e-1)))
    return {
        'x_prompt': nrm((BATCH, SEQ, D), 1.0),
        'x_sample': nrm((DEC_BATCH, DEC_SEQ, D), 1.0),
        'c_prompt': nrm((BATCH, D), 1.0),
        'c_sample': nrm((DEC_BATCH, D), 1.0),
        'ada_w': nrm((DEPTH, D, 6 * D), 0.5 * D ** -0.5),
        'ada_b': nrm((DEPTH, 6 * D), 0.01),
        'norm_w': gain((DEPTH, 2, D)),
        'hy_w_in': nrm((N_HY, D, 3 * D), D ** -0.5),
        'hy_conv_w': nrm((N_HY, HY_SHORT, 3 * D), HY_SHORT ** -0.5),
        'hy_conv_b': nrm((N_HY, 3 * D), 0.01),
        'hy_filt_w1': nrm((N_HY, HY_EMB, HY_FILT), HY_EMB ** -0.5),
        'hy_filt_b1': nrm((N_HY, HY_FILT), 0.01),
        'hy_filt_freq1': gain((N_HY, HY_FILT)),
        'hy_filt_w2': nrm((N_HY, HY_FILT, HY_FILT), HY_FILT ** -0.5),
        'hy_filt_b2': nrm((N_HY, HY_FILT), 0.01),
        'hy_filt_freq2': gain((N_HY, HY_FILT)),
        'hy_filt_w3': nrm((N_HY, HY_FILT, 2 * D), 0.008),
        'hy_skip': nrm((N_HY, D), 1.0),
        'hy_w_out': nrm((N_HY, D, D), D ** -0.5),
        'gdn_w_in': nrm((N_GDN, D, GDN_IN), D ** -0.5),
        'gdn_conv_w': nrm((N_GDN, GDN_CONV, GDN_QKV), GDN_CONV ** -0.5),
        'gdn_conv_b': nrm((N_GDN, GDN_QKV), 0.01),
        'gdn_w_ab': nrm((N_GDN, D, 4 * GDN_HV), D ** -0.5),
        'gdn_a_log': jnp.log(jax.random.uniform(next(keys), (N_GDN, 2, GDN_HV), f32, 1.0, 16.0)),
        'gdn_dt_bias': dt + jnp.log(-jnp.expm1(-dt)),
        'gdn_norm_w': gain((N_GDN, GDN_DV)),
        'gdn_w_out': nrm((N_GDN, GDN_HV * GDN_DV, D), (GDN_HV * GDN_DV) ** -0.5),
        'swa_w_qkv': nrm((N_SWA, D, (SWA_HQ + 2 * SWA_HKV) * SWA_DH), D ** -0.5),
        'swa_sink': nrm((N_SWA, SWA_HQ), 1.0),
        'swa_w_out': nrm((N_SWA, SWA_HQ * SWA_DH, D), (SWA_HQ * SWA_DH) ** -0.5),
        'mla_w_down': nrm((N_MLA, D, MLA_QRANK + MLA_KVRANK + MLA_ROPE), D ** -0.5),
        'mla_q_norm_w': gain((N_MLA, MLA_QRANK)),
        'mla_w_uq': nrm((N_MLA, MLA_QRANK, MLA_H * (MLA_NOPE + MLA_ROPE)), MLA_QRANK ** -0.5),
        'mla_kv_norm_w': gain((N_MLA, MLA_KVRANK)),
        'mla_w_ukv': nrm((N_MLA, MLA_KVRANK, MLA_H * (MLA_NOPE + MLA_DV)), MLA_KVRANK ** -0.5),
        'mla_w_out': nrm((N_MLA, MLA_H * MLA_DV, D), (MLA_H * MLA_DV) ** -0.5),
        'ffn_w_gu': nrm((DEPTH, D, 2 * D_FF), D ** -0.5),
        'ffn_w_down': nrm((DEPTH, D_FF, D), D_FF ** -0.5),
        'final_norm_w': gain((D,)),
    }


def reference(x_prompt, x_sample, c_prompt, c_sample, ada_w, ada_b, norm_w,
              hy_w_in, hy_conv_w, hy_conv_b, hy_filt_w1, hy_filt_b1, hy_filt_freq1,
              hy_filt_w2, hy_filt_b2, hy_filt_freq2, hy_filt_w3, hy_skip, hy_w_out,
              gdn_w_in, gdn_conv_w, gdn_conv_b, gdn_w_ab, gdn_a_log, gdn_dt_bias, gdn_norm_w, gdn_w_out,
              swa_w_qkv, swa_sink, swa_w_out,
              mla_w_down, mla_q_norm_w, mla_w_uq, mla_kv_norm_w, mla_w_ukv, mla_w_out,
              ffn_w_gu, ffn_w_down, final_norm_w):
    hy = (hy_w_in, hy_conv_w, hy_conv_b, hy_filt_w1, hy_filt_b1, hy_filt_freq1,
          hy_filt_w2, hy_filt_b2, hy_filt_freq2, hy_filt_w3, hy_skip, hy_w_out)
    gdn = (gdn_w_in, gdn_conv_w, gdn_conv_b, gdn_w_ab, gdn_a_log, gdn_dt_bias, gdn_norm_w, gdn_w_out)
    swa = (swa_w_qkv, swa_sink, swa_w_out)
    mla = (mla_w_down, mla_q_norm_w, mla_w_uq, mla_kv_norm_w, mla_w_ukv, mla_w_out)
    y_prompt = encoder_trunk(x_prompt, c_prompt, ada_w, ada_b, norm_w, hy, gdn, swa, mla,
                             ffn_w_gu, ffn_w_down, final_norm_w)
    y_sample = encoder_trunk(x_sample, c_sample, ada_w, ada_b, norm_w, hy, gdn, swa, mla,
                             ffn_w_gu, ffn_w_down, final_norm_w)
    return (y_prompt, y_sample)
```

```python
import math
import os
from contextlib import ExitStack

import numpy as np
import ml_dtypes
import concourse.bass as bass
import concourse.mybir as mybir
from concourse.bass_utils import run_bass_kernel_spmd

F32 = mybir.dt.float32
BF16 = mybir.dt.bfloat16
AF = mybir.ActivationFunctionType
ALU = mybir.AluOpType
AX = mybir.AxisListType

D = 1024
DFF = 2816
EPS = 1e-6
NCORES = 8
NS_DMA = 12


class Buf:
    __slots__ = ("name", "w", "rc", "rd")

    def __init__(self, name=""):
        self.name = name
        self.w = None
        self.rc = {}
        self.rd = []


class Op:
    __slots__ = ("eng", "idx", "fn", "deps", "signal", "sig", "isdma", "k")


ENGS = ("pe", "act", "dve", "pool", "sp")


class Prog:
    def __init__(self):
        self.streams = {e: [] for e in ENGS}
        self.dmas = {"sp": [], "pool": []}

    def op(self, eng, fn, reads=(), writes=(), dma=False, extra=()):
        o = Op()
        o.eng = eng
        o.fn = fn
        o.isdma = dma
        o.signal = dma
        o.idx = len(self.streams[eng])
        o.sig = None
        ds = set(extra)
        for b in reads:
            if b.w is not None:
                ds.add(b.w)
        for b in writes:
            if b.w is not None:
                ds.add(b.w)
            for r in b.rc.values():
                ds.add(r)
            for r in b.rd:
                ds.add(r)
        keep = []
        for d in ds:
            if d is o:
                continue
            if d.eng == eng and not d.isdma and not dma:
                if eng == "pe" or (o.idx - d.idx) > 2:
                    continue
            d.signal = True
            keep.append(d)
        o.deps = keep
        for b in reads:
            if dma:
                b.rd.append(o)
            else:
                b.rc[eng] = o
        for b in writes:
            b.w = o
            b.rc = {}
            b.rd = []
        self.streams[eng].append(o)
        if dma:
            o.k = len(self.dmas[eng])
            self.dmas[eng].append(o)
        return o

    def barrier(self):
        last = []
        for e in ENGS:
            if self.streams[e]:
                last.append(self.streams[e][-1])
        for q in ("sp", "pool"):
            last.extend(self.dmas[q][-NS_DMA:])
        for e in ENGS:
            self.op(e, lambda g: g.nop(), extra=[x for x in last])

    def finish(self):
        self.barrier()

    def emit(self, nc, stk):
        sems = {e: stk.enter_context(nc.semaphore("s_" + e)) for e in ENGS}
        dsem = {q: [stk.enter_context(nc.semaphore("d_%s%d" % (q, i))) for i in range(NS_DMA)]
                for q in ("sp", "pool")}
        for e in ENGS:
            cnt = 0
            for o in self.streams[e]:
                if o.isdma:
                    o.sig = (dsem[e][o.k % NS_DMA], 16 * (o.k // NS_DMA + 1))
                elif o.signal:
                    cnt += 1
                    o.sig = (sems[e], cnt)
        block = stk.enter_context(nc.Block())

        def make(e):
            ops = self.streams[e]

            def body(g):
                waited = {}
                for o in ops:
                    for d in o.deps:
                        s, v = d.sig
                        if waited.get(id(s), 0) < v:
                            g.wait_ge(s, v)
                            waited[id(s)] = v
                    if o.isdma and o.k >= NS_DMA:
                        s = dsem[e][o.k % NS_DMA]
                        v = 16 * (o.k // NS_DMA)
                        if waited.get(id(s), 0) < v:
                            g.wait_ge(s, v)
                            waited[id(s)] = v
                    ins = o.fn(g)
                    if o.isdma:
                        ins.then_inc(o.sig[0], 16)
                    elif o.signal:
                        ins.then_inc(o.sig[0], 1)
            return body

        block.tensor(make("pe"))
        block.scalar(make("act"))
        block.vector(make("dve"))
        block.gpsimd(make("pool"))
        block.sync(make("sp"))


class Cfg:
    def __init__(self, seqs, kinds):
        self.seqs = list(seqs)
        self.kinds = list(kinds)
        self.depth = len(kinds)
        self.nseq = len(seqs)
        self.ntok = sum(seqs)
        self.lmax = max(seqs)
        self.offs = [sum(seqs[:i]) for i in range(len(seqs))]


WEIGHT_SHAPES = {
    "swa_w_qkv": (1024, 1536), "swa_w_out": (1024, 1024),
    "mla_w_down": (1024, 544), "mla_w_uq": (256, 1536), "mla_w_ukv": (256, 2048), "mla_w_out": (1024, 1024),
    "hy_w_in": (1024, 3072), "hy_w_out": (1024, 1024),
    "gdn_w_in": (1024, 6144), "gdn_w_ab": (1024, 64), "gdn_w_out": (2048, 1024),
    "ffn_w_gu": (1024, 5632), "ffn_w_down": (2816, 1024),
}


class Builder:
    def __init__(self, cfg):
        self.cfg = cfg
        self.nc = bass.Bass("TRN2", target_bir_lowering=False)
        self.P = Prog()
        self.stk = ExitStack()
        self.dram = {}
        self.dbuf = {}
        self.uid = 0

    def din(self, name, shape, dt=F32):
        t = self.nc.dram_tensor(name, list(shape), dt, kind="ExternalInput").ap()
        self.dram[name] = t
        self.dbuf[name] = Buf(name)
        return t

    def dscratch(self, name, shape, dt):
        t = self.nc.dram_tensor(name, list(shape), dt, kind="Internal").ap()
        self.dram[name] = t
        self.dbuf[name] = Buf(name)
        return t

    def sb(self, stk, shape, dt, name=None):
        self.uid += 1
        return stk.enter_context(self.nc.sbuf_tensor("%s_%d" % (name or "t", self.uid), list(shape), dt))

    def ps(self, stk, shape, dt, name=None):
        self.uid += 1
        return stk.enter_context(self.nc.psum_tensor("%s_%d" % (name or "p", self.uid), list(shape), dt))

    def dma(self, out, in_, reads, writes, q="sp", slow=False, **kw):
        if slow:
            kw["allow_slow_non_contiguous"] = True
        dset = set(id(b) for b in self.dbuf.values())
        reads = [b for b in reads if id(b) not in dset]
        writes = [b for b in writes if id(b) not in dset]
        return self.P.op(q, lambda g: g.dma_start(out=out, in_=in_, **kw), reads=reads, writes=writes, dma=True)


    def mm(self, out, lhsT, rhs, start, stop, reads, writes):
        return self.P.op("pe", lambda g: g.matmul(out, lhsT, rhs, start=start, stop=stop), reads=reads, writes=writes)

    def tr(self, out, in_, ident, reads, writes):
        return self.P.op("pe", lambda g: g.transpose(out=out, in_=in_, identity=ident), reads=reads, writes=writes)

    def actf(self, out, in_, func, reads, writes, bias=None, scale=None, accum_out=None):
        kw = {}
        if bias is not None:
            kw["bias"] = bias
        if scale is not None:
            kw["scale"] = scale
        if accum_out is not None:
            kw["accum_out"] = accum_out
        return self.P.op("act", lambda g: g.activation(out=out, in_=in_, func=func, **kw), reads=reads, writes=writes)

    def cp(self, eng, out, in_, reads, writes):
        if eng == "act":
            return self.P.op("act", lambda g: g.copy(out=out, in_=in_), reads=reads, writes=writes)
        return self.P.op(eng, lambda g: g.tensor_copy(out=out, in_=in_), reads=reads, writes=writes)

    def tt(self, eng, out, in0, in1, op, reads, writes):
        return self.P.op(eng, lambda g: g.tensor_tensor(out=out, in0=in0, in1=in1, op=op), reads=reads, writes=writes)

    def ts(self, eng, out, in0, s1, s2, op0, op1, reads, writes):
        if s2 is None:
            return self.P.op(eng, lambda g: g.tensor_scalar(out=out, in0=in0, scalar1=s1, scalar2=None, op0=op0),
                             reads=reads, writes=writes)
        return self.P.op(eng, lambda g: g.tensor_scalar(out=out, in0=in0, scalar1=s1, scalar2=s2, op0=op0, op1=op1),
                         reads=reads, writes=writes)

    def stt(self, eng, out, in0, scalar, in1, op0, op1, reads, writes):
        return self.P.op(eng, lambda g: g.scalar_tensor_tensor(out=out, in0=in0, scalar=scalar, in1=in1, op0=op0, op1=op1),
                         reads=reads, writes=writes)

    def mset(self, eng, ap, val, writes):
        return self.P.op(eng, lambda g: g.memset(ap, val), writes=writes)

    def recip(self, out, in_, reads, writes):
        return self.P.op("dve", lambda g: g.reciprocal(out=out, in_=in_), reads=reads, writes=writes)

    def amul(self, out, in_, mul, reads, writes):
        return self.P.op("act", lambda g: g.mul(out=out, in_=in_, mul=mul), reads=reads, writes=writes)

    def build(self):
        cfg = self.cfg
        nc = self.nc
        P = self.P
        with self.stk:
            self._build()
            P.finish()
            P.emit(nc, self.stk)
        return nc

    def _build(self):
        cfg = self.cfg
        nc = self.nc
        P = self.P
        NT, NSQ, DEP = cfg.ntok, cfg.nseq, cfg.depth
        self.xin = self.din("xin", [NT, D])
        self.cin = self.din("cin", [NSQ, D])
        self.ada_w = self.din("ada_w", [DEP, D, 6 * D])
        self.ada_b = self.din("ada_b", [DEP, 6 * D])
        self.norm_w = self.din("norm_w", [DEP, 2, D])
        self.final_norm_w = self.din("final_norm_w", [1, D])
        self.ffn_w_gu = self.din("ffn_w_gu", [DEP, D, 2 * DFF])
        self.ffn_w_down = self.din("ffn_w_down", [DEP, DFF, D])
        self.yout = self.nc.dram_tensor("yout", [NT, D], F32, kind="ExternalOutput").ap()
        self.dbuf["yout"] = Buf("yout")
        self.xs = self.dscratch("xs", [NT, D], F32)
        self.hTd = self.dscratch("hTd", [D, cfg.lmax + 2], BF16)
        if getattr(cfg, "debug", False):
            self.oTd = self.nc.dram_tensor("oTd", [2048, cfg.lmax], BF16, kind="ExternalOutput").ap()
            self.hTd_dbg = True
        else:
            self.oTd = self.dscratch("oTd", [2048, cfg.lmax], BF16)
        self.dbuf["oTd"] = Buf("oTd")
        self.modd = self.dscratch("modd", [DEP, NSQ, 6 * D], F32)
        self.wb_gu = self.dscratch("wb_gu", [DEP, 11, 128, 8 * 2 * 256], BF16)
        self.wb_down = self.dscratch("wb_down", [DEP, 2, 128, 22 * 512], BF16)
        kinds = set(cfg.kinds)
        self.w = {}
        self.wb = {}
        if 2 in kinds:
            for n in ("swa_w_qkv", "swa_w_out"):
                r, c = WEIGHT_SHAPES[n]
                self.w[n] = self.din(n, [r, c])
                self.wb[n] = self.dscratch("wb_" + n, [r, c], BF16)
            self.swa_sink = self.din("swa_sink", [1, 16])
            self.swa_bias = self.din("swa_bias", [4, 3, 128, 512])
        if 3 in kinds:
            for n in ("mla_w_down", "mla_w_uq", "mla_w_ukv", "mla_w_out"):
                r, c = WEIGHT_SHAPES[n]
                self.w[n] = self.din(n, [r, c])
                self.wb[n] = self.dscratch("wb_" + n, [r, c], BF16)
            self.mla_qnw = self.din("mla_q_norm_w", [1, 256])
            self.mla_kvnw = self.din("mla_kv_norm_w", [1, 256])
            self.rope_cs = self.din("rope_cs", [2, 96, cfg.lmax])
        self.hy_done = set()
        if 0 in kinds:
            for n in ("hy_w_in", "hy_w_out"):
                r, c = WEIGHT_SHAPES[n]
                self.w[n] = self.din(n, [r, c])
                self.wb[n] = self.dscratch("wb_" + n, [r, c], BF16)
            self.hy_conv_w = self.din("hy_conv_w", [3, 3072])
            self.hy_conv_b = self.din("hy_conv_b", [1, 3072])
            self.hy_fw1 = self.din("hy_filt_w1", [33, 64])
            self.hy_fb1 = self.din("hy_filt_b1", [1, 64])
            self.hy_ff1 = self.din("hy_filt_freq1", [1, 64])
            self.hy_fw2 = self.din("hy_filt_w2", [64, 64])
            self.hy_fb2 = self.din("hy_filt_b2", [1, 64])
            self.hy_ff2 = self.din("hy_filt_freq2", [1, 64])
            self.hy_fw3 = self.din("hy_filt_w3", [64, 2048])
            self.hy_skip = self.din("hy_skip", [1, 1024])
            self.hy_rates = self.din("hy_rates", [1, 1024])
            self.hyc, self.hy_hs, self.hy_hd, self.hy_kre, self.hy_kim, self.hy_tc, self.hy_ts = {}, {}, {}, {}, {}, {}, {}
            self.hy_tabsrc = {}
            for L in sorted(set(cfg.seqs)):
                TCN, FC, F = self.hy_dims(L)
                self.hyc[L] = {"tn": self.din("hy_tn%d" % L, [128, TCN]), "feats": self.din("hy_feats%d" % L, [33, L]),
                               "wN": self.din("hy_wN%d" % L, [128, FC])}
                self.hy_tabsrc[L] = (self.din("hy_cos%d" % L, [F, F]), self.din("hy_sin%d" % L, [F, F]))
                NCB = (F + 255) // 256
                self.hy_tc[L] = self.dscratch("hy_tcb%d" % L, [NCB, 128, FC * 256], BF16)
                self.hy_ts[L] = self.dscratch("hy_tsb%d" % L, [NCB, 128, FC * 256], BF16)
                self.dbuf["hy_tab%d" % L] = Buf("hy_tab")
                self.hy_hs[L] = self.dscratch("hy_hs%d" % L, [L, 1024], BF16)
                self.hy_hd[L] = self.dscratch("hy_hd%d" % L, [L, 1024], BF16)
                self.hy_kre[L] = self.dscratch("hy_kre%d" % L, [F, 1024], F32)
                self.hy_kim[L] = self.dscratch("hy_kim%d" % L, [F, 1024], F32)
                self.dbuf["hy_K%d" % L] = Buf("hy_K")
        if 1 in kinds:
            for n in ("gdn_w_in", "gdn_w_ab", "gdn_w_out"):
                r, c = WEIGHT_SHAPES[n]
                self.w[n] = self.din(n, [r, c])
                self.wb[n] = self.dscratch("wb_" + n, [r, c], BF16)
            self.gdn_conv_w = self.din("gdn_conv_w", [3, 4096])
            self.gdn_conv_b = self.din("gdn_conv_b", [1, 4096])
            self.gdn_a_log = self.din("gdn_a_log", [1, 32])
            self.gdn_dt_bias = self.din("gdn_dt_bias", [1, 32])
            self.gdn_norm_w = self.din("gdn_norm_w", [1, 128])
            self.gdn_c = self.din("gdn_c", [9, 128, 128])
            self.gdn_esel = self.din("gdn_esel", [48, 32, 128])
        self.xbuf = {}
        stk = self.stk
        self.ident = self.sb(stk, [128, 128], BF16, "ident")
        self.identf = self.sb(stk, [128, 128], F32, "identf")
        self.ones_bf = self.sb(stk, [128, 128], BF16, "ones")
        self.b_const = Buf("const")
        cident = self.din("c_ident", [128, 128])
        self.dma(self.identf[:], cident[:, :], [], [self.b_const])
        self.dma(self.ident[:], cident[:, :], [], [self.b_const], q="pool")
        P.op("dve", lambda g: g.memset(self.ones_bf[:], 1.0), writes=[self.b_const])
        self.zcol = self.sb(stk, [128, 8, 2], BF16, "zcol")
        P.op("dve", lambda g: g.memset(self.zcol[:], 0.0), writes=[self.b_const])
        self.psf = [(self.ps(stk, [128, 512], F32, "psf"), Buf("psf%d" % i)) for i in range(4)]
        self.psacc = [(self.ps(stk, [128, 512], F32, "psa"), Buf("psa%d" % i)) for i in range(2)]
        self.psb = [(self.ps(stk, [128, 1024], BF16, "psb"), Buf("psb%d" % i)) for i in range(2)]
        self.psf_i = 0
        self.psb_i = 0

        self.convert_weights()
        self.compute_mod()
        for l in range(DEP):
            kind = cfg.kinds[l]
            for s in range(NSQ):
                self.stage_a(l, s)
                if kind == 2:
                    self.swa(l, s)
                elif kind == 3:
                    self.mla(l, s)
                elif kind == 1:
                    self.gdn(l, s)
                elif kind == 0:
                    if (l, cfg.seqs[s]) not in self.hy_done:
                        self.hy_done.add((l, cfg.seqs[s]))
                        self.hyena_filter(cfg.seqs[s])
                    self.hyena(l, s)
                else:
                    raise NotImplementedError
                self.stage_c(l, s, kind)

    def next_psf(self):
        r = self.psf[self.psf_i % len(self.psf)]
        self.psf_i += 1
        return r

    def next_psb(self):
        r = self.psb[self.psb_i % len(self.psb)]
        self.psb_i += 1
        return r

    def convert_weights(self):
        P = self.P
        cfg = self.cfg
        P.barrier()
        with ExitStack() as stk:
            NB = 3
            tiles = [(self.sb(stk, [128, 8192], BF16, "cv"), Buf("cv%d" % i)) for i in range(NB)]
            cnt = [0]

            def conv2d(src, dst, rows, cols, bsrc, bdst):
                k = max(1, 8192 // cols)
                r0 = 0
                while r0 < rows:
                    kk = min(k, (rows - r0) // 128)
                    t, b = tiles[cnt[0] % NB]
                    cnt[0] += 1
                    tv = t[:, 0:kk * cols].rearrange("p (k c) -> p k c", k=kk)
                    sv = src[r0:r0 + 128 * kk, :].rearrange("(k p) c -> p k c", p=128)
                    dv = dst[r0:r0 + 128 * kk, :].rearrange("(k p) c -> p k c", p=128)
                    self.dma(tv, sv, [bsrc], [b], q="pool")
                    self.dma(dv, tv, [b], [bdst], q="sp")
                    r0 += 128 * kk

            for l in range(cfg.depth):
                gdst = self.wb_gu[l].rearrange("g p (k u c) -> p g k u c", k=8, u=2)
                for k in range(8):
                    t, b = tiles[cnt[0] % NB]
                    cnt[0] += 1
                    self.dma(t[:, 0:2 * DFF], self.ffn_w_gu[l][k * 128:(k + 1) * 128, :], [], [b], q="pool")
                    tv = t[:, 0:2 * DFF].rearrange("p (u g c) -> p u g c", u=2, g=11)
                    for u in range(2):
                        self.dma(gdst[:, :, k, u, :], tv[:, u, :, :], [b], [], q="sp")
                ddst = self.wb_down[l].rearrange("h p (f c) -> p h f c", f=22)
                f0 = 0
                while f0 < 22:
                    kk = min(8, 22 - f0)
                    t, b = tiles[cnt[0] % NB]
                    cnt[0] += 1
                    tv = t[:, 0:kk * D].rearrange("p (k c) -> p k c", k=kk)
                    self.dma(tv, self.ffn_w_down[l][f0 * 128:(f0 + kk) * 128, :].rearrange("(k p) c -> p k c", p=128),
                             [], [b], q="pool")
                    for h in range(2):
                        self.dma(ddst[:, h, f0:f0 + kk, :], tv[:, :, h * 512:(h + 1) * 512], [b], [], q="sp")
                    f0 += kk
            for n in self.w:
                r, c = WEIGHT_SHAPES[n]
                conv2d(self.w[n], self.wb[n], r, c, self.dbuf[n], self.dbuf["wb_" + n])
            if 0 in set(cfg.kinds):
                for L in sorted(set(cfg.seqs)):
                    TCN, FC, F = self.hy_dims(L)
                    bsrc = Buf("tabsrc")
                    NCBf = F // 256
                    for ti_ in range(2):
                        srct = self.hy_tabsrc[L][ti_]
                        dstt = (self.hy_tc[L], self.hy_ts[L])[ti_].rearrange("b p (r c) -> p b r c", r=FC)
                        for rc in range(FC):
                            t, b = tiles[cnt[0] % NB]
                            cnt[0] += 1
                            self.dma(t[:, 0:F], srct[rc * 128:(rc + 1) * 128, :], [], [b], q="pool")
                            self.dma(dstt[:, 0:NCBf, rc, :], t[:, 0:NCBf * 256].rearrange("p (b c) -> p b c", c=256), [b], [], q="sp")
                            if F % 256:
                                self.dma(dstt[:, NCBf, rc, 0:128], t[:, NCBf * 256:F], [b], [], q="sp")
            z = self.sb(stk, [128, 8, 2], BF16, "z")
            bz = Buf("z")
            P.op("dve", lambda g: g.memset(z[:], 0.0), writes=[bz])
            hv = self.hTd.rearrange("(c p) t -> p c t", p=128)
            self.dma(hv[:, :, 0:1], z[:, :, 0:1], [bz], [self.dbuf["hTd"]], slow=True)
            self.dma(hv[:, :, self.cfg.lmax + 1:self.cfg.lmax + 2], z[:, :, 1:2], [bz], [self.dbuf["hTd"]], slow=True)
        P.barrier()

    def compute_mod(self):
        P = self.P
        cfg = self.cfg
        nc = self.nc
        NSQ = cfg.nseq
        with ExitStack() as stk:
            cT = self.sb(stk, [128, 8, NSQ], F32, "cT")
            caT = self.sb(stk, [128, 8, NSQ], F32, "caT")
            ones1 = self.sb(stk, [1, 8], F32, "ones1")
            bc = Buf("cT")
            for s in range(NSQ):
                self.dma(cT[:, :, s:s + 1], self.cin[s:s + 1, :].rearrange("o (c p) -> p c o", p=128),
                         [], [bc], slow=True)
            P.op("act", lambda g: g.activation(out=caT[:], in_=cT[:], func=AF.Silu), reads=[bc], writes=[bc])
            P.op("dve", lambda g: g.memset(ones1[:], 1.0), writes=[bc])
            NW = 2
            wt = [(self.sb(stk, [128, 8, 512], F32, "aw"), Buf("aw%d" % i)) for i in range(NW)]
            bt = [(self.sb(stk, [1, 512], F32, "ab"), Buf("ab%d" % i)) for i in range(NW)]
            rt = [(self.sb(stk, [NSQ, 512], F32, "ar"), Buf("ar%d" % i)) for i in range(NW)]
            i = 0
            for l in range(cfg.depth):
                for n in range(12):
                    w, bw = wt[i % NW]
                    bb, bbb = bt[i % NW]
                    r, br = rt[i % NW]
                    i += 1
                    self.dma(w[:], self.ada_w[l][:, n * 512:(n + 1) * 512].rearrange("(k p) c -> p k c", p=128),
                             [], [bw])
                    self.dma(bb[:], self.ada_b[l:l + 1, n * 512:(n + 1) * 512], [], [bbb])
                    pt, bp = self.next_psf()
                    for k in range(8):
                        P.op("pe", (lambda k=k, w=w, pt=pt: lambda g: g.matmul(
                            pt[0:NSQ, :], caT[:, k, :], w[:, k, :], start=(k == 0), stop=False))(),
                            reads=[bc, bw], writes=[bp])
                    P.op("pe", (lambda bb=bb, pt=pt: lambda g: g.matmul(
                        pt[0:NSQ, :], ones1[0:1, 0:NSQ], bb[0:1, :], start=False, stop=True))(),
                        reads=[bc, bbb], writes=[bp])
                    P.op("act", (lambda r=r, pt=pt: lambda g: g.copy(out=r[:], in_=pt[0:NSQ, :]))(),
                         reads=[bp], writes=[br])
                    self.dma(self.modd[l][:, n * 512:(n + 1) * 512], r[:], [br], [self.dbuf["modd"]])
        P.barrier()

    def load_mod_vectors(self, stk, l, s, which):
        P = self.P
        nc = self.nc
        base = 3 * D * which
        sh = self.sb(stk, [128, 8], F32, "sh")
        sc = self.sb(stk, [128, 8], F32, "sc")
        nw = self.sb(stk, [128, 8], F32, "nw")
        wm = self.sb(stk, [128, 8], F32, "wm")
        gt = self.sb(stk, [128, D], F32, "g")
        b = Buf("modv")
        row = self.modd[l][s:s + 1, :]
        self.dma(sh[:], row[:, base:base + D].rearrange("o (c p) -> p (o c)", p=128), [self.dbuf["modd"]], [b],
                 slow=True)
        self.dma(sc[:], row[:, base + D:base + 2 * D].rearrange("o (c p) -> p (o c)", p=128),
                 [self.dbuf["modd"]], [b], slow=True)
        self.dma(nw[:], self.norm_w[l][which:which + 1, :].rearrange("o (c p) -> p (o c)", p=128), [], [b],
                 slow=True)
        self.dma(gt[:], row[:, base + 2 * D:base + 3 * D].partition_broadcast(128), [self.dbuf["modd"]], [b])
        P.op("dve", lambda g: g.scalar_tensor_tensor(out=wm[:], in0=sc[:], scalar=1.0, in1=nw[:],
                                                     op0=ALU.add, op1=ALU.mult), reads=[b], writes=[b])
        return wm, sh, gt, b

    def norm_to_T(self, stk_tmp, xt, bx, nsub, wm, sh, bmod, hT, bh, tmp):
        P = self.P
        ss, junk, rstd, xsb, bt = tmp
        for j in range(nsub):
            P.op("act", (lambda j=j: lambda g: g.activation(out=junk[:], in_=xt[:, j, :], func=AF.Square,
                                                            accum_out=ss[:, j:j + 1]))(),
                 reads=[bx], writes=[bt])
        P.op("dve", lambda g: g.tensor_scalar(out=rstd[:, 0:nsub], in0=ss[:, 0:nsub], scalar1=1.0 / D, scalar2=EPS,
                                              op0=ALU.mult, op1=ALU.add), reads=[bt], writes=[bt])
        P.op("act", lambda g: g.sqrt(out=rstd[:, 0:nsub], in_=rstd[:, 0:nsub]), reads=[bt], writes=[bt])
        P.op("dve", lambda g: g.reciprocal(out=rstd[:, 0:nsub], in_=rstd[:, 0:nsub]), reads=[bt], writes=[bt])
        for j in range(nsub):
            P.op("dve", (lambda j=j: lambda g: g.tensor_scalar(out=xsb[:, j, :], in0=xt[:, j, :],
                                                               scalar1=rstd[:, j:j + 1], scalar2=None,
                                                               op0=ALU.mult))(),
                 reads=[bx, bt], writes=[bt])
        for c in range(8):
            pt, bp = self.next_psb()
            for j in range(nsub):
                P.op("pe", (lambda j=j, c=c, pt=pt: lambda g: g.transpose(
                    out=pt[:, j * 128:(j + 1) * 128], in_=xsb[:, j, c * 128:(c + 1) * 128], identity=self.ident[:]))(),
                    reads=[bt, self.b_const], writes=[bp])
            P.op("act", (lambda c=c, pt=pt: lambda g: g.activation(
                out=hT[:, c, 0:nsub * 128], in_=pt[:, 0:nsub * 128], func=AF.Identity,
                bias=sh[:, c:c + 1], scale=wm[:, c:c + 1]))(),
                reads=[bp, bmod], writes=[bh])

    def norm_tmp(self, stk):
        ss = self.sb(stk, [128, 4], F32, "ss")
        junk = self.sb(stk, [128, D], BF16, "junk")
        rstd = self.sb(stk, [128, 4], F32, "rstd")
        xsb = self.sb(stk, [128, 4, D], BF16, "xsb")
        return (ss, junk, rstd, xsb, Buf("ntmp"))

    def xsrc(self, l):
        return (self.xin, "xin") if l == 0 else (self.xs, "xs")

    def xb(self, key):
        if key not in self.xbuf:
            self.xbuf[key] = Buf("x%s" % (key,))
        return self.xbuf[key]

    def stage_a(self, l, s):
        P = self.P
        cfg = self.cfg
        L = cfg.seqs[s]
        off = cfg.offs[s]
        src, _ = self.xsrc(l)
        with ExitStack() as stk:
            wm, sh, gt, bmod = self.load_mod_vectors(stk, l, s, 0)
            NB = 2
            xts = [(self.sb(stk, [128, 4, D], F32, "xt"), Buf("xt%d" % i)) for i in range(NB)]
            hts = [(self.sb(stk, [128, 8, 512], BF16, "hT"), Buf("hT%d" % i)) for i in range(NB)]
            tmps = [self.norm_tmp(stk) for i in range(NB)]
            hv = self.hTd.rearrange("(c p) t -> p c t", p=128)
            t0 = 0
            i = 0
            while t0 < L:
                n = min(512, L - t0)
                nsub = n // 128
                xt, bx = xts[i % NB]
                hT, bh = hts[i % NB]
                self.dma(xt[:, 0:nsub, :], src[off + t0:off + t0 + n, :].rearrange("(j p) d -> p j d", p=128),
                         [self.xb((s, t0))], [bx])
                self.norm_to_T(stk, xt, bx, nsub, wm, sh, bmod, hT, bh, tmps[i % NB])
                self.dma(hv[:, :, 1 + t0:1 + t0 + n], hT[:, :, 0:n], [bh], [self.dbuf["hTd"]])
                t0 += n
                i += 1
            self.dma(hv[:, :, L + 1:L + 2], self.zcol[:, :, 0:1], [self.b_const], [self.dbuf["hTd"]], slow=True)
        P.barrier()

    def stage_c(self, l, s, kind):
        P = self.P
        cfg = self.cfg
        L = cfg.seqs[s]
        off = cfg.offs[s]
        src, _ = self.xsrc(l)
        last = (l == cfg.depth - 1)
        wname = {0: "hy_w_out", 1: "gdn_w_out", 2: "swa_w_out", 3: "mla_w_out"}[kind]
        OC = WEIGHT_SHAPES[wname][0] // 128
        NSET = 2 if OC == 8 else 1
        with ExitStack() as stk:
            wm1, sh1, g1, bm1 = self.load_mod_vectors(stk, l, s, 0)
            wm2, sh2, g2, bm2 = self.load_mod_vectors(stk, l, s, 1)
            wout = self.sb(stk, [128, OC, D], BF16, "wout")
            bwo = Buf("wout")
            self.dma(wout[:], self.wb[wname].rearrange("(k p) c -> p k c", p=128), [], [bwo])
            if last:
                fnw = self.sb(stk, [128, D], F32, "fnw")
                bfn = Buf("fnw")
                self.dma(fnw[:], self.final_norm_w[0:1, :].partition_broadcast(128), [], [bfn])
            xt = self.sb(stk, [128, 4, D], F32, "xt")
            bx = Buf("xt")
            oT = self.sb(stk, [128, OC, 512], BF16, "oT")
            boT = Buf("oT")
            sets = [(self.sb(stk, [128, 4, D], F32, "x1"), Buf("x1_%d" % i),
                     self.sb(stk, [128, 8, 512], BF16, "h2T"), Buf("h2T_%d" % i)) for i in range(NSET)]
            hid = self.sb(stk, [128, 22, 512], BF16, "hid")
            bhid = Buf("hid")
            tmp = self.norm_tmp(stk)
            NWB = 2
            GF = 2
            wgs = [(self.sb(stk, [128, 8, 2, GF * 128], BF16, "wgu"), Buf("wgu%d" % i)) for i in range(NWB)]
            wds = [(self.sb(stk, [128, 22, 512], BF16, "wd"), Buf("wd%d" % i)) for i in range(NWB)]
            sgs = [(self.sb(stk, [128, 512], F32, "sg"), Buf("sg%d" % i)) for i in range(2)]
            tts = [(self.sb(stk, [128, 512], F32, "tt"), Buf("tt%d" % i)) for i in range(2)]
            ov = self.oTd.rearrange("(c p) t -> p c t", p=128)
            gu_v = self.wb_gu[l]
            wd_v = self.wb_down[l]
            cnt = {"wg": 0, "wd": 0, "sg": 0, "tt": 0}
            tiles = []
            t0 = 0
            while t0 < L:
                n = min(512, L - t0)
                tiles.append((t0, n, n // 128))
                t0 += n

            def front(ti):
                t0, n, nsub = tiles[ti]
                x1, bx1, h2T, bh2 = sets[ti % NSET]
                self.dma(oT[:, :, 0:n], ov[:, 0:OC, t0:t0 + n], [], [boT])
                self.dma(xt[:, 0:nsub, :], src[off + t0:off + t0 + n, :].rearrange("(j p) d -> p j d", p=128),
                         [self.xb((s, t0))], [bx])
                for j in range(nsub):
                    for h in range(2):
                        pt, bp = self.next_psf()
                        for c in range(OC):
                            self.mm(pt[:, :], oT[:, c, j * 128:(j + 1) * 128], wout[:, c, h * 512:(h + 1) * 512],
                                    c == 0, c == OC - 1, [boT, bwo], [bp])
                        tt, btt = tts[cnt["tt"] % 2]
                        cnt["tt"] += 1
                        self.tt("dve", tt[:], pt[:, :], g1[:, h * 512:(h + 1) * 512], ALU.mult, [bp, bm1], [btt])
                        self.tt("dve", x1[:, j, h * 512:(h + 1) * 512], tt[:], xt[:, j, h * 512:(h + 1) * 512], ALU.add,
                                [btt, bx], [bx1])
                self.norm_to_T(stk, x1, bx1, nsub, wm2, sh2, bm2, h2T, bh2, tmp)

            def ffn_back(ti):
                t0, n, nsub = tiles[ti]
                x1, bx1, h2T, bh2 = sets[ti % NSET]
                for f0 in range(0, 22, GF):
                    wg, bwg = wgs[cnt["wg"] % NWB]
                    cnt["wg"] += 1
                    self.dma(wg[:, :, :, :].rearrange("p k u c -> p (k u c)"), gu_v[f0 // GF], [], [bwg])
                    for fi in range(GF):
                        f = f0 + fi
                        pg, bpg = self.next_psf()
                        pu, bpu = self.next_psf()
                        for k in range(8):
                            self.mm(pg[:, 0:n], wg[:, k, 0, fi * 128:(fi + 1) * 128], h2T[:, k, 0:n], k == 0, k == 7,
                                    [bwg, bh2], [bpg])
                        for k in range(8):
                            self.mm(pu[:, 0:n], wg[:, k, 1, fi * 128:(fi + 1) * 128], h2T[:, k, 0:n], k == 0, k == 7,
                                    [bwg, bh2], [bpu])
                        sg, bsg = sgs[cnt["sg"] % 2]
                        cnt["sg"] += 1
                        self.actf(sg[:, 0:n], pg[:, 0:n], AF.Silu, [bpg], [bsg])
                        self.tt("dve", hid[:, f, 0:n], pu[:, 0:n], sg[:, 0:n], ALU.mult, [bpu, bsg], [bhid])
                for h in range(2):
                    wd, bwd = wds[cnt["wd"] % NWB]
                    cnt["wd"] += 1
                    self.dma(wd[:, :, :].rearrange("p f c -> p (f c)"), wd_v[h], [], [bwd])
                    for j in range(nsub):
                        pt, bp = self.next_psf()
                        for f in range(22):
                            self.mm(pt[:, :], hid[:, f, j * 128:(j + 1) * 128], wd[:, f, :], f == 0, f == 21,
                                    [bhid, bwd], [bp])
                        tt, btt = tts[cnt["tt"] % 2]
                        cnt["tt"] += 1
                        self.tt("dve", tt[:], pt[:, :], g2[:, h * 512:(h + 1) * 512], ALU.mult, [bp, bm2], [btt])
                        self.tt("dve", x1[:, j, h * 512:(h + 1) * 512], tt[:], x1[:, j, h * 512:(h + 1) * 512], ALU.add,
                                [btt, bx1], [bx1])
                if not last:
                    self.dma(self.xs[off + t0:off + t0 + n, :].rearrange("(j p) d -> p j d", p=128),
                             x1[:, 0:nsub, :], [bx1], [self.xb((s, t0))])
                else:
                    ss, junk, rstd, xsb, bt = tmp
                    for j in range(nsub):
                        self.actf(junk[:], x1[:, j, :], AF.Square, [bx1], [bt], accum_out=ss[:, j:j + 1])
                    self.ts("dve", rstd[:, 0:nsub], ss[:, 0:nsub], 1.0 / D, EPS, ALU.mult, ALU.add, [bt], [bt])
                    P.op("act", (lambda rstd=rstd, nsub=nsub: lambda g: g.sqrt(out=rstd[:, 0:nsub], in_=rstd[:, 0:nsub]))(),
                         reads=[bt], writes=[bt])
                    self.recip(rstd[:, 0:nsub], rstd[:, 0:nsub], [bt], [bt])
                    for j in range(nsub):
                        self.stt("dve", xsb_out[:, j, :], x1[:, j, :], rstd[:, j:j + 1], fnw[:], ALU.mult, ALU.mult,
                                 [bx1, bt, bfn], [bxo])
                    self.dma(self.yout[off + t0:off + t0 + n, :].rearrange("(j p) d -> p j d", p=128),
                             xsb_out[:, 0:nsub, :], [bxo], [])

            if last:
                xsb_out, bxo = xt, bx
            if NSET == 2:
                front(0)
                for ti in range(len(tiles)):
                    if ti + 1 < len(tiles):
                        front(ti + 1)
                    ffn_back(ti)
            else:
                for ti in range(len(tiles)):
                    front(ti)
                    ffn_back(ti)
        P.barrier()

    def swa(self, l, s):
        P = self.P
        cfg = self.cfg
        nc = self.nc
        L = cfg.seqs[s]
        NBK = L // 128
        with ExitStack() as stk:
            wq = self.sb(stk, [128, 8, 1536], BF16, "wqkv")
            bw = Buf("wqkv")
            self.dma(wq[:], self.wb["swa_w_qkv"].rearrange("(k p) c -> p k c", p=128),
                     [self.dbuf["wb_swa_w_qkv"]], [bw])
            sk = self.sb(stk, [128, 16], F32, "sink")
            bsk = Buf("sink")
            self.dma(sk[:], self.swa_sink[0:1, :].partition_broadcast(128), [], [bsk])
            P.op("act", lambda g: g.activation(out=sk[:], in_=sk[:], func=AF.Exp), reads=[bsk], writes=[bsk])
            ones64 = self.ones_bf
            QT = self.sb(stk, [64, 4, L], BF16, "QT")
            KT = self.sb(stk, [64, L], BF16, "KT")
            V = self.sb(stk, [128, NBK, 64], BF16, "V")
            bq = Buf("qkv")
            bias = self.sb(stk, [128, 3, 512], F32, "bias")
            bb = Buf("bias")
            hts = [(self.sb(stk, [128, 8, 512], BF16, "hT"), Buf("hT%d" % i)) for i in range(2)]
            tms = [(self.sb(stk, [128, 512], F32, "tm"), Buf("tm%d" % i)) for i in range(2)]
            pts = [(self.sb(stk, [128, 512], BF16, "pT"), Buf("pT%d" % i)) for i in range(3)]
            rds = [(self.sb(stk, [64, 512], F32, "rd"), Buf("rd%d" % i)) for i in range(2)]
            ots = [(self.sb(stk, [64, 512], BF16, "ot"), Buf("ot%d" % i)) for i in range(2)]
            hv = self.hTd.rearrange("(c p) t -> p c t", p=128)
            hi = 0
            tmi = 0
            pti = 0
            rdi = 0
            for hk in range(4):
                self.dma(bias[:], self.swa_bias[hk].rearrange("c p n -> p c n"), [], [bb])
                t0 = 0
                while t0 < L:
                    n = min(512, L - t0)
                    nsub = n // 128
                    hT, bh = hts[hi % 2]
                    hi += 1
                    self.dma(hT[:, :, 0:n], hv[:, :, 1 + t0:1 + t0 + n], [self.dbuf["hTd"]], [bh])
                    for gq in range(5):
                        pt, bp = self.next_psf()
                        col = (hk * 4 + gq) * 64 if gq < 4 else 1024 + hk * 64
                        for k in range(8):
                            P.op("pe", (lambda k=k, pt=pt, col=col, hT=hT: lambda g: g.matmul(
                                pt[0:64, 0:n], wq[:, k, col:col + 64], hT[:, k, 0:n], start=(k == 0), stop=(k == 7)))(),
                                reads=[bw, bh], writes=[bp])
                        if gq < 4:
                            P.op("act", (lambda gq=gq, pt=pt, t0=t0: lambda g: g.mul(
                                out=QT[:, gq, t0:t0 + n], in_=pt[0:64, 0:n], mul=0.125))(),
                                reads=[bp], writes=[bq])
                        else:
                            P.op("act", (lambda pt=pt, t0=t0: lambda g: g.copy(out=KT[:, t0:t0 + n], in_=pt[0:64, 0:n]))(),
                                 reads=[bp], writes=[bq])
                    pt, bp = self.next_psf()
                    vcol = 1280 + hk * 64
                    for j in range(nsub):
                        for k in range(8):
                            P.op("pe", (lambda j=j, k=k, pt=pt, hT=hT, vcol=vcol: lambda g: g.matmul(
                                pt[:, j * 64:(j + 1) * 64], hT[:, k, j * 128:(j + 1) * 128], wq[:, k, vcol:vcol + 64],
                                start=(k == 0), stop=(k == 7)))(),
                                reads=[bw, bh], writes=[bp])
                    P.op("dve", (lambda pt=pt, t0=t0, nsub=nsub: lambda g: g.tensor_copy(
                        out=V[:, t0 // 128:t0 // 128 + nsub, :],
                        in_=pt[:, 0:nsub * 64].rearrange("p (j d) -> p j d", d=64)))(),
                        reads=[bp], writes=[bq])
                    t0 += n
                for i in range(NBK):
                    po, bpo = self.psacc[0]
                    pd, bpd = self.psacc[1]
                    cs = [c for c in (i - 1, i, i + 1) if 0 <= c < NBK]
                    sts = []
                    for ci, c in enumerate(cs):
                        ps_, bps = self.next_psf()
                        for a in range(4):
                            self.mm(ps_[:, a * 128:(a + 1) * 128], KT[:, c * 128:(c + 1) * 128],
                                    QT[:, a, i * 128:(i + 1) * 128], True, True, [bq], [bps])
                        sts.append((ps_, bps))
                    for ci, c in enumerate(cs):
                        ps_, bps = sts[ci]
                        tm, btm = tms[tmi % 2]
                        tmi += 1
                        P.op("dve", (lambda c=c, i=i, ps_=ps_, tm=tm: lambda g: g.tensor_tensor(
                            out=tm[:], in0=ps_[:, :], in1=bias[:, c - i + 1, :], op=ALU.add))(),
                            reads=[bps, bb], writes=[btm])
                        pT, bpT = pts[pti % 3]
                        pti += 1
                        P.op("act", (lambda tm=tm, pT=pT: lambda g: g.activation(out=pT[:], in_=tm[:], func=AF.Exp))(),
                             reads=[btm], writes=[bpT])
                        P.op("pe", (lambda c=c, pT=pT, po=po, ci=ci, nn=len(cs): lambda g: g.matmul(
                            po[0:64, :], V[:, c, :], pT[:], start=(ci == 0), stop=(ci == nn - 1)))(),
                            reads=[bq, bpT], writes=[bpo])
                        P.op("pe", (lambda pT=pT, pd=pd, ci=ci, nn=len(cs): lambda g: g.matmul(
                            pd[0:64, :], ones64[:, 0:64], pT[:], start=(ci == 0), stop=(ci == nn - 1)))(),
                            reads=[self.b_const, bpT], writes=[bpd])
                    rd, brd = rds[rdi % 2]
                    ot, bot = ots[rdi % 2]
                    rdi += 1
                    for gq in range(4):
                        hq = hk * 4 + gq
                        P.op("dve", (lambda gq=gq, hq=hq, rd=rd, pd=pd: lambda g: g.tensor_scalar(
                            out=rd[:, gq * 128:(gq + 1) * 128], in0=pd[0:64, gq * 128:(gq + 1) * 128],
                            scalar1=sk[0:64, hq:hq + 1], scalar2=None, op0=ALU.add))(),
                            reads=[bpd, bsk], writes=[brd])
                    P.op("dve", (lambda rd=rd: lambda g: g.reciprocal(out=rd[:], in_=rd[:]))(), reads=[brd], writes=[brd])
                    P.op("dve", (lambda rd=rd, ot=ot, po=po: lambda g: g.tensor_tensor(
                        out=ot[:], in0=po[0:64, :], in1=rd[:], op=ALU.mult))(),
                        reads=[bpo, brd], writes=[bot])
                    dst = self.oTd[hk * 256:(hk + 1) * 256, i * 128:(i + 1) * 128].rearrange("(a d) q -> d a q", d=64)
                    self.dma(dst, ot[:].rearrange("d (a q) -> d a q", a=4), [bot], [self.dbuf["oTd"]])
        P.barrier()

    def mla(self, l, s):
        P = self.P
        cfg = self.cfg
        L = cfg.seqs[s]
        NBK = L // 128
        SCALE = 96.0 ** -0.5
        with ExitStack() as stk:
            wd = self.sb(stk, [128, 8, 544], BF16, "wd")
            wdx = self.sb(stk, [128, 8, 192], BF16, "wdx")
            wuq = self.sb(stk, [128, 2, 1536], BF16, "wuq")
            wuqs = self.sb(stk, [128, 2, 1536], BF16, "wuqs")
            wukv = self.sb(stk, [128, 2, 2048], BF16, "wukv")
            bw = Buf("mlaw")
            self.dma(wd[:], self.wb["mla_w_down"].rearrange("(k p) c -> p k c", p=128), [self.dbuf["wb_mla_w_down"]], [bw])
            self.dma(wuq[:], self.wb["mla_w_uq"].rearrange("(k p) c -> p k c", p=128), [self.dbuf["wb_mla_w_uq"]], [bw])
            self.dma(wukv[:], self.wb["mla_w_ukv"].rearrange("(k p) c -> p k c", p=128), [self.dbuf["wb_mla_w_ukv"]], [bw])
            P.op("dve", lambda g: g.memset(wuqs[:], 0.0), writes=[bw])
            P.op("dve", lambda g: g.memset(wdx[:], 0.0), writes=[bw])
            for k in range(2):
                wv = wuq[:, k, :].rearrange("p (h c) -> p h c", c=96)
                wsv = wuqs[:, k, :].rearrange("p (h c) -> p h c", c=96)
                P.op("act", (lambda wv=wv, wsv=wsv: lambda g: g.mul(out=wsv[:, :, 64:80], in_=wv[:, :, 80:96], mul=-1.0))(),
                     reads=[bw], writes=[bw])
                P.op("act", (lambda wv=wv, wsv=wsv: lambda g: g.copy(out=wsv[:, :, 80:96], in_=wv[:, :, 64:80]))(),
                     reads=[bw], writes=[bw])
            for k in range(8):
                P.op("act", (lambda k=k: lambda g: g.copy(out=wdx[:, k, 64:96], in_=wd[:, k, 512:544]))(),
                     reads=[bw], writes=[bw])
                P.op("act", (lambda k=k: lambda g: g.mul(out=wdx[:, k, 160:176], in_=wd[:, k, 528:544], mul=-1.0))(),
                     reads=[bw], writes=[bw])
                P.op("act", (lambda k=k: lambda g: g.copy(out=wdx[:, k, 176:192], in_=wd[:, k, 512:528]))(),
                     reads=[bw], writes=[bw])
            qnw = self.sb(stk, [128, 256], F32, "qnw")
            kvnw = self.sb(stk, [128, 256], F32, "kvnw")
            self.dma(qnw[:], self.mla_qnw[0:1, :].partition_broadcast(128), [], [bw])
            self.dma(kvnw[:], self.mla_kvnw[0:1, :].partition_broadcast(128), [], [bw])
            cosF = self.sb(stk, [96, L], F32, "cosF")
            sinF = self.sb(stk, [96, L], F32, "sinF")
            self.dma(cosF[:], self.rope_cs[0][:, 0:L], [], [bw])
            self.dma(sinF[:], self.rope_cs[1][:, 0:L], [], [bw])
            cqT = self.sb(stk, [128, 2, L], BF16, "cqT")
            ckvT = self.sb(stk, [128, 2, L], BF16, "ckvT")
            KR = self.sb(stk, [96, L], BF16, "KR")
            bc = Buf("lat")
            hts = [(self.sb(stk, [128, 8, 512], BF16, "hT"), Buf("hT%d" % i)) for i in range(2)]
            dts = [(self.sb(stk, [128, 512], F32, "dt"), Buf("dt%d" % i)) for i in range(2)]
            dns = [(self.sb(stk, [128, 512], BF16, "dn"), Buf("dn%d" % i)) for i in range(2)]
            sm = self.sb(stk, [128, 8], F32, "sm")
            junk = self.sb(stk, [128, 256], BF16, "junk")
            bsm = Buf("sm")
            t1s = [(self.sb(stk, [96, 512], F32, "t1"), Buf("t1%d" % i)) for i in range(2)]
            t2s = [(self.sb(stk, [96, 512], F32, "t2"), Buf("t2%d" % i)) for i in range(2)]
            hv = self.hTd.rearrange("(c p) t -> p c t", p=128)
            hi = 0
            di = 0
            ti = 0

            def rope_combine(p1, bp1, p2, bp2, n, t0, out_ap, bout):
                nonlocal ti
                t1, bt1 = t1s[ti % 2]
                t2, bt2 = t2s[ti % 2]
                ti += 1
                P.op("dve", lambda g: g.tensor_tensor(out=t1[:, 0:n], in0=p1[0:96, 0:n], in1=cosF[:, t0:t0 + n], op=ALU.mult),
                     reads=[bp1, bw], writes=[bt1])
                P.op("dve", lambda g: g.tensor_tensor(out=t2[:, 0:n], in0=p2[0:96, 0:n], in1=sinF[:, t0:t0 + n], op=ALU.mult),
                     reads=[bp2, bw], writes=[bt2])
                P.op("pool", lambda g: g.tensor_tensor(out=out_ap, in0=t1[:, 0:n], in1=t2[:, 0:n], op=ALU.add),
                     reads=[bt1, bt2], writes=[bout])

            t0 = 0
            while t0 < L:
                n = min(512, L - t0)
                nsub = n // 128
                hT, bh = hts[hi % 2]
                hi += 1
                self.dma(hT[:, :, 0:n], hv[:, :, 1 + t0:1 + t0 + n], [self.dbuf["hTd"]], [bh])
                for j in range(nsub):
                    pt, bp = self.next_psf()
                    for k in range(8):
                        P.op("pe", (lambda j=j, k=k, pt=pt, hT=hT: lambda g: g.matmul(
                            pt[:, :], hT[:, k, j * 128:(j + 1) * 128], wd[:, k, 0:512], start=(k == 0), stop=(k == 7)))(),
                            reads=[bh, bw], writes=[bp])
                    dt, bdt = dts[di % 2]
                    dn, bdn = dns[di % 2]
                    di += 1
                    for q2 in range(2):
                        P.op("act", (lambda q2=q2, pt=pt: lambda g: g.activation(
                            out=junk[:], in_=pt[:, q2 * 256:(q2 + 1) * 256], func=AF.Square, accum_out=sm[:, q2:q2 + 1]))(),
                            reads=[bp], writes=[bsm])
                    P.op("dve", lambda g: g.tensor_scalar(out=sm[:, 2:4], in0=sm[:, 0:2], scalar1=1.0 / 256, scalar2=EPS,
                                                          op0=ALU.mult, op1=ALU.add), reads=[bsm], writes=[bsm])
                    P.op("act", lambda g: g.sqrt(out=sm[:, 2:4], in_=sm[:, 2:4]), reads=[bsm], writes=[bsm])
                    P.op("dve", lambda g: g.reciprocal(out=sm[:, 4:6], in_=sm[:, 2:4]), reads=[bsm], writes=[bsm])
                    for q2, nwt in ((0, qnw), (1, kvnw)):
                        P.op("dve", (lambda q2=q2, nwt=nwt, pt=pt, dn=dn: lambda g: g.scalar_tensor_tensor(
                            out=dn[:, q2 * 256:(q2 + 1) * 256], in0=pt[:, q2 * 256:(q2 + 1) * 256],
                            scalar=sm[:, 4 + q2:5 + q2], in1=nwt[:], op0=ALU.mult, op1=ALU.mult))(),
                            reads=[bp, bsm, bw], writes=[bdn])
                    pb, bpb = self.next_psb()
                    for cc in range(4):
                        P.op("pe", (lambda cc=cc, pb=pb, dn=dn: lambda g: g.transpose(
                            out=pb[:, cc * 128:(cc + 1) * 128], in_=dn[:, cc * 128:(cc + 1) * 128], identity=self.ident[:]))(),
                            reads=[bdn, self.b_const], writes=[bpb])
                    tcol = t0 + j * 128
                    P.op("act", (lambda pb=pb, tcol=tcol: lambda g: g.copy(
                        out=cqT[:, :, tcol:tcol + 128], in_=pb[:, 0:256].rearrange("p (c t) -> p c t", c=2)))(),
                        reads=[bpb], writes=[bc])
                    P.op("act", (lambda pb=pb, tcol=tcol: lambda g: g.copy(
                        out=ckvT[:, :, tcol:tcol + 128], in_=pb[:, 256:512].rearrange("p (c t) -> p c t", c=2)))(),
                        reads=[bpb], writes=[bc])
                p1, bp1 = self.next_psf()
                p2, bp2 = self.next_psf()
                for k in range(8):
                    P.op("pe", (lambda k=k, p1=p1, hT=hT: lambda g: g.matmul(
                        p1[0:96, 0:n], wdx[:, k, 0:96], hT[:, k, 0:n], start=(k == 0), stop=(k == 7)))(),
                        reads=[bh, bw], writes=[bp1])
                for k in range(8):
                    P.op("pe", (lambda k=k, p2=p2, hT=hT: lambda g: g.matmul(
                        p2[0:96, 0:n], wdx[:, k, 96:192], hT[:, k, 0:n], start=(k == 0), stop=(k == 7)))(),
                        reads=[bh, bw], writes=[bp2])
                rope_combine(p1, bp1, p2, bp2, n, t0, KR[:, t0:t0 + n], bc)
                t0 += n
            QT = self.sb(stk, [96, L], BF16, "QT")
            KT = self.sb(stk, [96, L], BF16, "KT")
            V = self.sb(stk, [128, NBK, 64], BF16, "V")
            bq = Buf("qkv")
            pts = [(self.sb(stk, [128, 512], BF16, "pT"), Buf("pT%d" % i)) for i in range(3)]
            rds = [(self.sb(stk, [64, 512], F32, "rd"), Buf("rd%d" % i)) for i in range(2)]
            ots = [(self.sb(stk, [64, 512], BF16, "ot"), Buf("ot%d" % i)) for i in range(2)]
            pti = 0
            rdi = 0
            for h in range(16):
                t0 = 0
                while t0 < L:
                    n = min(512, L - t0)
                    nsub = n // 128
                    p1, bp1 = self.next_psf()
                    p2, bp2 = self.next_psf()
                    for k in range(2):
                        P.op("pe", (lambda k=k, p1=p1, h=h, t0=t0, n=n: lambda g: g.matmul(
                            p1[0:96, 0:n], wuq[:, k, h * 96:(h + 1) * 96], cqT[:, k, t0:t0 + n], start=(k == 0), stop=(k == 1)))(),
                            reads=[bc, bw], writes=[bp1])
                    for k in range(2):
                        P.op("pe", (lambda k=k, p2=p2, h=h, t0=t0, n=n: lambda g: g.matmul(
                            p2[0:96, 0:n], wuqs[:, k, h * 96:(h + 1) * 96], cqT[:, k, t0:t0 + n], start=(k == 0), stop=(k == 1)))(),
                            reads=[bc, bw], writes=[bp2])
                    rope_combine(p1, bp1, p2, bp2, n, t0, QT[:, t0:t0 + n], bq)
                    pk, bpk = self.next_psf()
                    for k in range(2):
                        P.op("pe", (lambda k=k, pk=pk, h=h, t0=t0, n=n: lambda g: g.matmul(
                            pk[0:64, 0:n], wukv[:, k, h * 128:h * 128 + 64], ckvT[:, k, t0:t0 + n], start=(k == 0), stop=(k == 1)))(),
                            reads=[bc, bw], writes=[bpk])
                    P.op("act", (lambda pk=pk, t0=t0, n=n: lambda g: g.copy(out=KT[0:64, t0:t0 + n], in_=pk[0:64, 0:n]))(),
                         reads=[bpk], writes=[bq])
                    P.op("act", (lambda t0=t0, n=n: lambda g: g.copy(out=KT[64:96, t0:t0 + n], in_=KR[64:96, t0:t0 + n]))(),
                         reads=[bc], writes=[bq])
                    pv, bpv = self.next_psf()
                    for j in range(nsub):
                        for k in range(2):
                            P.op("pe", (lambda j=j, k=k, pv=pv, h=h, t0=t0: lambda g: g.matmul(
                                pv[:, j * 64:(j + 1) * 64], ckvT[:, k, t0 + j * 128:t0 + (j + 1) * 128],
                                wukv[:, k, h * 128 + 64:h * 128 + 128], start=(k == 0), stop=(k == 1)))(),
                                reads=[bc, bw], writes=[bpv])
                    P.op("dve", (lambda pv=pv, t0=t0, nsub=nsub: lambda g: g.tensor_copy(
                        out=V[:, t0 // 128:t0 // 128 + nsub, :],
                        in_=pv[:, 0:nsub * 64].rearrange("p (j d) -> p j d", d=64)))(),
                        reads=[bpv], writes=[bq])
                    t0 += n
                t0 = 0
                while t0 < L:
                    n = min(512, L - t0)
                    po, bpo = self.psacc[0]
                    pd, bpd = self.psacc[1]
                    AH = 2
                    sts = {}

                    def emit_st(c, t0=t0, n=n):
                        ps_, bps = self.next_psf()
                        self.mm(ps_[:, 0:n], KT[:, c * 128:(c + 1) * 128], QT[:, t0:t0 + n], True, True, [bq], [bps])
                        sts[c] = (ps_, bps)
                    for c in range(min(AH, NBK)):
                        emit_st(c)
                    for c in range(NBK):
                        if c + AH < NBK:
                            emit_st(c + AH)
                        ps_, bps = sts.pop(c)
                        pT, bpT = pts[pti % 3]
                        pti += 1
                        P.op("act", (lambda ps_=ps_, pT=pT, n=n: lambda g: g.activation(
                            out=pT[:, 0:n], in_=ps_[:, 0:n], func=AF.Exp, scale=SCALE))(),
                            reads=[bps], writes=[bpT])
                        P.op("pe", (lambda c=c, pT=pT, po=po, n=n: lambda g: g.matmul(
                            po[0:64, 0:n], V[:, c, :], pT[:, 0:n], start=(c == 0), stop=(c == NBK - 1)))(),
                            reads=[bq, bpT], writes=[bpo])
                        P.op("pe", (lambda c=c, pT=pT, pd=pd, n=n: lambda g: g.matmul(
                            pd[0:64, 0:n], self.ones_bf[:, 0:64], pT[:, 0:n], start=(c == 0), stop=(c == NBK - 1)))(),
                            reads=[self.b_const, bpT], writes=[bpd])
                    rd, brd = rds[rdi % 2]
                    ot, bot = ots[rdi % 2]
                    rdi += 1
                    P.op("dve", (lambda rd=rd, pd=pd, n=n: lambda g: g.reciprocal(out=rd[:, 0:n], in_=pd[0:64, 0:n]))(),
                         reads=[bpd], writes=[brd])
                    P.op("dve", (lambda rd=rd, ot=ot, po=po, n=n: lambda g: g.tensor_tensor(
                        out=ot[:, 0:n], in0=po[0:64, 0:n], in1=rd[:, 0:n], op=ALU.mult))(),
                        reads=[bpo, brd], writes=[bot])
                    self.dma(self.oTd[h * 64:(h + 1) * 64, t0:t0 + n], ot[:, 0:n], [bot], [self.dbuf["oTd"]])
                    t0 += n
        P.barrier()


    def gdn(self, l, s):
        P = self.P
        cfg = self.cfg
        L = cfg.seqs[s]
        NCH = L // 128
        NG = NCH * 16
        hv = self.hTd.rearrange("(c p) t -> p c t", p=128)
        win = self.wb["gdn_w_in"].rearrange("(k p) c -> p k c", p=128)
        with ExitStack() as stk:
            bk = Buf("gconst")
            cst = self.sb(stk, [128, 9, 128], F32, "gcst")
            self.dma(cst[:], self.gdn_c.rearrange("a p n -> p a n"), [], [bk])
            U = [cst[:, 0, :], cst[:, 1, :]]
            MB = [cst[:, 2, :], cst[:, 3, :]]
            S01 = [cst[:, 4, :], cst[:, 5, :]]
            onesf = self.sb(stk, [128, 128], F32, "onesf")
            self.mset("dve", onesf[:], 1.0, [bk])
            I4 = self.sb(stk, [128, 4, 128], F32, "I4")
            D32_4 = self.sb(stk, [128, 4, 128], F32, "D32_4")
            M64_4 = self.sb(stk, [128, 4, 128], F32, "M64_4")
            M128_4 = self.sb(stk, [128, 4, 128], F32, "M128_4")
            M4 = self.sb(stk, [128, 4, 128], F32, "M4")
            self.mset("dve", M4[:], 1.0, [bk])
            self.cp("act", M4[:, 0, :], cst[:, 4, :], [bk], [bk])
            self.cp("act", M4[:, 2, :], cst[:, 5, :], [bk], [bk])
            for u in range(4):
                self.cp("act", I4[:, u, :], self.identf[:], [self.b_const], [bk])
                self.cp("act", D32_4[:, u, :], cst[:, 6, :], [bk], [bk])
                self.cp("act", M64_4[:, u, :], cst[:, 7, :], [bk], [bk])
                self.cp("act", M128_4[:, u, :], cst[:, 8, :], [bk], [bk])
            dtb4 = self.sb(stk, [128, 4, 32], F32, "dtb4")
            nea4 = self.sb(stk, [128, 4, 32], F32, "nea4")
            for j in range(4):
                self.dma(dtb4[:, j, :], self.gdn_dt_bias[0:1, :].partition_broadcast(128), [], [bk])
                self.dma(nea4[:, j, :], self.gdn_a_log[0:1, :].partition_broadcast(128), [], [bk])
            self.actf(nea4[:], nea4[:], AF.Exp, [bk], [bk])
            self.amul(nea4[:], nea4[:], -1.0, [bk], [bk])
            wab = self.sb(stk, [128, 8, 64], BF16, "wab")
            self.dma(wab[:], self.wb["gdn_w_ab"].rearrange("(k p) c -> p k c", p=128), [self.dbuf["wb_gdn_w_ab"]], [bk])
            nwv = self.sb(stk, [128, 2, 128], F32, "nwv")
            for j in range(2):
                self.dma(nwv[:, j, :], self.gdn_norm_w[0:1, :].partition_broadcast(128), [], [bk])
            Be = self.sb(stk, [128, 2, NCH, 16], F32, "Be")
            nBe = self.sb(stk, [128, 2, NCH, 16], F32, "nBe")
            ngc = self.sb(stk, [128, 2, NCH, 16], F32, "ngc")
            egc = self.sb(stk, [128, 2, NCH, 16], F32, "egc")
            ekd = self.sb(stk, [128, 2, NCH, 16], F32, "ekd")
            ege = self.sb(stk, [128, 2, NCH, 16], F32, "ege")
            gcT = self.sb(stk, [48, L], F32, "gcT")
            bg = Buf("gates")

            def fl(t, d):
                return t[:, d, :, :].rearrange("p n h -> p (n h)")

            with ExitStack() as stk2:
                GF48 = self.sb(stk2, [128, NCH, 48], F32, "GF48")
                GB48 = self.sb(stk2, [128, NCH, 48], F32, "GB48")
                Gt = self.sb(stk2, [128, 2, NCH, 16], F32, "Gt")
                gc = self.sb(stk2, [128, 2, NCH, 16], F32, "gc")
                self.mset("dve", GF48[:], 0.0, [bg])
                self.mset("dve", GB48[:], 0.0, [bg])
                hts = [(self.sb(stk2, [128, 8, 512], BF16, "hT"), Buf("hT%d" % i)) for i in range(2)]
                tmp = self.sb(stk2, [128, 4, 32], F32, "gtmp")
                btmp = Buf("gtmp")
                t0 = 0
                i = 0
                while t0 < L:
                    n = min(512, L - t0)
                    nsub = n // 128
                    n0 = t0 // 128
                    hT, bh = hts[i % 2]
                    i += 1
                    self.dma(hT[:, :, 0:n], hv[:, :, 1 + t0:1 + t0 + n], [self.dbuf["hTd"]], [bh])
                    pt, bp = self.next_psf()
                    for j in range(nsub):
                        for k in range(8):
                            self.mm(pt[:, j * 64:(j + 1) * 64], hT[:, k, j * 128:(j + 1) * 128], wab[:, k, :],
                                    k == 0, k == 7, [bh, bk], [bp])
                    pv = pt[:, 0:nsub * 64].rearrange("p (j c) -> p j c", c=64)
                    self.tt("dve", tmp[:, 0:nsub, :], pv[:, :, 0:32], dtb4[:, 0:nsub, :], ALU.add, [bp, bk], [btmp])
                    self.actf(tmp[:, 0:nsub, :], tmp[:, 0:nsub, :], AF.Exp, [btmp], [btmp])
                    self.ts("dve", tmp[:, 0:nsub, :], tmp[:, 0:nsub, :], 1.0, None, ALU.add, None, [btmp], [btmp])
                    self.actf(tmp[:, 0:nsub, :], tmp[:, 0:nsub, :], AF.Ln, [btmp], [btmp])
                    for d in range(2):
                        self.tt("dve", Gt[:, d, n0:n0 + nsub, :], tmp[:, 0:nsub, d * 16:(d + 1) * 16],
                                nea4[:, 0:nsub, d * 16:(d + 1) * 16], ALU.mult, [btmp, bk], [bg])
                        self.actf(Be[:, d, n0:n0 + nsub, :], pv[:, :, 32 + d * 16:32 + (d + 1) * 16], AF.Sigmoid,
                                  [bp], [bg])
                    t0 += n
                self.cp("dve", GF48[:, :, 0:16], Gt[:, 0, :, :], [bg], [bg])
                self.cp("dve", GB48[:, :, 32:48], Gt[:, 1, :, :], [bg], [bg])
                for d in range(2):
                    pg, bpg = self.next_psf()
                    self.mm(pg[:, 0:NG], U[d], fl(Gt, d), True, True, [bg, bk], [bpg])
                    self.cp("act", fl(gc, d), pg[:, 0:NG], [bpg], [bg])
                    self.actf(fl(egc, d), pg[:, 0:NG], AF.Exp, [bpg], [bg])
                    pgt, bpgt = self.next_psf()
                    self.mm(pgt[:, 0:NG], onesf[:], fl(Gt, d), True, True, [bg, bk], [bpgt])
                    self.actf(fl(ege, d), pgt[:, 0:NG], AF.Exp, [bpgt], [bg])
                    self.tt("dve", fl(ekd, d), pgt[:, 0:NG], fl(gc, d), ALU.subtract, [bpgt, bg], [bg])
                    self.actf(fl(ekd, d), fl(ekd, d), AF.Exp, [bg], [bg])
                    self.amul(fl(ngc, d), fl(gc, d), -1.0, [bg], [bg])
                    self.amul(fl(nBe, d), fl(Be, d), -1.0, [bg], [bg])
                for n4 in range(0, NCH, 4):
                    pT, bpT = self.next_psf()
                    for n in range(n4, min(n4 + 4, NCH)):
                        cs = (n - n4) * 128
                        self.mm(pT[0:48, cs:cs + 128], GF48[:, n, :], U[0], True, False, [bg, bk], [bpT])
                        self.mm(pT[0:48, cs:cs + 128], GB48[:, n, :], U[1], False, True, [bg, bk], [bpT])
                    w = (min(n4 + 4, NCH) - n4) * 128
                    self.cp("act", gcT[:, n4 * 128:n4 * 128 + w], pT[0:48, 0:w], [bpT], [bg])
            P.barrier()

            DKS = 128.0 ** -0.5
            STOP = int(os.environ.get("GDN_STOP", "99"))
            lanes = 2 if L <= 2048 else 1
            LBs = []
            for li in range(lanes):
                LBs.append((self.sb(stk, [128, L], BF16, "qT"), self.sb(stk, [128, L], BF16, "kT"),
                            self.sb(stk, [128, NCH, 128], BF16, "ktok"), self.sb(stk, [128, NCH, 256], BF16, "vtok"),
                            self.sb(stk, [128, NCH, 2, 128], F32, "O"), Buf("qk%d" % li), Buf("O%d" % li)))

            def project(hk, LB):
                qT, kT, k_tok, v_tok, O, bqk, bO = LB
                with ExitStack() as stk2:
                    Wraw = self.sb(stk2, [128, 8, 256], BF16, "Wraw")
                    cwb = self.sb(stk2, [128, 3, 256], F32, "cwb")
                    Wt = self.sb(stk2, [128, 8, 3, 256], BF16, "Wt")
                    cbp = self.sb(stk2, [128, 2], F32, "cbp")
                    cbv = self.sb(stk2, [128, 256], F32, "cbv")
                    bW = Buf("W")
                    hts = [(self.sb(stk2, [128, 8, 514], BF16, "hTh"), Buf("hTh%d" % i)) for i in range(2)]
                    tq = self.sb(stk2, [128, 512], F32, "tq")
                    sq = self.sb(stk2, [128, 512], BF16, "sq")
                    rn = self.sb(stk2, [128, 512], F32, "rn")
                    tv = self.sb(stk2, [128, 256], F32, "tv")
                    btq = Buf("tq")
                    hi = 0
                    parts = [("q", hk * 128, 128), ("k", 1024 + hk * 128, 128), ("v", 2048 + 2 * hk * 128, 256)]
                    for pi, (pn, col0, wd_) in enumerate(parts):
                        self.dma(Wraw[:, :, 0:wd_], win[:, :, col0:col0 + wd_], [self.dbuf["wb_gdn_w_in"]], [bW])
                        for tap in range(3):
                            self.dma(cwb[:, tap, 0:wd_], self.gdn_conv_w[tap:tap + 1, col0:col0 + wd_].partition_broadcast(128),
                                     [], [bW])
                        if pi < 2:
                            self.dma(cbp[:, pi:pi + 1], self.gdn_conv_b[0:1, col0:col0 + 128].rearrange("o p -> p o"),
                                     [], [bW], slow=True)
                        else:
                            self.dma(cbv[:], self.gdn_conv_b[0:1, col0:col0 + 256].partition_broadcast(128), [], [bW])
                        for k in range(8):
                            for tap in range(3):
                                self.tt("dve", Wt[:, k, tap, 0:wd_], Wraw[:, k, 0:wd_], cwb[:, tap, 0:wd_], ALU.mult,
                                        [bW], [bW])
                        t0 = 0
                        while t0 < L:
                            n = min(512, L - t0)
                            nsub = n // 128
                            n0 = t0 // 128
                            hT, bh = hts[hi % 2]
                            hi += 1
                            self.dma(hT[:, :, 0:n + 2], hv[:, :, t0:t0 + n + 2], [self.dbuf["hTd"]], [bh])
                            if pi < 2:
                                pt, bp = self.next_psf()
                                idx = 0
                                for tap in range(3):
                                    for k in range(8):
                                        self.mm(pt[:, 0:n], Wt[:, k, tap, 0:128], hT[:, k, tap:tap + n],
                                                idx == 0, idx == 23, [bW, bh], [bp])
                                        idx += 1
                                self.actf(tq[:, 0:n], pt[:, 0:n], AF.Silu, [bp, bW], [btq], bias=cbp[:, pi:pi + 1])
                                self.tt("pool", sq[:, 0:n], tq[:, 0:n], tq[:, 0:n], ALU.mult, [btq], [btq])
                                pss, bpss = self.next_psf()
                                self.mm(pss[:, 0:n], self.ones_bf[:], sq[:, 0:n], True, True, [btq, self.b_const], [bpss])
                                self.ts("dve", rn[:, 0:n], pss[:, 0:n], EPS, None, ALU.add, None, [bpss], [btq])
                                P.op("act", (lambda rn=rn, n=n: lambda g: g.sqrt(out=rn[:, 0:n], in_=rn[:, 0:n]))(),
                                     reads=[btq], writes=[btq])
                                self.recip(rn[:, 0:n], rn[:, 0:n], [btq], [btq])
                                if pi == 0:
                                    self.stt("dve", qT[:, t0:t0 + n], tq[:, 0:n], DKS, rn[:, 0:n], ALU.mult, ALU.mult,
                                             [btq], [bqk])
                                else:
                                    self.tt("dve", kT[:, t0:t0 + n], tq[:, 0:n], rn[:, 0:n], ALU.mult, [btq], [bqk])
                                    pb, bpb = self.next_psb()
                                    for j in range(nsub):
                                        self.tr(pb[:, j * 128:(j + 1) * 128], kT[:, t0 + j * 128:t0 + (j + 1) * 128],
                                                self.ident[:], [bqk, self.b_const], [bpb])
                                    self.cp("act", k_tok[:, n0:n0 + nsub, :],
                                            pb[:, 0:nsub * 128].rearrange("p (j d) -> p j d", d=128), [bpb], [bqk])
                            else:
                                for j in range(nsub):
                                    pt, bp = self.next_psf()
                                    idx = 0
                                    for tap in range(3):
                                        for k in range(8):
                                            self.mm(pt[:, 0:256], hT[:, k, tap + j * 128:tap + (j + 1) * 128],
                                                    Wt[:, k, tap, 0:256], idx == 0, idx == 23, [bW, bh], [bp])
                                            idx += 1
                                    self.tt("dve", tv[:], pt[:, 0:256], cbv[:], ALU.add, [bp, bW], [btq])
                                    self.actf(v_tok[:, n0 + j, :], tv[:], AF.Silu, [btq], [bqk])
                            t0 += n
                P.barrier()

            def steps(hk, LB, stk3):
                qT, kT, k_tok, v_tok, O, bqk, bO = LB
                if True:
                    def t4(dt, nm):
                        return self.sb(stk3, [128, 4, 128], dt, nm)
                    kkq = t4(F32, "kkq"); bkkq = Buf("kkq")
                    dTi = t4(F32, "dTi"); bdTi = Buf("dTi")
                    dTs = t4(F32, "dTs"); bdTs = Buf("dTs")
                    intraT = t4(BF16, "intraT"); bintra = Buf("intraT")
                    XA = t4(F32, "XA"); XB = t4(F32, "XB")
                    YA = t4(F32, "YA"); YB = t4(F32, "YB")
                    PA = t4(F32, "PA"); PB = t4(F32, "PB")
                    E1b = t4(BF16, "E1b"); E2b = t4(BF16, "E2b")
                    Pb0 = t4(BF16, "Pb0"); Pb1 = t4(BF16, "Pb1"); Qb = t4(BF16, "Qb"); Gtb = t4(BF16, "Gtb")
                    bXA, bXB, bYA, bYB, bPA, bPB = [Buf(x) for x in ("XA", "XB", "YA", "YB", "PA", "PB")]
                    bE1, bE2, bPb0, bPb1, bQb, bGtb = [Buf(x) for x in ("E1", "E2", "Pb0", "Pb1", "Qb", "Gtb")]
                    Ttb = t4(BF16, "Ttb"); bTtb = Buf("Ttb")
                    rw = t4(BF16, "rw"); brw = Buf("rw")
                    kd = t4(BF16, "kd"); bkd = Buf("kd")
                    uf = t4(F32, "uf"); buf_ = Buf("uf")
                    wbt = t4(BF16, "wbt"); bwbt = Buf("wbt")
                    wT = t4(BF16, "wT"); bwT = Buf("wT")
                    vnew = t4(BF16, "vnew"); bvnew = Buf("vnew")
                    tB = t4(F32, "tB"); btB = Buf("tB")
                    tC = t4(F32, "tC"); btC = Buf("tC")
                    Sf = t4(F32, "Sf"); bSf = Buf("Sf")
                    Sbf = t4(BF16, "Sbf"); bSbf = Buf("Sbf")
                    esel = self.sb(stk3, [48, 4, 128], F32, "esel")
                    besel = Buf("esel")
                    for d in range(2):
                        self.dma(esel[:, 2 * d:2 * d + 2, :], self.gdn_esel[:, d * 16 + 2 * hk:d * 16 + 2 * hk + 2, :], [], [besel])
                    self.mset("dve", Sf[:], 0.0, [bSf])
                    self.mset("dve", Sbf[:], 0.0, [bSbf])
                    self.mset("pool", O[:], 0.0, [bO])

                    def f4(t):
                        return t[:, :, :].rearrange("p u d -> p (u d)")

                    yield
                    for st in range(NCH):
                        if STOP < 3:
                            break
                        chs = [st, st, NCH - 1 - st, NCH - 1 - st]
                        dirs = [0, 0, 1, 1]
                        vls = [0, 1, 0, 1]
                        hds = [2 * hk + v for v in vls]

                        def sc(t, u):
                            return t[:, dirs[u], chs[u], hds[u]:hds[u] + 1]
                        pk, bpk = self.next_psf()
                        for ci, n in enumerate((chs[0], chs[2])):
                            ksl = kT[:, n * 128:(n + 1) * 128]
                            self.mm(pk[:, (2 * ci) * 128:(2 * ci + 1) * 128], ksl, ksl, True, True, [bqk], [bpk])
                            self.mm(pk[:, (2 * ci + 1) * 128:(2 * ci + 2) * 128], ksl, qT[:, n * 128:(n + 1) * 128],
                                    True, True, [bqk], [bpk])
                        self.tt("dve", f4(kkq), pk[:, :], f4(M4), ALU.mult, [bpk, bk], [bkkq])
                        yield
                        pdf, bpdf = self.next_psf()
                        for u in range(4):
                            m = dirs[u] * 16 + hds[u]
                            cs = chs[u] * 128
                            self.mm(pdf[:, u * 128:(u + 1) * 128], esel[:, u, :], gcT[:, cs:cs + 128], True, False,
                                    [besel, bg], [bpdf])
                            self.mm(pdf[:, u * 128:(u + 1) * 128], self.identf[:], MB[dirs[u]], False, True,
                                    [bk, self.b_const], [bpdf])
                        for u in range(4):
                            self.actf(dTi[:, u, :], pdf[:, u * 128:(u + 1) * 128], AF.Exp, [bpdf, bg], [bdTi],
                                      bias=sc(ngc, u))
                        yield
                        if STOP < 4:
                            continue
                        Y0, bY0, X0, bX0 = YB, bYB, XB, bXB
                        for u in range(4):
                            self.stt("dve", Y0[:, u, :], kkq[:, 2 * (u // 2), :], sc(nBe, u), dTi[:, u, :],
                                     ALU.mult, ALU.mult, [bkkq, bg, bdTi], [bY0])
                            self.tt("pool", intraT[:, u, :], kkq[:, 2 * (u // 2) + 1, :], dTi[:, u, :], ALU.mult,
                                    [bkkq, bdTi], [bintra])
                        if STOP < 5:
                            continue
                        px, bpx = self.next_psf()
                        for u in range(4):
                            self.tr(px[:, u * 128:(u + 1) * 128], Y0[:, u, :], self.identf[:], [bY0, self.b_const], [bpx])
                        self.cp("act", f4(X0), px[:, :], [bpx], [bX0])
                        yield
                        self.tt("dve", f4(YA), f4(Y0), f4(D32_4), ALU.mult, [bY0, bk], [bYA])
                        self.tt("dve", f4(XA), f4(X0), f4(D32_4), ALU.mult, [bX0, bk], [bXA])
                        self.tt("pool", f4(E1b), f4(X0), f4(M64_4), ALU.mult, [bX0, bk], [bE1])
                        self.tt("pool", f4(E2b), f4(X0), f4(M128_4), ALU.mult, [bX0, bk], [bE2])
                        Y, bY, X, bX = YA, bYA, XA, bXA
                        Pm, bPm = PA, bPA
                        self.tt("dve", f4(Pm), f4(Y), f4(I4), ALU.add, [bY, bk], [bPm])
                        yield
                        for lev in range(1, 5):
                            Xn, bXn = (XB, bXB) if X is XA else (XA, bXA)
                            Yn, bYn = (YB, bYB) if Y is YA else (YA, bYA)
                            Pn, bPn = (PB, bPB) if Pm is PA else (PA, bPA)
                            pxn, bpxn = self.next_psf()
                            for u in range(4):
                                self.mm(pxn[:, u * 128:(u + 1) * 128], Y[:, u, :], X[:, u, :], True, True, [bY, bX], [bpxn])
                            if lev < 4:
                                pyn, bpyn = self.next_psf()
                                for u in range(4):
                                    self.mm(pyn[:, u * 128:(u + 1) * 128], X[:, u, :], Y[:, u, :], True, True, [bY, bX], [bpyn])
                            self.cp("act", f4(Xn), pxn[:, :], [bpxn], [bXn])
                            if lev < 4:
                                self.cp("act", f4(Yn), pyn[:, :], [bpyn], [bYn])
                            yield
                            ppn, bppn = self.next_psf()
                            for u in range(4):
                                self.mm(ppn[:, u * 128:(u + 1) * 128], Xn[:, u, :], Pm[:, u, :], True, True,
                                        [bXn, bPm], [bppn])
                            if lev < 4:
                                self.tt("dve", f4(Pn), ppn[:, :], f4(Pm), ALU.add, [bppn, bPm], [bPn])
                            else:
                                self.tt("dve", f4(Pb0), ppn[:, :], f4(Pm), ALU.add, [bppn, bPm], [bPb0])
                            yield
                            X, bX, Pm, bPm = Xn, bXn, Pn, bPn
                            if lev < 4:
                                Y, bY = Yn, bYn
                        pq0, bpq0 = self.next_psb()
                        for u in range(4):
                            self.tr(pq0[:, u * 128:(u + 1) * 128], Pb0[:, u, :], self.ident[:], [bPb0, self.b_const], [bpq0])
                        self.cp("act", f4(Qb), pq0[:, 0:512], [bpq0], [bQb])
                        pg2, bpg2 = self.next_psf()
                        for u in range(4):
                            self.mm(pg2[:, u * 128:(u + 1) * 128], E1b[:, u, :], Pb0[:, u, :], True, True, [bE1, bPb0], [bpg2])
                        self.cp("act", f4(Gtb), pg2[:, :], [bpg2], [bGtb])
                        yield
                        pp1, bpp1 = self.next_psf()
                        for u in range(4):
                            self.mm(pp1[:, u * 128:(u + 1) * 128], Qb[:, u, :], Gtb[:, u, :], True, True, [bQb, bGtb], [bpp1])
                        self.tt("dve", f4(Pb1), pp1[:, :], f4(Pb0), ALU.add, [bpp1, bPb0], [bPb1])
                        yield
                        pq1, bpq1 = self.next_psb()
                        for u in range(4):
                            self.tr(pq1[:, u * 128:(u + 1) * 128], Pb1[:, u, :], self.ident[:], [bPb1, self.b_const], [bpq1])
                        self.cp("act", f4(Qb), pq1[:, 0:512], [bpq1], [bQb])
                        pg3, bpg3 = self.next_psf()
                        for u in range(4):
                            self.mm(pg3[:, u * 128:(u + 1) * 128], E2b[:, u, :], Pb1[:, u, :], True, True, [bE2, bPb1], [bpg3])
                        self.cp("act", f4(Gtb), pg3[:, :], [bpg3], [bGtb])
                        yield
                        pp2, bpp2 = self.next_psf()
                        for u in range(4):
                            self.mm(pp2[:, u * 128:(u + 1) * 128], Qb[:, u, :], Gtb[:, u, :], True, True, [bQb, bGtb], [bpp2])
                        self.tt("dve", f4(Ttb), pp2[:, :], f4(Pb1), ALU.add, [bpp2, bPb1], [bTtb])
                        yield
                        if STOP < 6:
                            continue
                        for u in range(4):
                            self.actf(rw[:, u, :], k_tok[:, chs[u], :], AF.Identity, [bqk, bg], [brw], scale=sc(egc, u))
                            self.actf(kd[:, u, :], k_tok[:, chs[u], :], AF.Identity, [bqk, bg], [bkd], scale=sc(ekd, u))
                        pz0, bpz0 = self.next_psf()
                        pz1, bpz1 = self.next_psf()
                        for u in range(4):
                            pz, bpz = (pz0, bpz0) if u < 2 else (pz1, bpz1)
                            uu = u % 2
                            self.mm(pz[:, uu * 256:uu * 256 + 128], Ttb[:, u, :],
                                    v_tok[:, chs[u], vls[u] * 128:(vls[u] + 1) * 128], True, True, [bTtb, bqk], [bpz])
                            self.mm(pz[:, uu * 256 + 128:uu * 256 + 256], Ttb[:, u, :], rw[:, u, :], True, True,
                                    [bTtb, brw], [bpz])
                        for u in range(4):
                            pz, bpz = (pz0, bpz0) if u < 2 else (pz1, bpz1)
                            uu = u % 2
                            self.ts("dve", uf[:, u, :], pz[:, uu * 256:uu * 256 + 128], sc(Be, u), None, ALU.mult, None,
                                    [bpz, bg], [buf_])
                            self.ts("dve", wbt[:, u, :], pz[:, uu * 256 + 128:uu * 256 + 256], sc(Be, u), None, ALU.mult,
                                    None, [bpz, bg], [bwbt])
                        pw, bpw = self.next_psb()
                        for u in range(4):
                            self.tr(pw[:, u * 128:(u + 1) * 128], wbt[:, u, :], self.ident[:], [bwbt, self.b_const], [bpw])
                        self.cp("act", f4(wT), pw[:, 0:512], [bpw], [bwT])
                        yield
                        if STOP < 7:
                            continue
                        p1, bp1 = self.next_psf()
                        for u in range(4):
                            self.mm(p1[:, u * 128:(u + 1) * 128], wT[:, u, :], Sbf[:, u, :], True, True, [bwT, bSbf], [bp1])
                        self.tt("dve", f4(vnew), f4(uf), p1[:, :], ALU.subtract, [buf_, bp1], [bvnew])
                        yield
                        pa, bpa = self.next_psf()
                        for u in range(4):
                            self.mm(pa[:, u * 128:(u + 1) * 128], qT[:, chs[u] * 128:(chs[u] + 1) * 128], Sbf[:, u, :],
                                    True, True, [bqk, bSbf], [bpa])
                        pbm, bpbm = self.next_psf()
                        for u in range(4):
                            self.mm(pbm[:, u * 128:(u + 1) * 128], intraT[:, u, :], vnew[:, u, :], True, True,
                                    [bintra, bvnew], [bpbm])
                        self.cp("act", f4(tB), pbm[:, :], [bpbm], [btB])
                        yield
                        for u in range(4):
                            self.stt("dve", tC[:, u, :], pa[:, u * 128:(u + 1) * 128], sc(egc, u), tB[:, u, :],
                                     ALU.mult, ALU.add, [bpa, bg, btB], [btC])
                        for d in range(2):
                            n = chs[2 * d]
                            ov_ = O[:, n, :, :].rearrange("p v d -> p (v d)")
                            self.tt("pool", ov_, ov_, tC[:, 2 * d:2 * d + 2, :].rearrange("p u d -> p (u d)"), ALU.add,
                                    [btC, bO], [bO])
                        pS, bpS = self.next_psf()
                        for u in range(4):
                            self.mm(pS[:, u * 128:(u + 1) * 128], kd[:, u, :], vnew[:, u, :], True, True, [bkd, bvnew], [bpS])
                        for u in range(4):
                            self.stt("dve", Sf[:, u, :], Sf[:, u, :], sc(ege, u), pS[:, u * 128:(u + 1) * 128],
                                     ALU.mult, ALU.add, [bSf, bg, bpS], [bSf])
                        self.cp("act", f4(Sbf), f4(Sf), [bSf], [bSbf])
                        yield

            def output(hk, LB):
                qT, kT, k_tok, v_tok, O, bqk, bO = LB
                if STOP < 8:
                    return
                with ExitStack() as stk4:
                    Wz = self.sb(stk4, [128, 8, 256], BF16, "Wz")
                    bWz = Buf("Wz")
                    zc = 4096 + 2 * hk * 128
                    self.dma(Wz[:], win[:, :, zc:zc + 256], [self.dbuf["wb_gdn_w_in"]], [bWz])
                    hts = [(self.sb(stk4, [128, 8, 512], BF16, "hTo"), Buf("hTo%d" % i)) for i in range(2)]
                    sz = self.sb(stk4, [128, 256], F32, "sz"); bsz = Buf("sz")
                    sm = self.sb(stk4, [128, 8], F32, "sm"); bsm = Buf("sm")
                    junk = self.sb(stk4, [128, 128], BF16, "junk")
                    on = self.sb(stk4, [128, 2, 128], F32, "on"); bon = Buf("on")
                    ob = self.sb(stk4, [128, 256], BF16, "ob"); bob = Buf("ob")
                    oTs = [(self.sb(stk4, [128, 2, 512], BF16, "oTs"), Buf("oTs%d" % i)) for i in range(2)]
                    ovw = self.oTd[2 * hk * 128:(2 * hk + 2) * 128, :].rearrange("(c p) t -> p c t", p=128)
                    t0 = 0
                    i = 0
                    while t0 < L:
                        n = min(512, L - t0)
                        nsub = n // 128
                        n0 = t0 // 128
                        hT, bh = hts[i % 2]
                        oTt, boTt = oTs[i % 2]
                        i += 1
                        self.dma(hT[:, :, 0:n], hv[:, :, 1 + t0:1 + t0 + n], [self.dbuf["hTd"]], [bh])
                        pb, bpb = self.next_psb()
                        for j in range(nsub):
                            pz, bpz = self.next_psf()
                            for k in range(8):
                                self.mm(pz[:, 0:256], hT[:, k, j * 128:(j + 1) * 128], Wz[:, k, :], k == 0, k == 7,
                                        [bh, bWz], [bpz])
                            self.actf(sz[:], pz[:, 0:256], AF.Silu, [bpz], [bsz])
                            for v in range(2):
                                self.actf(junk[:], O[:, n0 + j, v, :], AF.Square, [bO], [bsm], accum_out=sm[:, v:v + 1])
                            self.ts("dve", sm[:, 2:4], sm[:, 0:2], 1.0 / 128, EPS, ALU.mult, ALU.add, [bsm], [bsm])
                            P.op("act", lambda g: g.sqrt(out=sm[:, 2:4], in_=sm[:, 2:4]), reads=[bsm], writes=[bsm])
                            self.recip(sm[:, 4:6], sm[:, 2:4], [bsm], [bsm])
                            for v in range(2):
                                self.stt("dve", on[:, v, :], O[:, n0 + j, v, :], sm[:, 4 + v:5 + v], nwv[:, v, :],
                                         ALU.mult, ALU.mult, [bO, bsm, bk], [bon])
                            self.tt("pool", ob[:], on[:, :, :].rearrange("p v d -> p (v d)"), sz[:], ALU.mult,
                                    [bon, bsz], [bob])
                            for v in range(2):
                                self.tr(pb[:, v * 512 + j * 128:v * 512 + (j + 1) * 128], ob[:, v * 128:(v + 1) * 128],
                                        self.ident[:], [bob, self.b_const], [bpb])
                        for v in range(2):
                            self.cp("act", oTt[:, v, 0:n], pb[:, v * 512:v * 512 + n], [bpb], [boTt])
                        self.dma(ovw[:, :, t0:t0 + n], oTt[:, :, 0:n], [boTt], [self.dbuf["oTd"]])
                        t0 += n
                P.barrier()

            for hk0 in range(0, 8, lanes):
                if STOP < 2:
                    break
                for li in range(lanes):
                    project(hk0 + li, LBs[li])
                with ExitStack() as stk3:
                    active = [steps(hk0 + li, LBs[li], stk3) for li in range(lanes)]
                    while active:
                        for gen in list(active):
                            try:
                                next(gen)
                            except StopIteration:
                                active.remove(gen)
                P.barrier()
                for li in range(lanes):
                    output(hk0 + li, LBs[li])
        P.barrier()


    def hy_dims(self, L):
        TCN = L // 128
        FC = TCN + 1
        return TCN, FC, FC * 128

    def range_reduce(self, a, kk, b):
        C = 12582912.0
        self.ts("dve", kk, a, 1.0 / (2.0 * math.pi), C, ALU.mult, ALU.add, [b], [b])
        self.ts("dve", kk, kk, C, None, ALU.subtract, None, [b], [b])
        self.stt("dve", a, kk, -2.0 * math.pi, a, ALU.mult, ALU.add, [b], [b])

    def hyena_filter(self, L):
        P = self.P
        TCN, FC, F = self.hy_dims(L)
        cons = self.hyc[L]
        hs_d, hd_d = self.hy_hs[L], self.hy_hd[L]
        bhs = self.dbuf["hy_hs%d" % L]
        bkk = self.dbuf["hy_K%d" % L]
        Kre_d, Kim_d = self.hy_kre[L], self.hy_kim[L]
        TWO_PI = 2.0 * math.pi
        with ExitStack() as stk:
            bk = Buf("fconst")
            fw1 = self.sb(stk, [33, 64], F32, "fw1")
            fw2 = self.sb(stk, [64, 64], F32, "fw2")
            fw3 = self.sb(stk, [64, 2048], F32, "fw3")
            fv = self.sb(stk, [64, 4], F32, "fv")
            self.dma(fw1[:], self.hy_fw1[:, :], [], [bk])
            self.dma(fw2[:], self.hy_fw2[:, :], [], [bk])
            self.dma(fw3[:], self.hy_fw3[:, :], [], [bk])
            for i, src in enumerate((self.hy_fb1, self.hy_ff1, self.hy_fb2, self.hy_ff2)):
                self.dma(fv[:, i:i + 1], src[0:1, :].rearrange("o p -> p o"), [], [bk], slow=True)
            rates = self.sb(stk, [128, 1024], F32, "rates")
            self.dma(rates[:], self.hy_rates[0:1, :].partition_broadcast(128), [], [bk])
            ntn = self.sb(stk, [128, TCN], F32, "ntn")
            self.dma(ntn[:], cons["tn"][:, :], [], [bk])
            self.amul(ntn[:], ntn[:], -1.0, [bk], [bk])
            negpi = self.sb(stk, [128, 1], F32, "negpi")
            self.mset("dve", negpi[:], -math.pi, [bk])
            featsT = self.sb(stk, [33, L], F32, "featsT")
            self.dma(featsT[:], cons["feats"][:, :], [], [bk])
            z2T = self.sb(stk, [64, L], F32, "z2T")
            bz = Buf("z2T")
            a1 = self.sb(stk, [64, 512], F32, "a1")
            z1 = self.sb(stk, [64, 512], F32, "z1")
            kk = self.sb(stk, [64, 512], F32, "kk")
            ba = Buf("a1")
            t0 = 0
            while t0 < L:
                n = min(512, L - t0)
                ps, bp = self.next_psf()
                self.mm(ps[0:64, 0:n], fw1[:, :], featsT[:, t0:t0 + n], True, True, [bk], [bp])
                self.ts("dve", a1[:, 0:n], ps[0:64, 0:n], fv[:, 0:1], fv[:, 1:2], ALU.add, ALU.mult, [bp, bk], [ba])
                self.range_reduce(a1[:, 0:n], kk[:, 0:n], ba)
                self.actf(z1[:, 0:n], a1[:, 0:n], AF.Sin, [ba, bk], [ba], scale=0.999999)
                ps2, bp2 = self.next_psf()
                self.mm(ps2[0:64, 0:n], fw2[:, :], z1[:, 0:n], True, True, [bk, ba], [bp2])
                self.ts("dve", a1[:, 0:n], ps2[0:64, 0:n], fv[:, 2:3], fv[:, 3:4], ALU.add, ALU.mult, [bp2, bk], [ba])
                self.range_reduce(a1[:, 0:n], kk[:, 0:n], ba)
                self.actf(z2T[:, t0:t0 + n], a1[:, 0:n], AF.Sin, [ba, bk], [bz], scale=0.999999)
                t0 += n
            wnd = self.sb(stk, [128, 1024], F32, "wnd")
            bwn = Buf("wnd")
            hfb = self.sb(stk, [128, 2048], F32, "hfb")
            bhf = Buf("hfb")
            hst = [(self.sb(stk, [128, 1024], BF16, "hst"), Buf("hst%d" % i)) for i in range(2)]
            hdt = [(self.sb(stk, [128, 1024], BF16, "hdt"), Buf("hdt%d" % i)) for i in range(2)]
            for tc in range(TCN):
                self.actf(wnd[:], rates[:], AF.Exp, [bk], [bwn], scale=ntn[:, tc:tc + 1])
                for q4 in range(4):
                    ps, bp = self.next_psf()
                    self.mm(ps[:, :], z2T[:, tc * 128:(tc + 1) * 128], fw3[:, q4 * 512:(q4 + 1) * 512], True, True,
                            [bz, bk], [bp])
                    self.tt("dve", hfb[:, q4 * 512:(q4 + 1) * 512], ps[:, :], wnd[:, (q4 % 2) * 512:(q4 % 2 + 1) * 512],
                            ALU.mult, [bp, bwn], [bhf])
                if tc == 0:
                    self.mset("dve", hfb[0:1, 1024:2048], 0.0, [bhf])
                hs_t, bhs_t = hst[tc % 2]
                hd_t, bhd_t = hdt[tc % 2]
                self.tt("pool", hs_t[:], hfb[:, 0:1024], hfb[:, 1024:2048], ALU.add, [bhf], [bhs_t])
                self.tt("pool", hd_t[:], hfb[:, 0:1024], hfb[:, 1024:2048], ALU.subtract, [bhf], [bhd_t])
                self.dma(hs_d[tc * 128:(tc + 1) * 128, :], hs_t[:], [bhs_t], [bhs])
                self.dma(hd_d[tc * 128:(tc + 1) * 128, :], hd_t[:], [bhd_t], [bhs])
        P.barrier()
        with ExitStack() as stk:
            bk = Buf("sconst")
            wN = self.sb(stk, [128, FC], F32, "wN")
            nwN = self.sb(stk, [128, FC], F32, "nwN")
            self.dma(wN[:], cons["wN"][:, :], [], [bk])
            self.amul(nwN[:], wN[:], -1.0, [bk], [bk])
            HS = self.sb(stk, [128, TCN, 512], BF16, "HS")
            HD = self.sb(stk, [128, TCN, 512], BF16, "HD")
            bH = Buf("HS")
            tabs = [(self.sb(stk, [128, TCN, 256], BF16, "tc"), self.sb(stk, [128, TCN, 256], BF16, "tsn"), Buf("tab%d" % i))
                    for i in range(2)]
            kts = [(self.sb(stk, [128, 512], F32, "kre"), self.sb(stk, [128, 512], F32, "kim"), Buf("kt%d" % i))
                   for i in range(2)]
            cv = self.hy_tc[L].rearrange("b p (r c) -> b p r c", r=FC)
            sv = self.hy_ts[L].rearrange("b p (r c) -> b p r c", r=FC)
            ti = 0
            ki = 0
            for chh in range(2):
                self.dma(HS[:], hs_d[:, chh * 512:(chh + 1) * 512].rearrange("(c p) n -> p c n", p=128), [bhs], [bH])
                self.dma(HD[:], hd_d[:, chh * 512:(chh + 1) * 512].rearrange("(c p) n -> p c n", p=128), [bhs], [bH])
                for fb in range(0, FC, 2):
                    nf = min(2, FC - fb)
                    TCb, TSb, btab = tabs[ti % 2]
                    ti += 1
                    self.dma(TCb[:, :, 0:nf * 128], cv[fb // 2][:, 0:TCN, 0:nf * 128], [], [btab])
                    self.dma(TSb[:, :, 0:nf * 128], sv[fb // 2][:, 0:TCN, 0:nf * 128], [], [btab])
                    for fi in range(nf):
                        fc = fb + fi
                        pc, bpc = self.next_psf()
                        for tc in range(TCN):
                            self.mm(pc[:, :], TCb[:, tc, fi * 128:(fi + 1) * 128], HS[:, tc, :], tc == 0, tc == TCN - 1,
                                    [btab, bH], [bpc])
                        pq, bpq = self.next_psf()
                        for tc in range(TCN):
                            self.mm(pq[:, :], TSb[:, tc, fi * 128:(fi + 1) * 128], HD[:, tc, :], tc == 0, tc == TCN - 1,
                                    [btab, bH], [bpq])
                        kre, kim, bkt = kts[ki % 2]
                        ki += 1
                        self.ts("dve", kre[:], pc[:, :], wN[:, fc:fc + 1], None, ALU.mult, None, [bpc, bk], [bkt])
                        self.ts("dve", kim[:], pq[:, :], nwN[:, fc:fc + 1], None, ALU.mult, None, [bpq, bk], [bkt])
                        self.dma(Kre_d[fc * 128:(fc + 1) * 128, chh * 512:(chh + 1) * 512], kre[:], [bkt], [bkk])
                        self.dma(Kim_d[fc * 128:(fc + 1) * 128, chh * 512:(chh + 1) * 512], kim[:], [bkt], [bkk])
        P.barrier()

    def hyena(self, l, s):
        P = self.P
        cfg = self.cfg
        L = cfg.seqs[s]
        TCN, FC, F = self.hy_dims(L)
        CG = 512 if L <= 2048 else 256
        CGC = CG // 128
        TT = min(L, 512 if L <= 2048 else 256)
        hv = self.hTd.rearrange("(c p) t -> p c t", p=128)
        win = self.wb["hy_w_in"].rearrange("(k p) c -> p k c", p=128)
        Kre_d, Kim_d = self.hy_kre[L], self.hy_kim[L]
        bkk = self.dbuf["hy_K%d" % L]
        btabd = self.dbuf["hy_tab%d" % L]
        cv = self.hy_tc[L].rearrange("b p (r c) -> b p r c", r=FC)
        sv = self.hy_ts[L].rearrange("b p (r c) -> b p r c", r=FC)
        with ExitStack() as stk:
            skp = self.sb(stk, [128, 8], F32, "skip")
            cbp = self.sb(stk, [128, 24], F32, "cbp")
            bk = Buf("hconst")
            self.dma(skp[:], self.hy_skip[0:1, :].rearrange("o (c p) -> p (o c)", p=128), [], [bk], slow=True)
            self.dma(cbp[:], self.hy_conv_b[0:1, :].rearrange("o (c p) -> p (o c)", p=128), [], [bk], slow=True)
            vg = self.sb(stk, [128, TCN, CG], BF16, "vg")
            vgT = self.sb(stk, [128, CGC, L], BF16, "vgT")
            x0T = self.sb(stk, [128, CGC, L], BF16, "x0T")
            Yr = self.sb(stk, [128, FC, CG], BF16, "Yr")
            Yi = self.sb(stk, [128, FC, CG], BF16, "Yi")
            bvg = Buf("vg")
            bY = Buf("Y")
            for g in range(1024 // CG):
                c0 = g * CG
                with ExitStack() as stk2:
                    Wraw = self.sb(stk2, [128, 8, 128], BF16, "Wraw")
                    cwb = self.sb(stk2, [128, 3, 128], F32, "cwb")
                    Wts = [self.sb(stk2, [128, 8, 3, 128], BF16, "Wt%d" % i) for i in range(3)]
                    bW = Buf("W")
                    hts = [(self.sb(stk2, [128, 8, 514], BF16, "hTh"), Buf("hTh%d" % i)) for i in range(2)]
                    t1 = self.sb(stk2, [128, 512], F32, "t1")
                    bt1 = Buf("t1")
                    hi = 0
                    for cc in range(CGC):
                        ch0 = c0 + cc * 128
                        cols = [ch0, 1024 + ch0, 2048 + ch0]
                        for pi in range(3):
                            self.dma(Wraw[:], win[:, :, cols[pi]:cols[pi] + 128], [self.dbuf["wb_hy_w_in"]], [bW])
                            for tap in range(3):
                                self.dma(cwb[:, tap, :], self.hy_conv_w[tap:tap + 1, cols[pi]:cols[pi] + 128].partition_broadcast(128),
                                         [], [bW])
                            for k in range(8):
                                for tap in range(3):
                                    self.tt("dve", Wts[pi][:, k, tap, :], Wraw[:, k, :], cwb[:, tap, :], ALU.mult, [bW], [bW])
                        t0 = 0
                        while t0 < L:
                            n = min(512, L - t0)
                            nsub = n // 128
                            n0 = t0 // 128
                            hT, bh = hts[hi % 2]
                            hi += 1
                            self.dma(hT[:, :, 0:n + 2], hv[:, :, t0:t0 + n + 2], [self.dbuf["hTd"]], [bh])
                            pss = []
                            for pi in (1, 2, 0):
                                pt, bp = self.next_psf()
                                idx = 0
                                for tap in range(3):
                                    for k in range(8):
                                        self.mm(pt[:, 0:n], Wts[pi][:, k, tap, :], hT[:, k, tap:tap + n], idx == 0, idx == 23,
                                                [bW, bh], [bp])
                                        idx += 1
                                pss.append((pt, bp))
                            (p1, bp1), (p2, bp2), (p0, bp0) = pss
                            cb = lambda pi: cbp[:, (cols[pi] // 128):(cols[pi] // 128) + 1]
                            self.actf(t1[:, 0:n], p1[:, 0:n], AF.Identity, [bp1, bk], [bt1], bias=cb(1))
                            self.stt("dve", vgT[:, cc, t0:t0 + n], p2[:, 0:n], cb(2), t1[:, 0:n], ALU.add, ALU.mult,
                                     [bp2, bk, bt1], [bvg])
                            self.actf(x0T[:, cc, t0:t0 + n], p0[:, 0:n], AF.Identity, [bp0, bk], [bvg], bias=cb(0))
                            pb, bpb = self.next_psb()
                            for j in range(nsub):
                                self.tr(pb[:, j * 128:(j + 1) * 128], vgT[:, cc, t0 + j * 128:t0 + (j + 1) * 128], self.ident[:],
                                        [bvg, self.b_const], [bpb])
                            self.cp("act", vg[:, n0:n0 + nsub, cc * 128:(cc + 1) * 128],
                                    pb[:, 0:nsub * 128].rearrange("p (j d) -> p j d", d=128), [bpb], [bvg])
                            t0 += n
                P.barrier()
                with ExitStack() as stk2:
                    tabs = [(self.sb(stk2, [128, TCN, 256], BF16, "tc"), self.sb(stk2, [128, TCN, 256], BF16, "tsn"),
                             Buf("tab%d" % i)) for i in range(2)]
                    kts = [(self.sb(stk2, [128, CG], F32, "kre"), self.sb(stk2, [128, CG], F32, "kim"), Buf("kt%d" % i))
                           for i in range(2)]
                    tms = [[self.sb(stk2, [128, CG], F32, "tm%d" % j) for j in range(4)] for i in range(2)]
                    btms = [Buf("tm%d" % i) for i in range(2)]
                    ti = 0
                    ki = 0
                    for fb in range(0, FC, 2):
                        nf = min(2, FC - fb)
                        TCb, TSb, btab = tabs[ti % 2]
                        ti += 1
                        self.dma(TCb[:, :, 0:nf * 128], cv[fb // 2][:, 0:TCN, 0:nf * 128], [], [btab])
                        self.dma(TSb[:, :, 0:nf * 128], sv[fb // 2][:, 0:TCN, 0:nf * 128], [], [btab])
                        for fi in range(nf):
                            fc = fb + fi
                            kre, kim, bkt = kts[ki % 2]
                            tm = tms[ki % 2]
                            btm = btms[ki % 2]
                            ki += 1
                            self.dma(kre[:], Kre_d[fc * 128:(fc + 1) * 128, c0:c0 + CG], [bkk], [bkt])
                            self.dma(kim[:], Kim_d[fc * 128:(fc + 1) * 128, c0:c0 + CG], [bkk], [bkt])
                            pa, bpa = self.next_psf()
                            for tc in range(TCN):
                                self.mm(pa[:, 0:CG], TCb[:, tc, fi * 128:(fi + 1) * 128], vg[:, tc, :], tc == 0, tc == TCN - 1,
                                        [btab, bvg], [bpa])
                            pbq, bpbq = self.next_psf()
                            for tc in range(TCN):
                                self.mm(pbq[:, 0:CG], TSb[:, tc, fi * 128:(fi + 1) * 128], vg[:, tc, :], tc == 0, tc == TCN - 1,
                                        [btab, bvg], [bpbq])
                            self.tt("dve", tm[0][:], pa[:, 0:CG], kre[:], ALU.mult, [bpa, bkt], [btm])
                            self.tt("dve", tm[1][:], pbq[:, 0:CG], kim[:], ALU.mult, [bpbq, bkt], [btm])
                            self.tt("dve", tm[2][:], pbq[:, 0:CG], kre[:], ALU.mult, [bpbq, bkt], [btm])
                            self.tt("dve", tm[3][:], pa[:, 0:CG], kim[:], ALU.mult, [bpa, bkt], [btm])
                            self.tt("pool", Yr[:, fc, :], tm[0][:], tm[1][:], ALU.add, [btm], [bY])
                            self.tt("pool", Yi[:, fc, :], tm[2][:], tm[3][:], ALU.subtract, [btm], [bY])
                P.barrier()
                with ExitStack() as stk2:
                    tabs = [(self.sb(stk2, [128, FC, TT], BF16, "ic"), self.sb(stk2, [128, FC, TT], BF16, "isn"),
                             Buf("itab%d" % i)) for i in range(2)]
                    yts = [(self.sb(stk2, [128, TT], F32, "yt"), Buf("yt%d" % i)) for i in range(2)]
                    ots = [(self.sb(stk2, [128, CGC, TT], BF16, "ot"), Buf("ot%d" % i)) for i in range(2)]
                    ovw = self.oTd[c0:c0 + CG, :].rearrange("(c p) t -> p c t", p=128)
                    ti = 0
                    yi = 0
                    for t0 in range(0, L, TT):
                        TCb, TSb, btab = tabs[ti % 2]
                        ot, bot = ots[ti % 2]
                        ti += 1
                        for jb in range(TT // 256):
                            self.dma(TCb[:, :, jb * 256:(jb + 1) * 256], cv[t0 // 256 + jb][:, 0:FC, :], [], [btab])
                            self.dma(TSb[:, :, jb * 256:(jb + 1) * 256], sv[t0 // 256 + jb][:, 0:FC, :], [], [btab])
                        for cc in range(CGC):
                            py, bpy = self.next_psf()
                            for fc in range(FC):
                                self.mm(py[:, 0:TT], Yr[:, fc, cc * 128:(cc + 1) * 128], TCb[:, fc, :], fc == 0, False,
                                        [bY, btab], [bpy])
                                self.mm(py[:, 0:TT], Yi[:, fc, cc * 128:(cc + 1) * 128], TSb[:, fc, :], False, fc == FC - 1,
                                        [bY, btab], [bpy])
                            yt, byt = yts[yi % 2]
                            yi += 1
                            chn = (c0 // 128) + cc
                            self.stt("dve", yt[:], vgT[:, cc, t0:t0 + TT], skp[:, chn:chn + 1], py[:, 0:TT], ALU.mult, ALU.add,
                                     [bvg, bk, bpy], [byt])
                            self.tt("pool", ot[:, cc, :], yt[:], x0T[:, cc, t0:t0 + TT], ALU.mult, [byt, bvg], [bot])
                        self.dma(ovw[:, :, t0:t0 + TT], ot[:], [bot], [self.dbuf["oTd"]])
                P.barrier()
        P.barrier()


def swa_bias_table():
    W = 128
    out = np.zeros((4, 3, 128, 512), np.float32)
    slopes = (2.0 ** (-8.0 * np.arange(1, 17, dtype=np.float32) / 16)).reshape(4, 4)
    k = np.arange(128)[:, None]
    q = np.arange(128)[None, :]
    for c in range(3):
        rel = (k + (c - 1) * W) - q
        dist = np.abs(rel).astype(np.float32)
        for hk in range(4):
            for g in range(4):
                b = -slopes[hk, g] * dist
                b = np.where(np.abs(rel) <= W, b, -30000.0)
                out[hk, c, :, g * 128:(g + 1) * 128] = b
    return out


def hy_rates():
    r = np.abs(np.linspace(math.log(1e-2) / 1.5, math.log(1e-2) / 0.3, 1024, dtype=np.float32))
    return r.reshape(1, 1024).astype(np.float32)


_HY_CACHE = {}


def hy_tables(L):
    if L in _HY_CACHE:
        return _HY_CACHE[L]
    TCN = L // 128
    FC = TCN + 1
    F = FC * 128
    N = 2 * L
    pos = np.arange(L, dtype=np.float32)[:, None]
    t = pos / np.float32(max(L - 1, 1))
    freqs = np.linspace(1e-4, 15, 16, dtype=np.float32)[None, :]
    ang = freqs * np.float32(2.0 * math.pi / L) * pos
    feats = np.concatenate([t, np.cos(ang), -np.sin(ang)], axis=-1).astype(np.float32)
    tn = (np.arange(L, dtype=np.float32) / np.float32(max(L - 1, 1))).reshape(TCN, 128).T
    a = np.arange(F, dtype=np.int64)
    prod = (a[:, None] * a[None, :]) % N
    base = 2.0 * np.pi * np.arange(N, dtype=np.float64) / N
    cosv = np.cos(base).astype(np.float32)
    sinv = np.sin(base).astype(np.float32)
    wf = np.full(F, 2.0 / N, np.float64)
    wf[0] = 1.0 / N
    wf[L] = 1.0 / N
    wf[L + 1:] = 0.0
    out = {"hy_tn%d" % L: np.ascontiguousarray(tn, np.float32), "hy_feats%d" % L: np.ascontiguousarray(feats.T),
           "hy_wN%d" % L: np.ascontiguousarray(wf.astype(np.float32).reshape(FC, 128).T),
           "hy_cos%d" % L: cosv[prod], "hy_sin%d" % L: sinv[prod]}
    _HY_CACHE[L] = out
    return out


def gdn_consts():
    j = np.arange(128)[:, None]
    i = np.arange(128)[None, :]
    c = np.zeros((9, 128, 128), np.float32)
    c[6] = (j // 32 == i // 32)
    c[7] = (j // 64 == i // 64) & (j // 32 != i // 32)
    c[8] = (j // 64 != i // 64)
    c[0] = (j <= i)
    c[1] = (j >= i)
    c[2] = np.where(i >= j, 0.0, -30000.0)
    c[3] = np.where(i <= j, 0.0, -30000.0)
    c[4] = (i > j)
    c[5] = (i < j)
    e = np.zeros((48, 32, 128), np.float32)
    for u in range(32):
        r = u if u < 16 else 32 + (u - 16)
        e[r, u, :] = 1.0
    return c, e


def rope_table(L):
    inv = 10000.0 ** (-np.arange(0, 32, 2, dtype=np.float32) / 32)
    ang = np.arange(L, dtype=np.float32)[None, :] * inv[:, None]
    cs = np.zeros((2, 96, L), np.float32)
    cs[0, 0:64] = 1.0
    cs[0, 64:80] = np.cos(ang)
    cs[0, 80:96] = np.cos(ang)
    cs[1, 64:80] = np.sin(ang)
    cs[1, 80:96] = np.sin(ang)
    return cs


def make_inputs(cfg, core_x, core_c, weights):
    m = {"xin": np.ascontiguousarray(core_x, np.float32), "cin": np.ascontiguousarray(core_c, np.float32),
         "c_ident": np.eye(128, dtype=np.float32)}
    for n in ("ada_w", "ada_b", "norm_w", "ffn_w_gu", "ffn_w_down"):
        m[n] = np.ascontiguousarray(weights[n][:cfg.depth], np.float32)
    m["final_norm_w"] = np.ascontiguousarray(weights["final_norm_w"].reshape(1, D), np.float32)
    kinds = set(cfg.kinds)
    if 2 in kinds:
        m["swa_w_qkv"] = np.ascontiguousarray(weights["swa_w_qkv"][0])
        m["swa_w_out"] = np.ascontiguousarray(weights["swa_w_out"][0])
        m["swa_sink"] = np.ascontiguousarray(weights["swa_sink"][0:1])
        m["swa_bias"] = swa_bias_table()
    if 0 in kinds:
        for n in ("hy_w_in", "hy_w_out", "hy_conv_w", "hy_filt_w1", "hy_filt_w2", "hy_filt_w3"):
            m[n] = np.ascontiguousarray(weights[n][0])
        for n in ("hy_conv_b", "hy_filt_b1", "hy_filt_freq1", "hy_filt_b2", "hy_filt_freq2", "hy_skip"):
            m[n] = np.ascontiguousarray(weights[n][0:1])
        m["hy_rates"] = hy_rates()
        for L in sorted(set(cfg.seqs)):
            for k_, v_ in hy_tables(L).items():
                m[k_] = v_
    if 1 in kinds:
        for n in ("gdn_w_in", "gdn_w_ab", "gdn_w_out", "gdn_conv_w"):
            m[n] = np.ascontiguousarray(weights[n][0])
        m["gdn_conv_b"] = np.ascontiguousarray(weights["gdn_conv_b"][0:1])
        m["gdn_a_log"] = np.ascontiguousarray(weights["gdn_a_log"][0].reshape(1, 32))
        m["gdn_dt_bias"] = np.ascontiguousarray(weights["gdn_dt_bias"][0].reshape(1, 32))
        m["gdn_norm_w"] = np.ascontiguousarray(weights["gdn_norm_w"][0:1])
        m["gdn_c"], m["gdn_esel"] = gdn_consts()
    if 3 in kinds:
        for n in ("mla_w_down", "mla_w_uq", "mla_w_ukv", "mla_w_out"):
            m[n] = np.ascontiguousarray(weights[n][0])
        m["mla_q_norm_w"] = np.ascontiguousarray(weights["mla_q_norm_w"][0:1])
        m["mla_kv_norm_w"] = np.ascontiguousarray(weights["mla_kv_norm_w"][0:1])
        m["rope_cs"] = rope_table(cfg.lmax)
    return m


_NC_CACHE = {}


def get_nc(cfg):
    key = (tuple(cfg.seqs), tuple(cfg.kinds))
    if key not in _NC_CACHE:
        _NC_CACHE[key] = Builder(cfg).build()
    return _NC_CACHE[key]


def kernel(**inputs):
    x_prompt = np.asarray(inputs["x_prompt"], np.float32)
    x_sample = np.asarray(inputs["x_sample"], np.float32)
    c_prompt = np.asarray(inputs["c_prompt"], np.float32)
    c_sample = np.asarray(inputs["c_sample"], np.float32)
    B, Lp, _ = x_prompt.shape
    Bs, Ls, _ = x_sample.shape
    NP = B // NCORES
    cfg = Cfg([Lp] * NP + [Ls], [0, 1, 2, 3])
    nc = get_nc(cfg)
    in_maps = []
    for c in range(NCORES):
        xs = [x_prompt[c * NP + i] for i in range(NP)] + [x_sample[c % Bs]]
        cs = [c_prompt[c * NP + i] for i in range(NP)] + [c_sample[c % Bs]]
        in_maps.append(make_inputs(cfg, np.concatenate(xs, 0), np.stack(cs, 0), inputs))
    res = run_bass_kernel_spmd(nc, in_maps, core_ids=list(range(NCORES)))
    y_prompt = np.zeros_like(x_prompt)
    y_sample = np.zeros_like(x_sample)
    for c in range(NCORES):
        y = res.results[c]["yout"]
        for i in range(NP):
            y_prompt[c * NP + i] = y[i * Lp:(i + 1) * Lp]
        if c < Bs:
            y_sample[c] = y[NP * Lp:NP * Lp + Ls]
    return (y_prompt, y_sample)
```

```python
import math
import os
from contextlib import ExitStack

import numpy as np
import ml_dtypes
import concourse.bass as bass
import concourse.mybir as mybir
from concourse.bass_utils import run_bass_kernel_spmd

F32 = mybir.dt.float32
BF16 = mybir.dt.bfloat16
AF = mybir.ActivationFunctionType
ALU = mybir.AluOpType
AX = mybir.AxisListType

D = 1024
DFF = 2816
EPS = 1e-6
NCORES = 8
NS_DMA = 12


class Buf:
    __slots__ = ("name", "w", "rc", "rd")

    def __init__(self, name=""):
        self.name = name
        self.w = None
        self.rc = {}
        self.rd = []


class Op:
    __slots__ = ("eng", "idx", "fn", "deps", "signal", "sig", "isdma", "k")


ENGS = ("pe", "act", "dve", "pool", "sp")


class Prog:
    def __init__(self):
        self.streams = {e: [] for e in ENGS}
        self.dmas = {"sp": [], "pool": []}

    def op(self, eng, fn, reads=(), writes=(), dma=False, extra=()):
        o = Op()
        o.eng = eng
        o.fn = fn
        o.isdma = dma
        o.signal = dma
        o.idx = len(self.streams[eng])
        o.sig = None
        ds = set(extra)
        for b in reads:
            if b.w is not None:
                ds.add(b.w)
        for b in writes:
            if b.w is not None:
                ds.add(b.w)
            for r in b.rc.values():
                ds.add(r)
            for r in b.rd:
                ds.add(r)
        keep = []
        for d in ds:
            if d is o:
                continue
            if d.eng == eng and not d.isdma and not dma:
                if eng == "pe" or (o.idx - d.idx) > 2:
                    continue
            d.signal = True
            keep.append(d)
        o.deps = keep
        for b in reads:
            if dma:
                b.rd.append(o)
            else:
                b.rc[eng] = o
        for b in writes:
            b.w = o
            b.rc = {}
            b.rd = []
        self.streams[eng].append(o)
        if dma:
            o.k = len(self.dmas[eng])
            self.dmas[eng].append(o)
        return o

    def barrier(self):
        last = []
        for e in ENGS:
            if self.streams[e]:
                last.append(self.streams[e][-1])
        for q in ("sp", "pool"):
            last.extend(self.dmas[q][-NS_DMA:])
        for e in ENGS:
            self.op(e, lambda g: g.nop(), extra=[x for x in last])

    def finish(self):
        self.barrier()

    def emit(self, nc, stk):
        sems = {e: stk.enter_context(nc.semaphore("s_" + e)) for e in ENGS}
        dsem = {q: [stk.enter_context(nc.semaphore("d_%s%d" % (q, i))) for i in range(NS_DMA)]
                for q in ("sp", "pool")}
        for e in ENGS:
            cnt = 0
            for o in self.streams[e]:
                if o.isdma:
                    o.sig = (dsem[e][o.k % NS_DMA], 16 * (o.k // NS_DMA + 1))
                elif o.signal:
                    cnt += 1
                    o.sig = (sems[e], cnt)
        block = stk.enter_context(nc.Block())

        def make(e):
            ops = self.streams[e]

            def body(g):
                waited = {}
                for o in ops:
                    for d in o.deps:
                        s, v = d.sig
                        if waited.get(id(s), 0) < v:
                            g.wait_ge(s, v)
                            waited[id(s)] = v
                    if o.isdma and o.k >= NS_DMA:
                        s = dsem[e][o.k % NS_DMA]
                        v = 16 * (o.k // NS_DMA)
                        if waited.get(id(s), 0) < v:
                            g.wait_ge(s, v)
                            waited[id(s)] = v
                    ins = o.fn(g)
                    if o.isdma:
                        ins.then_inc(o.sig[0], 16)
                    elif o.signal:
                        ins.then_inc(o.sig[0], 1)
            return body

        block.tensor(make("pe"))
        block.scalar(make("act"))
        block.vector(make("dve"))
        block.gpsimd(make("pool"))
        block.sync(make("sp"))


class Cfg:
    def __init__(self, seqs, kinds):
        self.seqs = list(seqs)
        self.kinds = list(kinds)
        self.depth = len(kinds)
        self.nseq = len(seqs)
        self.ntok = sum(seqs)
        self.lmax = max(seqs)
        self.offs = [sum(seqs[:i]) for i in range(len(seqs))]


WEIGHT_SHAPES = {
    "swa_w_qkv": (1024, 1536), "swa_w_out": (1024, 1024),
    "mla_w_down": (1024, 544), "mla_w_uq": (256, 1536), "mla_w_ukv": (256, 2048), "mla_w_out": (1024, 1024),
    "hy_w_in": (1024, 3072), "hy_w_out": (1024, 1024),
    "gdn_w_in": (1024, 6144), "gdn_w_ab": (1024, 64), "gdn_w_out": (2048, 1024),
    "ffn_w_gu": (1024, 5632), "ffn_w_down": (2816, 1024),
}


class Builder:
    def __init__(self, cfg):
        self.cfg = cfg
        self.nc = bass.Bass("TRN2", target_bir_lowering=False)
        self.P = Prog()
        self.stk = ExitStack()
        self.dram = {}
        self.dbuf = {}
        self.uid = 0

    def din(self, name, shape, dt=F32):
        t = self.nc.dram_tensor(name, list(shape), dt, kind="ExternalInput").ap()
        self.dram[name] = t
        self.dbuf[name] = Buf(name)
        return t

    def dscratch(self, name, shape, dt):
        t = self.nc.dram_tensor(name, list(shape), dt, kind="Internal").ap()
        self.dram[name] = t
        self.dbuf[name] = Buf(name)
        return t

    def sb(self, stk, shape, dt, name=None):
        self.uid += 1
        return stk.enter_context(self.nc.sbuf_tensor("%s_%d" % (name or "t", self.uid), list(shape), dt))

    def ps(self, stk, shape, dt, name=None):
        self.uid += 1
        return stk.enter_context(self.nc.psum_tensor("%s_%d" % (name or "p", self.uid), list(shape), dt))

    def dma(self, out, in_, reads, writes, q="sp", slow=False, **kw):
        if slow:
            kw["allow_slow_non_contiguous"] = True
        dset = set(id(b) for b in self.dbuf.values())
        reads = [b for b in reads if id(b) not in dset]
        writes = [b for b in writes if id(b) not in dset]
        return self.P.op(q, lambda g: g.dma_start(out=out, in_=in_, **kw), reads=reads, writes=writes, dma=True)


    def mm(self, out, lhsT, rhs, start, stop, reads, writes):
        return self.P.op("pe", lambda g: g.matmul(out, lhsT, rhs, start=start, stop=stop), reads=reads, writes=writes)

    def tr(self, out, in_, ident, reads, writes):
        return self.P.op("pe", lambda g: g.transpose(out=out, in_=in_, identity=ident), reads=reads, writes=writes)

    def actf(self, out, in_, func, reads, writes, bias=None, scale=None, accum_out=None):
        kw = {}
        if bias is not None:
            kw["bias"] = bias
        if scale is not None:
            kw["scale"] = scale
        if accum_out is not None:
            kw["accum_out"] = accum_out
        return self.P.op("act", lambda g: g.activation(out=out, in_=in_, func=func, **kw), reads=reads, writes=writes)

    def cp(self, eng, out, in_, reads, writes):
        if eng == "act":
            return self.P.op("act", lambda g: g.copy(out=out, in_=in_), reads=reads, writes=writes)
        return self.P.op(eng, lambda g: g.tensor_copy(out=out, in_=in_), reads=reads, writes=writes)

    def tt(self, eng, out, in0, in1, op, reads, writes):
        return self.P.op(eng, lambda g: g.tensor_tensor(out=out, in0=in0, in1=in1, op=op), reads=reads, writes=writes)

    def ts(self, eng, out, in0, s1, s2, op0, op1, reads, writes):
        if s2 is None:
            return self.P.op(eng, lambda g: g.tensor_scalar(out=out, in0=in0, scalar1=s1, scalar2=None, op0=op0),
                             reads=reads, writes=writes)
        return self.P.op(eng, lambda g: g.tensor_scalar(out=out, in0=in0, scalar1=s1, scalar2=s2, op0=op0, op1=op1),
                         reads=reads, writes=writes)

    def stt(self, eng, out, in0, scalar, in1, op0, op1, reads, writes):
        return self.P.op(eng, lambda g: g.scalar_tensor_tensor(out=out, in0=in0, scalar=scalar, in1=in1, op0=op0, op1=op1),
                         reads=reads, writes=writes)

    def mset(self, eng, ap, val, writes):
        return self.P.op(eng, lambda g: g.memset(ap, val), writes=writes)

    def recip(self, out, in_, reads, writes):
        return self.P.op("dve", lambda g: g.reciprocal(out=out, in_=in_), reads=reads, writes=writes)

    def amul(self, out, in_, mul, reads, writes):
        return self.P.op("act", lambda g: g.mul(out=out, in_=in_, mul=mul), reads=reads, writes=writes)

    def build(self):
        cfg = self.cfg
        nc = self.nc
        P = self.P
        with self.stk:
            self._build()
            P.finish()
            P.emit(nc, self.stk)
        return nc

    def _build(self):
        cfg = self.cfg
        nc = self.nc
        P = self.P
        NT, NSQ, DEP = cfg.ntok, cfg.nseq, cfg.depth
        self.xin = self.din("xin", [NT, D])
        self.cin = self.din("cin", [NSQ, D])
        self.ada_w = self.din("ada_w", [DEP, D, 6 * D])
        self.ada_b = self.din("ada_b", [DEP, 6 * D])
        self.norm_w = self.din("norm_w", [DEP, 2, D])
        self.final_norm_w = self.din("final_norm_w", [1, D])
        self.ffn_w_gu = self.din("ffn_w_gu", [DEP, D, 2 * DFF])
        self.ffn_w_down = self.din("ffn_w_down", [DEP, DFF, D])
        self.yout = self.nc.dram_tensor("yout", [NT, D], F32, kind="ExternalOutput").ap()
        self.dbuf["yout"] = Buf("yout")
        self.xs = self.dscratch("xs", [NT, D], F32)
        self.hTd = self.dscratch("hTd", [D, cfg.lmax + 2], BF16)
        if getattr(cfg, "debug", False):
            self.oTd = self.nc.dram_tensor("oTd", [2048, cfg.lmax], BF16, kind="ExternalOutput").ap()
            self.hTd_dbg = True
        else:
            self.oTd = self.dscratch("oTd", [2048, cfg.lmax], BF16)
        self.dbuf["oTd"] = Buf("oTd")
        self.modd = self.dscratch("modd", [DEP, NSQ, 6 * D], F32)
        self.wb_gu = self.dscratch("wb_gu", [DEP, 11, 128, 8 * 2 * 256], BF16)
        self.wb_down = self.dscratch("wb_down", [DEP, 2, 128, 22 * 512], BF16)
        kinds = set(cfg.kinds)
        self.w = {}
        self.wb = {}
        if 2 in kinds:
            for n in ("swa_w_qkv", "swa_w_out"):
                r, c = WEIGHT_SHAPES[n]
                self.w[n] = self.din(n, [r, c])
                self.wb[n] = self.dscratch("wb_" + n, [r, c], BF16)
            self.swa_sink = self.din("swa_sink", [1, 16])
            self.swa_bias = self.din("swa_bias", [4, 3, 128, 512])
        if 3 in kinds:
            for n in ("mla_w_down", "mla_w_uq", "mla_w_ukv", "mla_w_out"):
                r, c = WEIGHT_SHAPES[n]
                self.w[n] = self.din(n, [r, c])
                self.wb[n] = self.dscratch("wb_" + n, [r, c], BF16)
            self.mla_qnw = self.din("mla_q_norm_w", [1, 256])
            self.mla_kvnw = self.din("mla_kv_norm_w", [1, 256])
            self.rope_cs = self.din("rope_cs", [2, 96, cfg.lmax])
        self.hy_done = set()
        if 0 in kinds:
            for n in ("hy_w_in", "hy_w_out"):
                r, c = WEIGHT_SHAPES[n]
                self.w[n] = self.din(n, [r, c])
                self.wb[n] = self.dscratch("wb_" + n, [r, c], BF16)
            self.hy_conv_w = self.din("hy_conv_w", [3, 3072])
            self.hy_conv_b = self.din("hy_conv_b", [1, 3072])
            self.hy_fw1 = self.din("hy_filt_w1", [33, 64])
            self.hy_fb1 = self.din("hy_filt_b1", [1, 64])
            self.hy_ff1 = self.din("hy_filt_freq1", [1, 64])
            self.hy_fw2 = self.din("hy_filt_w2", [64, 64])
            self.hy_fb2 = self.din("hy_filt_b2", [1, 64])
            self.hy_ff2 = self.din("hy_filt_freq2", [1, 64])
            self.hy_fw3 = self.din("hy_filt_w3", [64, 2048])
            self.hy_skip = self.din("hy_skip", [1, 1024])
            self.hy_rates = self.din("hy_rates", [1, 1024])
            self.hyc, self.hy_hs, self.hy_hd, self.hy_kre, self.hy_kim, self.hy_tc, self.hy_ts = {}, {}, {}, {}, {}, {}, {}
            self.hy_tabsrc = {}
            for L in sorted(set(cfg.seqs)):
                TCN, FC, F = self.hy_dims(L)
                self.hyc[L] = {"tn": self.din("hy_tn%d" % L, [128, TCN]), "feats": self.din("hy_feats%d" % L, [33, L]),
                               "wN": self.din("hy_wN%d" % L, [128, FC])}
                self.hy_tabsrc[L] = (self.din("hy_cos%d" % L, [F, F]), self.din("hy_sin%d" % L, [F, F]))
                NCB = (F + 255) // 256
                self.hy_tc[L] = self.dscratch("hy_tcb%d" % L, [NCB, 128, FC * 256], BF16)
                self.hy_ts[L] = self.dscratch("hy_tsb%d" % L, [NCB, 128, FC * 256], BF16)
                self.dbuf["hy_tab%d" % L] = Buf("hy_tab")
                self.hy_hs[L] = self.dscratch("hy_hs%d" % L, [L, 1024], BF16)
                self.hy_hd[L] = self.dscratch("hy_hd%d" % L, [L, 1024], BF16)
                self.hy_kre[L] = self.dscratch("hy_kre%d" % L, [F, 1024], F32)
                self.hy_kim[L] = self.dscratch("hy_kim%d" % L, [F, 1024], F32)
                self.dbuf["hy_K%d" % L] = Buf("hy_K")
        if 1 in kinds:
            for n in ("gdn_w_in", "gdn_w_ab", "gdn_w_out"):
                r, c = WEIGHT_SHAPES[n]
                self.w[n] = self.din(n, [r, c])
                self.wb[n] = self.dscratch("wb_" + n, [r, c], BF16)
            self.gdn_conv_w = self.din("gdn_conv_w", [3, 4096])
            self.gdn_conv_b = self.din("gdn_conv_b", [1, 4096])
            self.gdn_a_log = self.din("gdn_a_log", [1, 32])
            self.gdn_dt_bias = self.din("gdn_dt_bias", [1, 32])
            self.gdn_norm_w = self.din("gdn_norm_w", [1, 128])
            self.gdn_c = self.din("gdn_c", [9, 128, 128])
            self.gdn_esel = self.din("gdn_esel", [48, 32, 128])
        self.xbuf = {}
        stk = self.stk
        self.ident = self.sb(stk, [128, 128], BF16, "ident")
        self.identf = self.sb(stk, [128, 128], F32, "identf")
        self.ones_bf = self.sb(stk, [128, 128], BF16, "ones")
        self.b_const = Buf("const")
        cident = self.din("c_ident", [128, 128])
        self.dma(self.identf[:], cident[:, :], [], [self.b_const])
        self.dma(self.ident[:], cident[:, :], [], [self.b_const], q="pool")
        P.op("dve", lambda g: g.memset(self.ones_bf[:], 1.0), writes=[self.b_const])
        self.zcol = self.sb(stk, [128, 8, 2], BF16, "zcol")
        P.op("dve", lambda g: g.memset(self.zcol[:], 0.0), writes=[self.b_const])
        self.psf = [(self.ps(stk, [128, 512], F32, "psf"), Buf("psf%d" % i)) for i in range(4)]
        self.psacc = [(self.ps(stk, [128, 512], F32, "psa"), Buf("psa%d" % i)) for i in range(2)]
        self.psb = [(self.ps(stk, [128, 1024], BF16, "psb"), Buf("psb%d" % i)) for i in range(2)]
        self.psf_i = 0
        self.psb_i = 0

        self.convert_weights()
        self.compute_mod()
        for l in range(DEP):
            kind = cfg.kinds[l]
            for s in range(NSQ):
                self.stage_a(l, s)
                if kind == 2:
                    self.swa(l, s)
                elif kind == 3:
                    self.mla(l, s)
                elif kind == 1:
                    self.gdn(l, s)
                elif kind == 0:
                    if (l, cfg.seqs[s]) not in self.hy_done:
                        self.hy_done.add((l, cfg.seqs[s]))
                        self.hyena_filter(cfg.seqs[s])
                    self.hyena(l, s)
                else:
                    raise NotImplementedError
                self.stage_c(l, s, kind)

    def next_psf(self):
        r = self.psf[self.psf_i % len(self.psf)]
        self.psf_i += 1
        return r

    def next_psb(self):
        r = self.psb[self.psb_i % len(self.psb)]
        self.psb_i += 1
        return r

    def convert_weights(self):
        P = self.P
        cfg = self.cfg
        P.barrier()
        with ExitStack() as stk:
            NB = 3
            tiles = [(self.sb(stk, [128, 8192], BF16, "cv"), Buf("cv%d" % i)) for i in range(NB)]
            cnt = [0]

            def conv2d(src, dst, rows, cols, bsrc, bdst):
                k = max(1, 8192 // cols)
                r0 = 0
                while r0 < rows:
                    kk = min(k, (rows - r0) // 128)
                    t, b = tiles[cnt[0] % NB]
                    cnt[0] += 1
                    tv = t[:, 0:kk * cols].rearrange("p (k c) -> p k c", k=kk)
                    sv = src[r0:r0 + 128 * kk, :].rearrange("(k p) c -> p k c", p=128)
                    dv = dst[r0:r0 + 128 * kk, :].rearrange("(k p) c -> p k c", p=128)
                    self.dma(tv, sv, [bsrc], [b], q="pool")
                    self.dma(dv, tv, [b], [bdst], q="sp")
                    r0 += 128 * kk

            for l in range(cfg.depth):
                gdst = self.wb_gu[l].rearrange("g p (k u c) -> p g k u c", k=8, u=2)
                for k in range(8):
                    t, b = tiles[cnt[0] % NB]
                    cnt[0] += 1
                    self.dma(t[:, 0:2 * DFF], self.ffn_w_gu[l][k * 128:(k + 1) * 128, :], [], [b], q="pool")
                    tv = t[:, 0:2 * DFF].rearrange("p (u g c) -> p u g c", u=2, g=11)
                    for u in range(2):
                        self.dma(gdst[:, :, k, u, :], tv[:, u, :, :], [b], [], q="sp")
                ddst = self.wb_down[l].rearrange("h p (f c) -> p h f c", f=22)
                f0 = 0
                while f0 < 22:
                    kk = min(8, 22 - f0)
                    t, b = tiles[cnt[0] % NB]
                    cnt[0] += 1
                    tv = t[:, 0:kk * D].rearrange("p (k c) -> p k c", k=kk)
                    self.dma(tv, self.ffn_w_down[l][f0 * 128:(f0 + kk) * 128, :].rearrange("(k p) c -> p k c", p=128),
                             [], [b], q="pool")
                    for h in range(2):
                        self.dma(ddst[:, h, f0:f0 + kk, :], tv[:, :, h * 512:(h + 1) * 512], [b], [], q="sp")
                    f0 += kk
            for n in self.w:
                r, c = WEIGHT_SHAPES[n]
                conv2d(self.w[n], self.wb[n], r, c, self.dbuf[n], self.dbuf["wb_" + n])
            if 0 in set(cfg.kinds):
                for L in sorted(set(cfg.seqs)):
                    TCN, FC, F = self.hy_dims(L)
                    bsrc = Buf("tabsrc")
                    NCBf = F // 256
                    for ti_ in range(2):
                        srct = self.hy_tabsrc[L][ti_]
                        dstt = (self.hy_tc[L], self.hy_ts[L])[ti_].rearrange("b p (r c) -> p b r c", r=FC)
                        for rc in range(FC):
                            t, b = tiles[cnt[0] % NB]
                            cnt[0] += 1
                            self.dma(t[:, 0:F], srct[rc * 128:(rc + 1) * 128, :], [], [b], q="pool")
                            self.dma(dstt[:, 0:NCBf, rc, :], t[:, 0:NCBf * 256].rearrange("p (b c) -> p b c", c=256), [b], [], q="sp")
                            if F % 256:
                                self.dma(dstt[:, NCBf, rc, 0:128], t[:, NCBf * 256:F], [b], [], q="sp")
            z = self.sb(stk, [128, 8, 2], BF16, "z")
            bz = Buf("z")
            P.op("dve", lambda g: g.memset(z[:], 0.0), writes=[bz])
            hv = self.hTd.rearrange("(c p) t -> p c t", p=128)
            self.dma(hv[:, :, 0:1], z[:, :, 0:1], [bz], [self.dbuf["hTd"]], slow=True)
            self.dma(hv[:, :, self.cfg.lmax + 1:self.cfg.lmax + 2], z[:, :, 1:2], [bz], [self.dbuf["hTd"]], slow=True)
        P.barrier()

    def compute_mod(self):
        P = self.P
        cfg = self.cfg
        nc = self.nc
        NSQ = cfg.nseq
        with ExitStack() as stk:
            cT = self.sb(stk, [128, 8, NSQ], F32, "cT")
            caT = self.sb(stk, [128, 8, NSQ], F32, "caT")
            ones1 = self.sb(stk, [1, 8], F32, "ones1")
            bc = Buf("cT")
            for s in range(NSQ):
                self.dma(cT[:, :, s:s + 1], self.cin[s:s + 1, :].rearrange("o (c p) -> p c o", p=128),
                         [], [bc], slow=True)
            P.op("act", lambda g: g.activation(out=caT[:], in_=cT[:], func=AF.Silu), reads=[bc], writes=[bc])
            P.op("dve", lambda g: g.memset(ones1[:], 1.0), writes=[bc])
            NW = 2
            wt = [(self.sb(stk, [128, 8, 512], F32, "aw"), Buf("aw%d" % i)) for i in range(NW)]
            bt = [(self.sb(stk, [1, 512], F32, "ab"), Buf("ab%d" % i)) for i in range(NW)]
            rt = [(self.sb(stk, [NSQ, 512], F32, "ar"), Buf("ar%d" % i)) for i in range(NW)]
            i = 0
            for l in range(cfg.depth):
                for n in range(12):
                    w, bw = wt[i % NW]
                    bb, bbb = bt[i % NW]
                    r, br = rt[i % NW]
                    i += 1
                    self.dma(w[:], self.ada_w[l][:, n * 512:(n + 1) * 512].rearrange("(k p) c -> p k c", p=128),
                             [], [bw])
                    self.dma(bb[:], self.ada_b[l:l + 1, n * 512:(n + 1) * 512], [], [bbb])
                    pt, bp = self.next_psf()
                    for k in range(8):
                        P.op("pe", (lambda k=k, w=w, pt=pt: lambda g: g.matmul(
                            pt[0:NSQ, :], caT[:, k, :], w[:, k, :], start=(k == 0), stop=False))(),
                            reads=[bc, bw], writes=[bp])
                    P.op("pe", (lambda bb=bb, pt=pt: lambda g: g.matmul(
                        pt[0:NSQ, :], ones1[0:1, 0:NSQ], bb[0:1, :], start=False, stop=True))(),
                        reads=[bc, bbb], writes=[bp])
                    P.op("act", (lambda r=r, pt=pt: lambda g: g.copy(out=r[:], in_=pt[0:NSQ, :]))(),
                         reads=[bp], writes=[br])
                    self.dma(self.modd[l][:, n * 512:(n + 1) * 512], r[:], [br], [self.dbuf["modd"]])
        P.barrier()

    def load_mod_vectors(self, stk, l, s, which):
        P = self.P
        nc = self.nc
        base = 3 * D * which
        sh = self.sb(stk, [128, 8], F32, "sh")
        sc = self.sb(stk, [128, 8], F32, "sc")
        nw = self.sb(stk, [128, 8], F32, "nw")
        wm = self.sb(stk, [128, 8], F32, "wm")
        gt = self.sb(stk, [128, D], F32, "g")
        b = Buf("modv")
        row = self.modd[l][s:s + 1, :]
        self.dma(sh[:], row[:, base:base + D].rearrange("o (c p) -> p (o c)", p=128), [self.dbuf["modd"]], [b],
                 slow=True)
        self.dma(sc[:], row[:, base + D:base + 2 * D].rearrange("o (c p) -> p (o c)", p=128),
                 [self.dbuf["modd"]], [b], slow=True)
        self.dma(nw[:], self.norm_w[l][which:which + 1, :].rearrange("o (c p) -> p (o c)", p=128), [], [b],
                 slow=True)
        self.dma(gt[:], row[:, base + 2 * D:base + 3 * D].partition_broadcast(128), [self.dbuf["modd"]], [b])
        P.op("dve", lambda g: g.scalar_tensor_tensor(out=wm[:], in0=sc[:], scalar=1.0, in1=nw[:],
                                                     op0=ALU.add, op1=ALU.mult), reads=[b], writes=[b])
        return wm, sh, gt, b

    def norm_to_T(self, stk_tmp, xt, bx, nsub, wm, sh, bmod, hT, bh, tmp):
        P = self.P
        ss, junk, rstd, xsb, bt = tmp
        for j in range(nsub):
            P.op("act", (lambda j=j: lambda g: g.activation(out=junk[:], in_=xt[:, j, :], func=AF.Square,
                                                            accum_out=ss[:, j:j + 1]))(),
                 reads=[bx], writes=[bt])
        P.op("dve", lambda g: g.tensor_scalar(out=rstd[:, 0:nsub], in0=ss[:, 0:nsub], scalar1=1.0 / D, scalar2=EPS,
                                              op0=ALU.mult, op1=ALU.add), reads=[bt], writes=[bt])
        P.op("act", lambda g: g.sqrt(out=rstd[:, 0:nsub], in_=rstd[:, 0:nsub]), reads=[bt], writes=[bt])
        P.op("dve", lambda g: g.reciprocal(out=rstd[:, 0:nsub], in_=rstd[:, 0:nsub]), reads=[bt], writes=[bt])
        for j in range(nsub):
            P.op("dve", (lambda j=j: lambda g: g.tensor_scalar(out=xsb[:, j, :], in0=xt[:, j, :],
                                                               scalar1=rstd[:, j:j + 1], scalar2=None,
                                                               op0=ALU.mult))(),
                 reads=[bx, bt], writes=[bt])
        for c in range(8):
            pt, bp = self.next_psb()
            for j in range(nsub):
                P.op("pe", (lambda j=j, c=c, pt=pt: lambda g: g.transpose(
                    out=pt[:, j * 128:(j + 1) * 128], in_=xsb[:, j, c * 128:(c + 1) * 128], identity=self.ident[:]))(),
                    reads=[bt, self.b_const], writes=[bp])
            P.op("act", (lambda c=c, pt=pt: lambda g: g.activation(
                out=hT[:, c, 0:nsub * 128], in_=pt[:, 0:nsub * 128], func=AF.Identity,
                bias=sh[:, c:c + 1], scale=wm[:, c:c + 1]))(),
                reads=[bp, bmod], writes=[bh])

    def norm_tmp(self, stk):
        ss = self.sb(stk, [128, 4], F32, "ss")
        junk = self.sb(stk, [128, D], BF16, "junk")
        rstd = self.sb(stk, [128, 4], F32, "rstd")
        xsb = self.sb(stk, [128, 4, D], BF16, "xsb")
        return (ss, junk, rstd, xsb, Buf("ntmp"))

    def xsrc(self, l):
        return (self.xin, "xin") if l == 0 else (self.xs, "xs")

    def xb(self, key):
        if key not in self.xbuf:
            self.xbuf[key] = Buf("x%s" % (key,))
        return self.xbuf[key]

    def stage_a(self, l, s):
        P = self.P
        cfg = self.cfg
        L = cfg.seqs[s]
        off = cfg.offs[s]
        src, _ = self.xsrc(l)
        with ExitStack() as stk:
            wm, sh, gt, bmod = self.load_mod_vectors(stk, l, s, 0)
            NB = 2
            xts = [(self.sb(stk, [128, 4, D], F32, "xt"), Buf("xt%d" % i)) for i in range(NB)]
            hts = [(self.sb(stk, [128, 8, 512], BF16, "hT"), Buf("hT%d" % i)) for i in range(NB)]
            tmps = [self.norm_tmp(stk) for i in range(NB)]
            hv = self.hTd.rearrange("(c p) t -> p c t", p=128)
            t0 = 0
            i = 0
            while t0 < L:
                n = min(512, L - t0)
                nsub = n // 128
                xt, bx = xts[i % NB]
                hT, bh = hts[i % NB]
                self.dma(xt[:, 0:nsub, :], src[off + t0:off + t0 + n, :].rearrange("(j p) d -> p j d", p=128),
                         [self.xb((s, t0))], [bx])
                self.norm_to_T(stk, xt, bx, nsub, wm, sh, bmod, hT, bh, tmps[i % NB])
                self.dma(hv[:, :, 1 + t0:1 + t0 + n], hT[:, :, 0:n], [bh], [self.dbuf["hTd"]])
                t0 += n
                i += 1
            self.dma(hv[:, :, L + 1:L + 2], self.zcol[:, :, 0:1], [self.b_const], [self.dbuf["hTd"]], slow=True)
        P.barrier()

    def stage_c(self, l, s, kind):
        P = self.P
        cfg = self.cfg
        L = cfg.seqs[s]
        off = cfg.offs[s]
        src, _ = self.xsrc(l)
        last = (l == cfg.depth - 1)
        wname = {0: "hy_w_out", 1: "gdn_w_out", 2: "swa_w_out", 3: "mla_w_out"}[kind]
        OC = WEIGHT_SHAPES[wname][0] // 128
        NSET = 2 if OC == 8 else 1
        with ExitStack() as stk:
            wm1, sh1, g1, bm1 = self.load_mod_vectors(stk, l, s, 0)
            wm2, sh2, g2, bm2 = self.load_mod_vectors(stk, l, s, 1)
            wout = self.sb(stk, [128, OC, D], BF16, "wout")
            bwo = Buf("wout")
            self.dma(wout[:], self.wb[wname].rearrange("(k p) c -> p k c", p=128), [], [bwo])
            if last:
                fnw = self.sb(stk, [128, D], F32, "fnw")
                bfn = Buf("fnw")
                self.dma(fnw[:], self.final_norm_w[0:1, :].partition_broadcast(128), [], [bfn])
            xt = self.sb(stk, [128, 4, D], F32, "xt")
            bx = Buf("xt")
            oT = self.sb(stk, [128, OC, 512], BF16, "oT")
            boT = Buf("oT")
            sets = [(self.sb(stk, [128, 4, D], F32, "x1"), Buf("x1_%d" % i),
                     self.sb(stk, [128, 8, 512], BF16, "h2T"), Buf("h2T_%d" % i)) for i in range(NSET)]
            hid = self.sb(stk, [128, 22, 512], BF16, "hid")
            bhid = Buf("hid")
            tmp = self.norm_tmp(stk)
            NWB = 2
            GF = 2
            wgs = [(self.sb(stk, [128, 8, 2, GF * 128], BF16, "wgu"), Buf("wgu%d" % i)) for i in range(NWB)]
            wds = [(self.sb(stk, [128, 22, 512], BF16, "wd"), Buf("wd%d" % i)) for i in range(NWB)]
            sgs = [(self.sb(stk, [128, 512], F32, "sg"), Buf("sg%d" % i)) for i in range(2)]
            tts = [(self.sb(stk, [128, 512], F32, "tt"), Buf("tt%d" % i)) for i in range(2)]
            ov = self.oTd.rearrange("(c p) t -> p c t", p=128)
            gu_v = self.wb_gu[l]
            wd_v = self.wb_down[l]
            cnt = {"wg": 0, "wd": 0, "sg": 0, "tt": 0}
            tiles = []
            t0 = 0
            while t0 < L:
                n = min(512, L - t0)
                tiles.append((t0, n, n // 128))
                t0 += n

            def front(ti):
                t0, n, nsub = tiles[ti]
                x1, bx1, h2T, bh2 = sets[ti % NSET]
                self.dma(oT[:, :, 0:n], ov[:, 0:OC, t0:t0 + n], [], [boT])
                self.dma(xt[:, 0:nsub, :], src[off + t0:off + t0 + n, :].rearrange("(j p) d -> p j d", p=128),
                         [self.xb((s, t0))], [bx])
                for j in range(nsub):
                    for h in range(2):
                        pt, bp = self.next_psf()
                        for c in range(OC):
                            self.mm(pt[:, :], oT[:, c, j * 128:(j + 1) * 128], wout[:, c, h * 512:(h + 1) * 512],
                                    c == 0, c == OC - 1, [boT, bwo], [bp])
                        tt, btt = tts[cnt["tt"] % 2]
                        cnt["tt"] += 1
                        self.tt("dve", tt[:], pt[:, :], g1[:, h * 512:(h + 1) * 512], ALU.mult, [bp, bm1], [btt])
                        self.tt("dve", x1[:, j, h * 512:(h + 1) * 512], tt[:], xt[:, j, h * 512:(h + 1) * 512], ALU.add,
                                [btt, bx], [bx1])
                self.norm_to_T(stk, x1, bx1, nsub, wm2, sh2, bm2, h2T, bh2, tmp)

            def ffn_back(ti):
                t0, n, nsub = tiles[ti]
                x1, bx1, h2T, bh2 = sets[ti % NSET]
                for f0 in range(0, 22, GF):
                    wg, bwg = wgs[cnt["wg"] % NWB]
                    cnt["wg"] += 1
                    self.dma(wg[:, :, :, :].rearrange("p k u c -> p (k u c)"), gu_v[f0 // GF], [], [bwg])
                    for fi in range(GF):
                        f = f0 + fi
                        pg, bpg = self.next_psf()
                        pu, bpu = self.next_psf()
                        for k in range(8):
                            self.mm(pg[:, 0:n], wg[:, k, 0, fi * 128:(fi + 1) * 128], h2T[:, k, 0:n], k == 0, k == 7,
                                    [bwg, bh2], [bpg])
                        for k in range(8):
                            self.mm(pu[:, 0:n], wg[:, k, 1, fi * 128:(fi + 1) * 128], h2T[:, k, 0:n], k == 0, k == 7,
                                    [bwg, bh2], [bpu])
                        sg, bsg = sgs[cnt["sg"] % 2]
                        cnt["sg"] += 1
                        self.actf(sg[:, 0:n], pg[:, 0:n], AF.Silu, [bpg], [bsg])
                        self.tt("dve", hid[:, f, 0:n], pu[:, 0:n], sg[:, 0:n], ALU.mult, [bpu, bsg], [bhid])
                for h in range(2):
                    wd, bwd = wds[cnt["wd"] % NWB]
                    cnt["wd"] += 1
                    self.dma(wd[:, :, :].rearrange("p f c -> p (f c)"), wd_v[h], [], [bwd])
                    for j in range(nsub):
                        pt, bp = self.next_psf()
                        for f in range(22):
                            self.mm(pt[:, :], hid[:, f, j * 128:(j + 1) * 128], wd[:, f, :], f == 0, f == 21,
                                    [bhid, bwd], [bp])
                        tt, btt = tts[cnt["tt"] % 2]
                        cnt["tt"] += 1
                        self.tt("dve", tt[:], pt[:, :], g2[:, h * 512:(h + 1) * 512], ALU.mult, [bp, bm2], [btt])
                        self.tt("dve", x1[:, j, h * 512:(h + 1) * 512], tt[:], x1[:, j, h * 512:(h + 1) * 512], ALU.add,
                                [btt, bx1], [bx1])
                if not last:
                    self.dma(self.xs[off + t0:off + t0 + n, :].rearrange("(j p) d -> p j d", p=128),
                             x1[:, 0:nsub, :], [bx1], [self.xb((s, t0))])
                else:
                    ss, junk, rstd, xsb, bt = tmp
                    for j in range(nsub):
                        self.actf(junk[:], x1[:, j, :], AF.Square, [bx1], [bt], accum_out=ss[:, j:j + 1])
                    self.ts("dve", rstd[:, 0:nsub], ss[:, 0:nsub], 1.0 / D, EPS, ALU.mult, ALU.add, [bt], [bt])
                    P.op("act", (lambda rstd=rstd, nsub=nsub: lambda g: g.sqrt(out=rstd[:, 0:nsub], in_=rstd[:, 0:nsub]))(),
                         reads=[bt], writes=[bt])
                    self.recip(rstd[:, 0:nsub], rstd[:, 0:nsub], [bt], [bt])
                    for j in range(nsub):
                        self.stt("dve", xsb_out[:, j, :], x1[:, j, :], rstd[:, j:j + 1], fnw[:], ALU.mult, ALU.mult,
                                 [bx1, bt, bfn], [bxo])
                    self.dma(self.yout[off + t0:off + t0 + n, :].rearrange("(j p) d -> p j d", p=128),
                             xsb_out[:, 0:nsub, :], [bxo], [])

            if last:
                xsb_out, bxo = xt, bx
            if NSET == 2:
                front(0)
                for ti in range(len(tiles)):
                    if ti + 1 < len(tiles):
                        front(ti + 1)
                    ffn_back(ti)
            else:
                for ti in range(len(tiles)):
                    front(ti)
                    ffn_back(ti)
        P.barrier()

    def swa(self, l, s):
        P = self.P
        cfg = self.cfg
        nc = self.nc
        L = cfg.seqs[s]
        NBK = L // 128
        with ExitStack() as stk:
            wq = self.sb(stk, [128, 8, 1536], BF16, "wqkv")
            bw = Buf("wqkv")
            self.dma(wq[:], self.wb["swa_w_qkv"].rearrange("(k p) c -> p k c", p=128),
                     [self.dbuf["wb_swa_w_qkv"]], [bw])
            sk = self.sb(stk, [128, 16], F32, "sink")
            bsk = Buf("sink")
            self.dma(sk[:], self.swa_sink[0:1, :].partition_broadcast(128), [], [bsk])
            P.op("act", lambda g: g.activation(out=sk[:], in_=sk[:], func=AF.Exp), reads=[bsk], writes=[bsk])
            ones64 = self.ones_bf
            QT = self.sb(stk, [64, 4, L], BF16, "QT")
            KT = self.sb(stk, [64, L], BF16, "KT")
            V = self.sb(stk, [128, NBK, 64], BF16, "V")
            bq = Buf("qkv")
            bias = self.sb(stk, [128, 3, 512], F32, "bias")
            bb = Buf("bias")
            hts = [(self.sb(stk, [128, 8, 512], BF16, "hT"), Buf("hT%d" % i)) for i in range(2)]
            tms = [(self.sb(stk, [128, 512], F32, "tm"), Buf("tm%d" % i)) for i in range(2)]
            pts = [(self.sb(stk, [128, 512], BF16, "pT"), Buf("pT%d" % i)) for i in range(3)]
            rds = [(self.sb(stk, [64, 512], F32, "rd"), Buf("rd%d" % i)) for i in range(2)]
            ots = [(self.sb(stk, [64, 512], BF16, "ot"), Buf("ot%d" % i)) for i in range(2)]
            hv = self.hTd.rearrange("(c p) t -> p c t", p=128)
            hi = 0
            tmi = 0
            pti = 0
            rdi = 0
            for hk in range(4):
                self.dma(bias[:], self.swa_bias[hk].rearrange("c p n -> p c n"), [], [bb])
                t0 = 0
                while t0 < L:
                    n = min(512, L - t0)
                    nsub = n // 128
                    hT, bh = hts[hi % 2]
                    hi += 1
                    self.dma(hT[:, :, 0:n], hv[:, :, 1 + t0:1 + t0 + n], [self.dbuf["hTd"]], [bh])
                    for gq in range(5):
                        pt, bp = self.next_psf()
                        col = (hk * 4 + gq) * 64 if gq < 4 else 1024 + hk * 64
                        for k in range(8):
                            P.op("pe", (lambda k=k, pt=pt, col=col, hT=hT: lambda g: g.matmul(
                                pt[0:64, 0:n], wq[:, k, col:col + 64], hT[:, k, 0:n], start=(k == 0), stop=(k == 7)))(),
                                reads=[bw, bh], writes=[bp])
                        if gq < 4:
                            P.op("act", (lambda gq=gq, pt=pt, t0=t0: lambda g: g.mul(
                                out=QT[:, gq, t0:t0 + n], in_=pt[0:64, 0:n], mul=0.125))(),
                                reads=[bp], writes=[bq])
                        else:
                            P.op("act", (lambda pt=pt, t0=t0: lambda g: g.copy(out=KT[:, t0:t0 + n], in_=pt[0:64, 0:n]))(),
                                 reads=[bp], writes=[bq])
                    pt, bp = self.next_psf()
                    vcol = 1280 + hk * 64
                    for j in range(nsub):
                        for k in range(8):
                            P.op("pe", (lambda j=j, k=k, pt=pt, hT=hT, vcol=vcol: lambda g: g.matmul(
                                pt[:, j * 64:(j + 1) * 64], hT[:, k, j * 128:(j + 1) * 128], wq[:, k, vcol:vcol + 64],
                                start=(k == 0), stop=(k == 7)))(),
                                reads=[bw, bh], writes=[bp])
                    P.op("dve", (lambda pt=pt, t0=t0, nsub=nsub: lambda g: g.tensor_copy(
                        out=V[:, t0 // 128:t0 // 128 + nsub, :],
                        in_=pt[:, 0:nsub * 64].rearrange("p (j d) -> p j d", d=64)))(),
                        reads=[bp], writes=[bq])
                    t0 += n
                for i in range(NBK):
                    po, bpo = self.psacc[0]
                    pd, bpd = self.psacc[1]
                    cs = [c for c in (i - 1, i, i + 1) if 0 <= c < NBK]
                    sts = []
                    for ci, c in enumerate(cs):
                        ps_, bps = self.next_psf()
                        for a in range(4):
                            self.mm(ps_[:, a * 128:(a + 1) * 128], KT[:, c * 128:(c + 1) * 128],
                                    QT[:, a, i * 128:(i + 1) * 128], True, True, [bq], [bps])
                        sts.append((ps_, bps))
                    for ci, c in enumerate(cs):
                        ps_, bps = sts[ci]
                        tm, btm = tms[tmi % 2]
                        tmi += 1
                        P.op("dve", (lambda c=c, i=i, ps_=ps_, tm=tm: lambda g: g.tensor_tensor(
                            out=tm[:], in0=ps_[:, :], in1=bias[:, c - i + 1, :], op=ALU.add))(),
                            reads=[bps, bb], writes=[btm])
                        pT, bpT = pts[pti % 3]
                        pti += 1
                        P.op("act", (lambda tm=tm, pT=pT: lambda g: g.activation(out=pT[:], in_=tm[:], func=AF.Exp))(),
                             reads=[btm], writes=[bpT])
                        P.op("pe", (lambda c=c, pT=pT, po=po, ci=ci, nn=len(cs): lambda g: g.matmul(
                            po[0:64, :], V[:, c, :], pT[:], start=(ci == 0), stop=(ci == nn - 1)))(),
                            reads=[bq, bpT], writes=[bpo])
                        P.op("pe", (lambda pT=pT, pd=pd, ci=ci, nn=len(cs): lambda g: g.matmul(
                            pd[0:64, :], ones64[:, 0:64], pT[:], start=(ci == 0), stop=(ci == nn - 1)))(),
                            reads=[self.b_const, bpT], writes=[bpd])
                    rd, brd = rds[rdi % 2]
                    ot, bot = ots[rdi % 2]
                    rdi += 1
                    for gq in range(4):
                        hq = hk * 4 + gq
                        P.op("dve", (lambda gq=gq, hq=hq, rd=rd, pd=pd: lambda g: g.tensor_scalar(
                            out=rd[:, gq * 128:(gq + 1) * 128], in0=pd[0:64, gq * 128:(gq + 1) * 128],
                            scalar1=sk[0:64, hq:hq + 1], scalar2=None, op0=ALU.add))(),
                            reads=[bpd, bsk], writes=[brd])
                    P.op("dve", (lambda rd=rd: lambda g: g.reciprocal(out=rd[:], in_=rd[:]))(), reads=[brd], writes=[brd])
                    P.op("dve", (lambda rd=rd, ot=ot, po=po: lambda g: g.tensor_tensor(
                        out=ot[:], in0=po[0:64, :], in1=rd[:], op=ALU.mult))(),
                        reads=[bpo, brd], writes=[bot])
                    dst = self.oTd[hk * 256:(hk + 1) * 256, i * 128:(i + 1) * 128].rearrange("(a d) q -> d a q", d=64)
                    self.dma(dst, ot[:].rearrange("d (a q) -> d a q", a=4), [bot], [self.dbuf["oTd"]])
        P.barrier()

    def mla(self, l, s):
        P = self.P
        cfg = self.cfg
        L = cfg.seqs[s]
        NBK = L // 128
        SCALE = 96.0 ** -0.5
        with ExitStack() as stk:
            wd = self.sb(stk, [128, 8, 544], BF16, "wd")
            wdx = self.sb(stk, [128, 8, 192], BF16, "wdx")
            wuq = self.sb(stk, [128, 2, 1536], BF16, "wuq")
            wuqs = self.sb(stk, [128, 2, 1536], BF16, "wuqs")
            wukv = self.sb(stk, [128, 2, 2048], BF16, "wukv")
            bw = Buf("mlaw")
            self.dma(wd[:], self.wb["mla_w_down"].rearrange("(k p) c -> p k c", p=128), [self.dbuf["wb_mla_w_down"]], [bw])
            self.dma(wuq[:], self.wb["mla_w_uq"].rearrange("(k p) c -> p k c", p=128), [self.dbuf["wb_mla_w_uq"]], [bw])
            self.dma(wukv[:], self.wb["mla_w_ukv"].rearrange("(k p) c -> p k c", p=128), [self.dbuf["wb_mla_w_ukv"]], [bw])
            P.op("dve", lambda g: g.memset(wuqs[:], 0.0), writes=[bw])
            P.op("dve", lambda g: g.memset(wdx[:], 0.0), writes=[bw])
            for k in range(2):
                wv = wuq[:, k, :].rearrange("p (h c) -> p h c", c=96)
                wsv = wuqs[:, k, :].rearrange("p (h c) -> p h c", c=96)
                P.op("act", (lambda wv=wv, wsv=wsv: lambda g: g.mul(out=wsv[:, :, 64:80], in_=wv[:, :, 80:96], mul=-1.0))(),
                     reads=[bw], writes=[bw])
                P.op("act", (lambda wv=wv, wsv=wsv: lambda g: g.copy(out=wsv[:, :, 80:96], in_=wv[:, :, 64:80]))(),
                     reads=[bw], writes=[bw])
            for k in range(8):
                P.op("act", (lambda k=k: lambda g: g.copy(out=wdx[:, k, 64:96], in_=wd[:, k, 512:544]))(),
                     reads=[bw], writes=[bw])
                P.op("act", (lambda k=k: lambda g: g.mul(out=wdx[:, k, 160:176], in_=wd[:, k, 528:544], mul=-1.0))(),
                     reads=[bw], writes=[bw])
                P.op("act", (lambda k=k: lambda g: g.copy(out=wdx[:, k, 176:192], in_=wd[:, k, 512:528]))(),
                     reads=[bw], writes=[bw])
            qnw = self.sb(stk, [128, 256], F32, "qnw")
            kvnw = self.sb(stk, [128, 256], F32, "kvnw")
            self.dma(qnw[:], self.mla_qnw[0:1, :].partition_broadcast(128), [], [bw])
            self.dma(kvnw[:], self.mla_kvnw[0:1, :].partition_broadcast(128), [], [bw])
            cosF = self.sb(stk, [96, L], F32, "cosF")
            sinF = self.sb(stk, [96, L], F32, "sinF")
            self.dma(cosF[:], self.rope_cs[0][:, 0:L], [], [bw])
            self.dma(sinF[:], self.rope_cs[1][:, 0:L], [], [bw])
            cqT = self.sb(stk, [128, 2, L], BF16, "cqT")
            ckvT = self.sb(stk, [128, 2, L], BF16, "ckvT")
            KR = self.sb(stk, [96, L], BF16, "KR")
            bc = Buf("lat")
            hts = [(self.sb(stk, [128, 8, 512], BF16, "hT"), Buf("hT%d" % i)) for i in range(2)]
            dts = [(self.sb(stk, [128, 512], F32, "dt"), Buf("dt%d" % i)) for i in range(2)]
            dns = [(self.sb(stk, [128, 512], BF16, "dn"), Buf("dn%d" % i)) for i in range(2)]
            sm = self.sb(stk, [128, 8], F32, "sm")
            junk = self.sb(stk, [128, 256], BF16, "junk")
            bsm = Buf("sm")
            t1s = [(self.sb(stk, [96, 512], F32, "t1"), Buf("t1%d" % i)) for i in range(2)]
            t2s = [(self.sb(stk, [96, 512], F32, "t2"), Buf("t2%d" % i)) for i in range(2)]
            hv = self.hTd.rearrange("(c p) t -> p c t", p=128)
            hi = 0
            di = 0
            ti = 0

            def rope_combine(p1, bp1, p2, bp2, n, t0, out_ap, bout):
                nonlocal ti
                t1, bt1 = t1s[ti % 2]
                t2, bt2 = t2s[ti % 2]
                ti += 1
                P.op("dve", lambda g: g.tensor_tensor(out=t1[:, 0:n], in0=p1[0:96, 0:n], in1=cosF[:, t0:t0 + n], op=ALU.mult),
                     reads=[bp1, bw], writes=[bt1])
                P.op("dve", lambda g: g.tensor_tensor(out=t2[:, 0:n], in0=p2[0:96, 0:n], in1=sinF[:, t0:t0 + n], op=ALU.mult),
                     reads=[bp2, bw], writes=[bt2])
                P.op("pool", lambda g: g.tensor_tensor(out=out_ap, in0=t1[:, 0:n], in1=t2[:, 0:n], op=ALU.add),
                     reads=[bt1, bt2], writes=[bout])

            t0 = 0
            while t0 < L:
                n = min(512, L - t0)
                nsub = n // 128
                hT, bh = hts[hi % 2]
                hi += 1
                self.dma(hT[:, :, 0:n], hv[:, :, 1 + t0:1 + t0 + n], [self.dbuf["hTd"]], [bh])
                for j in range(nsub):
                    pt, bp = self.next_psf()
                    for k in range(8):
                        P.op("pe", (lambda j=j, k=k, pt=pt, hT=hT: lambda g: g.matmul(
                            pt[:, :], hT[:, k, j * 128:(j + 1) * 128], wd[:, k, 0:512], start=(k == 0), stop=(k == 7)))(),
                            reads=[bh, bw], writes=[bp])
                    dt, bdt = dts[di % 2]
                    dn, bdn = dns[di % 2]
                    di += 1
                    for q2 in range(2):
                        P.op("act", (lambda q2=q2, pt=pt: lambda g: g.activation(
                            out=junk[:], in_=pt[:, q2 * 256:(q2 + 1) * 256], func=AF.Square, accum_out=sm[:, q2:q2 + 1]))(),
                            reads=[bp], writes=[bsm])
                    P.op("dve", lambda g: g.tensor_scalar(out=sm[:, 2:4], in0=sm[:, 0:2], scalar1=1.0 / 256, scalar2=EPS,
                                                          op0=ALU.mult, op1=ALU.add), reads=[bsm], writes=[bsm])
                    P.op("act", lambda g: g.sqrt(out=sm[:, 2:4], in_=sm[:, 2:4]), reads=[bsm], writes=[bsm])
                    P.op("dve", lambda g: g.reciprocal(out=sm[:, 4:6], in_=sm[:, 2:4]), reads=[bsm], writes=[bsm])
                    for q2, nwt in ((0, qnw), (1, kvnw)):
                        P.op("dve", (lambda q2=q2, nwt=nwt, pt=pt, dn=dn: lambda g: g.scalar_tensor_tensor(
                            out=dn[:, q2 * 256:(q2 + 1) * 256], in0=pt[:, q2 * 256:(q2 + 1) * 256],
                            scalar=sm[:, 4 + q2:5 + q2], in1=nwt[:], op0=ALU.mult, op1=ALU.mult))(),
                            reads=[bp, bsm, bw], writes=[bdn])
                    pb, bpb = self.next_psb()
                    for cc in range(4):
                        P.op("pe", (lambda cc=cc, pb=pb, dn=dn: lambda g: g.transpose(
                            out=pb[:, cc * 128:(cc + 1) * 128], in_=dn[:, cc * 128:(cc + 1) * 128], identity=self.ident[:]))(),
                            reads=[bdn, self.b_const], writes=[bpb])
                    tcol = t0 + j * 128
                    P.op("act", (lambda pb=pb, tcol=tcol: lambda g: g.copy(
                        out=cqT[:, :, tcol:tcol + 128], in_=pb[:, 0:256].rearrange("p (c t) -> p c t", c=2)))(),
                        reads=[bpb], writes=[bc])
                    P.op("act", (lambda pb=pb, tcol=tcol: lambda g: g.copy(
                        out=ckvT[:, :, tcol:tcol + 128], in_=pb[:, 256:512].rearrange("p (c t) -> p c t", c=2)))(),
                        reads=[bpb], writes=[bc])
                p1, bp1 = self.next_psf()
                p2, bp2 = self.next_psf()
                for k in range(8):
                    P.op("pe", (lambda k=k, p1=p1, hT=hT: lambda g: g.matmul(
                        p1[0:96, 0:n], wdx[:, k, 0:96], hT[:, k, 0:n], start=(k == 0), stop=(k == 7)))(),
                        reads=[bh, bw], writes=[bp1])
                for k in range(8):
                    P.op("pe", (lambda k=k, p2=p2, hT=hT: lambda g: g.matmul(
                        p2[0:96, 0:n], wdx[:, k, 96:192], hT[:, k, 0:n], start=(k == 0), stop=(k == 7)))(),
                        reads=[bh, bw], writes=[bp2])
                rope_combine(p1, bp1, p2, bp2, n, t0, KR[:, t0:t0 + n], bc)
                t0 += n
            QT = self.sb(stk, [96, L], BF16, "QT")
            KT = self.sb(stk, [96, L], BF16, "KT")
            V = self.sb(stk, [128, NBK, 64], BF16, "V")
            bq = Buf("qkv")
            pts = [(self.sb(stk, [128, 512], BF16, "pT"), Buf("pT%d" % i)) for i in range(3)]
            rds = [(self.sb(stk, [64, 512], F32, "rd"), Buf("rd%d" % i)) for i in range(2)]
            ots = [(self.sb(stk, [64, 512], BF16, "ot"), Buf("ot%d" % i)) for i in range(2)]
            pti = 0
            rdi = 0
            for h in range(16):
                t0 = 0
                while t0 < L:
                    n = min(512, L - t0)
                    nsub = n // 128
                    p1, bp1 = self.next_psf()
                    p2, bp2 = self.next_psf()
                    for k in range(2):
                        P.op("pe", (lambda k=k, p1=p1, h=h, t0=t0, n=n: lambda g: g.matmul(
                            p1[0:96, 0:n], wuq[:, k, h * 96:(h + 1) * 96], cqT[:, k, t0:t0 + n], start=(k == 0), stop=(k == 1)))(),
                            reads=[bc, bw], writes=[bp1])
                    for k in range(2):
                        P.op("pe", (lambda k=k, p2=p2, h=h, t0=t0, n=n: lambda g: g.matmul(
                            p2[0:96, 0:n], wuqs[:, k, h * 96:(h + 1) * 96], cqT[:, k, t0:t0 + n], start=(k == 0), stop=(k == 1)))(),
                            reads=[bc, bw], writes=[bp2])
                    rope_combine(p1, bp1, p2, bp2, n, t0, QT[:, t0:t0 + n], bq)
                    pk, bpk = self.next_psf()
                    for k in range(2):
                        P.op("pe", (lambda k=k, pk=pk, h=h, t0=t0, n=n: lambda g: g.matmul(
                            pk[0:64, 0:n], wukv[:, k, h * 128:h * 128 + 64], ckvT[:, k, t0:t0 + n], start=(k == 0), stop=(k == 1)))(),
                            reads=[bc, bw], writes=[bpk])
                    P.op("act", (lambda pk=pk, t0=t0, n=n: lambda g: g.copy(out=KT[0:64, t0:t0 + n], in_=pk[0:64, 0:n]))(),
                         reads=[bpk], writes=[bq])
                    P.op("act", (lambda t0=t0, n=n: lambda g: g.copy(out=KT[64:96, t0:t0 + n], in_=KR[64:96, t0:t0 + n]))(),
                         reads=[bc], writes=[bq])
                    pv, bpv = self.next_psf()
                    for j in range(nsub):
                        for k in range(2):
                            P.op("pe", (lambda j=j, k=k, pv=pv, h=h, t0=t0: lambda g: g.matmul(
                                pv[:, j * 64:(j + 1) * 64], ckvT[:, k, t0 + j * 128:t0 + (j + 1) * 128],
                                wukv[:, k, h * 128 + 64:h * 128 + 128], start=(k == 0), stop=(k == 1)))(),
                                reads=[bc, bw], writes=[bpv])
                    P.op("dve", (lambda pv=pv, t0=t0, nsub=nsub: lambda g: g.tensor_copy(
                        out=V[:, t0 // 128:t0 // 128 + nsub, :],
                        in_=pv[:, 0:nsub * 64].rearrange("p (j d) -> p j d", d=64)))(),
                        reads=[bpv], writes=[bq])
                    t0 += n
                t0 = 0
                while t0 < L:
                    n = min(512, L - t0)
                    po, bpo = self.psacc[0]
                    pd, bpd = self.psacc[1]
                    AH = 3
                    sts = {}

                    def emit_st(c, t0=t0, n=n):
                        ps_, bps = self.next_psf()
                        self.mm(ps_[:, 0:n], KT[:, c * 128:(c + 1) * 128], QT[:, t0:t0 + n], True, True, [bq], [bps])
                        sts[c] = (ps_, bps)
                    for c in range(min(AH, NBK)):
                        emit_st(c)
                    for c in range(NBK):
                        if c + AH < NBK:
                            emit_st(c + AH)
                        ps_, bps = sts.pop(c)
                        pT, bpT = pts[pti % 3]
                        pti += 1
                        P.op("act", (lambda ps_=ps_, pT=pT, n=n: lambda g: g.activation(
                            out=pT[:, 0:n], in_=ps_[:, 0:n], func=AF.Exp, scale=SCALE))(),
                            reads=[bps], writes=[bpT])
                        P.op("pe", (lambda c=c, pT=pT, po=po, n=n: lambda g: g.matmul(
                            po[0:64, 0:n], V[:, c, :], pT[:, 0:n], start=(c == 0), stop=(c == NBK - 1)))(),
                            reads=[bq, bpT], writes=[bpo])
                        P.op("pe", (lambda c=c, pT=pT, pd=pd, n=n: lambda g: g.matmul(
                            pd[0:64, 0:n], self.ones_bf[:, 0:64], pT[:, 0:n], start=(c == 0), stop=(c == NBK - 1)))(),
                            reads=[self.b_const, bpT], writes=[bpd])
                    rd, brd = rds[rdi % 2]
                    ot, bot = ots[rdi % 2]
                    rdi += 1
                    P.op("dve", (lambda rd=rd, pd=pd, n=n: lambda g: g.reciprocal(out=rd[:, 0:n], in_=pd[0:64, 0:n]))(),
                         reads=[bpd], writes=[brd])
                    P.op("dve", (lambda rd=rd, ot=ot, po=po, n=n: lambda g: g.tensor_tensor(
                        out=ot[:, 0:n], in0=po[0:64, 0:n], in1=rd[:, 0:n], op=ALU.mult))(),
                        reads=[bpo, brd], writes=[bot])
                    self.dma(self.oTd[h * 64:(h + 1) * 64, t0:t0 + n], ot[:, 0:n], [bot], [self.dbuf["oTd"]])
                    t0 += n
        P.barrier()


    def gdn(self, l, s):
        P = self.P
        cfg = self.cfg
        L = cfg.seqs[s]
        NCH = L // 128
        NG = NCH * 16
        hv = self.hTd.rearrange("(c p) t -> p c t", p=128)
        win = self.wb["gdn_w_in"].rearrange("(k p) c -> p k c", p=128)
        with ExitStack() as stk:
            bk = Buf("gconst")
            cst = self.sb(stk, [128, 9, 128], F32, "gcst")
            self.dma(cst[:], self.gdn_c.rearrange("a p n -> p a n"), [], [bk])
            U = [cst[:, 0, :], cst[:, 1, :]]
            MB = [cst[:, 2, :], cst[:, 3, :]]
            S01 = [cst[:, 4, :], cst[:, 5, :]]
            onesf = self.sb(stk, [128, 128], F32, "onesf")
            self.mset("dve", onesf[:], 1.0, [bk])
            I4 = self.sb(stk, [128, 4, 128], F32, "I4")
            D32_4 = self.sb(stk, [128, 4, 128], F32, "D32_4")
            M64_4 = self.sb(stk, [128, 4, 128], F32, "M64_4")
            M128_4 = self.sb(stk, [128, 4, 128], F32, "M128_4")
            M4 = self.sb(stk, [128, 4, 128], F32, "M4")
            self.mset("dve", M4[:], 1.0, [bk])
            self.cp("act", M4[:, 0, :], cst[:, 4, :], [bk], [bk])
            self.cp("act", M4[:, 2, :], cst[:, 5, :], [bk], [bk])
            for u in range(4):
                self.cp("act", I4[:, u, :], self.identf[:], [self.b_const], [bk])
                self.cp("act", D32_4[:, u, :], cst[:, 6, :], [bk], [bk])
                self.cp("act", M64_4[:, u, :], cst[:, 7, :], [bk], [bk])
                self.cp("act", M128_4[:, u, :], cst[:, 8, :], [bk], [bk])
            dtb4 = self.sb(stk, [128, 4, 32], F32, "dtb4")
            nea4 = self.sb(stk, [128, 4, 32], F32, "nea4")
            for j in range(4):
                self.dma(dtb4[:, j, :], self.gdn_dt_bias[0:1, :].partition_broadcast(128), [], [bk])
                self.dma(nea4[:, j, :], self.gdn_a_log[0:1, :].partition_broadcast(128), [], [bk])
            self.actf(nea4[:], nea4[:], AF.Exp, [bk], [bk])
            self.amul(nea4[:], nea4[:], -1.0, [bk], [bk])
            wab = self.sb(stk, [128, 8, 64], BF16, "wab")
            self.dma(wab[:], self.wb["gdn_w_ab"].rearrange("(k p) c -> p k c", p=128), [self.dbuf["wb_gdn_w_ab"]], [bk])
            nwv = self.sb(stk, [128, 2, 128], F32, "nwv")
            for j in range(2):
                self.dma(nwv[:, j, :], self.gdn_norm_w[0:1, :].partition_broadcast(128), [], [bk])
            Be = self.sb(stk, [128, 2, NCH, 16], F32, "Be")
            nBe = self.sb(stk, [128, 2, NCH, 16], F32, "nBe")
            ngc = self.sb(stk, [128, 2, NCH, 16], F32, "ngc")
            egc = self.sb(stk, [128, 2, NCH, 16], F32, "egc")
            ekd = self.sb(stk, [128, 2, NCH, 16], F32, "ekd")
            ege = self.sb(stk, [128, 2, NCH, 16], F32, "ege")
            gcT = self.sb(stk, [48, L], F32, "gcT")
            bg = Buf("gates")

            def fl(t, d):
                return t[:, d, :, :].rearrange("p n h -> p (n h)")

            with ExitStack() as stk2:
                GF48 = self.sb(stk2, [128, NCH, 48], F32, "GF48")
                GB48 = self.sb(stk2, [128, NCH, 48], F32, "GB48")
                Gt = self.sb(stk2, [128, 2, NCH, 16], F32, "Gt")
                gc = self.sb(stk2, [128, 2, NCH, 16], F32, "gc")
                self.mset("dve", GF48[:], 0.0, [bg])
                self.mset("dve", GB48[:], 0.0, [bg])
                hts = [(self.sb(stk2, [128, 8, 512], BF16, "hT"), Buf("hT%d" % i)) for i in range(2)]
                tmp = self.sb(stk2, [128, 4, 32], F32, "gtmp")
                btmp = Buf("gtmp")
                t0 = 0
                i = 0
                while t0 < L:
                    n = min(512, L - t0)
                    nsub = n // 128
                    n0 = t0 // 128
                    hT, bh = hts[i % 2]
                    i += 1
                    self.dma(hT[:, :, 0:n], hv[:, :, 1 + t0:1 + t0 + n], [self.dbuf["hTd"]], [bh])
                    pt, bp = self.next_psf()
                    for j in range(nsub):
                        for k in range(8):
                            self.mm(pt[:, j * 64:(j + 1) * 64], hT[:, k, j * 128:(j + 1) * 128], wab[:, k, :],
                                    k == 0, k == 7, [bh, bk], [bp])
                    pv = pt[:, 0:nsub * 64].rearrange("p (j c) -> p j c", c=64)
                    self.tt("dve", tmp[:, 0:nsub, :], pv[:, :, 0:32], dtb4[:, 0:nsub, :], ALU.add, [bp, bk], [btmp])
                    self.actf(tmp[:, 0:nsub, :], tmp[:, 0:nsub, :], AF.Exp, [btmp], [btmp])
                    self.ts("dve", tmp[:, 0:nsub, :], tmp[:, 0:nsub, :], 1.0, None, ALU.add, None, [btmp], [btmp])
                    self.actf(tmp[:, 0:nsub, :], tmp[:, 0:nsub, :], AF.Ln, [btmp], [btmp])
                    for d in range(2):
                        self.tt("dve", Gt[:, d, n0:n0 + nsub, :], tmp[:, 0:nsub, d * 16:(d + 1) * 16],
                                nea4[:, 0:nsub, d * 16:(d + 1) * 16], ALU.mult, [btmp, bk], [bg])
                        self.actf(Be[:, d, n0:n0 + nsub, :], pv[:, :, 32 + d * 16:32 + (d + 1) * 16], AF.Sigmoid,
                                  [bp], [bg])
                    t0 += n
                self.cp("dve", GF48[:, :, 0:16], Gt[:, 0, :, :], [bg], [bg])
                self.cp("dve", GB48[:, :, 32:48], Gt[:, 1, :, :], [bg], [bg])
                for d in range(2):
                    pg, bpg = self.next_psf()
                    self.mm(pg[:, 0:NG], U[d], fl(Gt, d), True, True, [bg, bk], [bpg])
                    self.cp("act", fl(gc, d), pg[:, 0:NG], [bpg], [bg])
                    self.actf(fl(egc, d), pg[:, 0:NG], AF.Exp, [bpg], [bg])
                    pgt, bpgt = self.next_psf()
                    self.mm(pgt[:, 0:NG], onesf[:], fl(Gt, d), True, True, [bg, bk], [bpgt])
                    self.actf(fl(ege, d), pgt[:, 0:NG], AF.Exp, [bpgt], [bg])
                    self.tt("dve", fl(ekd, d), pgt[:, 0:NG], fl(gc, d), ALU.subtract, [bpgt, bg], [bg])
                    self.actf(fl(ekd, d), fl(ekd, d), AF.Exp, [bg], [bg])
                    self.amul(fl(ngc, d), fl(gc, d), -1.0, [bg], [bg])
                    self.amul(fl(nBe, d), fl(Be, d), -1.0, [bg], [bg])
                for n4 in range(0, NCH, 4):
                    pT, bpT = self.next_psf()
                    for n in range(n4, min(n4 + 4, NCH)):
                        cs = (n - n4) * 128
                        self.mm(pT[0:48, cs:cs + 128], GF48[:, n, :], U[0], True, False, [bg, bk], [bpT])
                        self.mm(pT[0:48, cs:cs + 128], GB48[:, n, :], U[1], False, True, [bg, bk], [bpT])
                    w = (min(n4 + 4, NCH) - n4) * 128
                    self.cp("act", gcT[:, n4 * 128:n4 * 128 + w], pT[0:48, 0:w], [bpT], [bg])
            P.barrier()

            DKS = 128.0 ** -0.5
            STOP = int(os.environ.get("GDN_STOP", "99"))
            lanes = 2 if L <= 2048 else 1
            LBs = []
            for li in range(lanes):
                LBs.append((self.sb(stk, [128, L], BF16, "qT"), self.sb(stk, [128, L], BF16, "kT"),
                            self.sb(stk, [128, NCH, 128], BF16, "ktok"), self.sb(stk, [128, NCH, 256], BF16, "vtok"),
                            self.sb(stk, [128, NCH, 2, 128], F32, "O"), Buf("qk%d" % li), Buf("O%d" % li)))

            def project(hk, LB):
                qT, kT, k_tok, v_tok, O, bqk, bO = LB
                with ExitStack() as stk2:
                    Wraw = self.sb(stk2, [128, 8, 256], BF16, "Wraw")
                    cwb = self.sb(stk2, [128, 3, 256], F32, "cwb")
                    Wt = self.sb(stk2, [128, 8, 3, 256], BF16, "Wt")
                    cbp = self.sb(stk2, [128, 2], F32, "cbp")
                    cbv = self.sb(stk2, [128, 256], F32, "cbv")
                    bW = Buf("W")
                    hts = [(self.sb(stk2, [128, 8, 514], BF16, "hTh"), Buf("hTh%d" % i)) for i in range(2)]
                    tq = self.sb(stk2, [128, 512], F32, "tq")
                    sq = self.sb(stk2, [128, 512], BF16, "sq")
                    rn = self.sb(stk2, [128, 512], F32, "rn")
                    tv = self.sb(stk2, [128, 256], F32, "tv")
                    btq = Buf("tq")
                    hi = 0
                    parts = [("q", hk * 128, 128), ("k", 1024 + hk * 128, 128), ("v", 2048 + 2 * hk * 128, 256)]
                    for pi, (pn, col0, wd_) in enumerate(parts):
                        self.dma(Wraw[:, :, 0:wd_], win[:, :, col0:col0 + wd_], [self.dbuf["wb_gdn_w_in"]], [bW])
                        for tap in range(3):
                            self.dma(cwb[:, tap, 0:wd_], self.gdn_conv_w[tap:tap + 1, col0:col0 + wd_].partition_broadcast(128),
                                     [], [bW])
                        if pi < 2:
                            self.dma(cbp[:, pi:pi + 1], self.gdn_conv_b[0:1, col0:col0 + 128].rearrange("o p -> p o"),
                                     [], [bW], slow=True)
                        else:
                            self.dma(cbv[:], self.gdn_conv_b[0:1, col0:col0 + 256].partition_broadcast(128), [], [bW])
                        for k in range(8):
                            for tap in range(3):
                                self.tt("dve", Wt[:, k, tap, 0:wd_], Wraw[:, k, 0:wd_], cwb[:, tap, 0:wd_], ALU.mult,
                                        [bW], [bW])
                        t0 = 0
                        while t0 < L:
                            n = min(512, L - t0)
                            nsub = n // 128
                            n0 = t0 // 128
                            hT, bh = hts[hi % 2]
                            hi += 1
                            self.dma(hT[:, :, 0:n + 2], hv[:, :, t0:t0 + n + 2], [self.dbuf["hTd"]], [bh])
                            if pi < 2:
                                pt, bp = self.next_psf()
                                idx = 0
                                for tap in range(3):
                                    for k in range(8):
                                        self.mm(pt[:, 0:n], Wt[:, k, tap, 0:128], hT[:, k, tap:tap + n],
                                                idx == 0, idx == 23, [bW, bh], [bp])
                                        idx += 1
                                self.actf(tq[:, 0:n], pt[:, 0:n], AF.Silu, [bp, bW], [btq], bias=cbp[:, pi:pi + 1])
                                self.tt("pool", sq[:, 0:n], tq[:, 0:n], tq[:, 0:n], ALU.mult, [btq], [btq])
                                pss, bpss = self.next_psf()
                                self.mm(pss[:, 0:n], self.ones_bf[:], sq[:, 0:n], True, True, [btq, self.b_const], [bpss])
                                self.ts("dve", rn[:, 0:n], pss[:, 0:n], EPS, None, ALU.add, None, [bpss], [btq])
                                P.op("act", (lambda rn=rn, n=n: lambda g: g.sqrt(out=rn[:, 0:n], in_=rn[:, 0:n]))(),
                                     reads=[btq], writes=[btq])
                                self.recip(rn[:, 0:n], rn[:, 0:n], [btq], [btq])
                                if pi == 0:
                                    self.stt("dve", qT[:, t0:t0 + n], tq[:, 0:n], DKS, rn[:, 0:n], ALU.mult, ALU.mult,
                                             [btq], [bqk])
                                else:
                                    self.tt("dve", kT[:, t0:t0 + n], tq[:, 0:n], rn[:, 0:n], ALU.mult, [btq], [bqk])
                                    pb, bpb = self.next_psb()
                                    for j in range(nsub):
                                        self.tr(pb[:, j * 128:(j + 1) * 128], kT[:, t0 + j * 128:t0 + (j + 1) * 128],
                                                self.ident[:], [bqk, self.b_const], [bpb])
                                    self.cp("act", k_tok[:, n0:n0 + nsub, :],
                                            pb[:, 0:nsub * 128].rearrange("p (j d) -> p j d", d=128), [bpb], [bqk])
                            else:
                                for j in range(nsub):
                                    pt, bp = self.next_psf()
                                    idx = 0
                                    for tap in range(3):
                                        for k in range(8):
                                            self.mm(pt[:, 0:256], hT[:, k, tap + j * 128:tap + (j + 1) * 128],
                                                    Wt[:, k, tap, 0:256], idx == 0, idx == 23, [bW, bh], [bp])
                                            idx += 1
                                    self.tt("dve", tv[:], pt[:, 0:256], cbv[:], ALU.add, [bp, bW], [btq])
                                    self.actf(v_tok[:, n0 + j, :], tv[:], AF.Silu, [btq], [bqk])
                            t0 += n
                P.barrier()

            def steps(hk, LB, stk3):
                qT, kT, k_tok, v_tok, O, bqk, bO = LB
                if True:
                    def t4(dt, nm):
                        return self.sb(stk3, [128, 4, 128], dt, nm)
                    kkq = t4(F32, "kkq"); bkkq = Buf("kkq")
                    dTi = t4(F32, "dTi"); bdTi = Buf("dTi")
                    dTs = t4(F32, "dTs"); bdTs = Buf("dTs")
                    intraT = t4(BF16, "intraT"); bintra = Buf("intraT")
                    XA = t4(F32, "XA"); XB = t4(F32, "XB")
                    YA = t4(F32, "YA"); YB = t4(F32, "YB")
                    PA = t4(F32, "PA"); PB = t4(F32, "PB")
                    E1b = t4(BF16, "E1b"); E2b = t4(BF16, "E2b")
                    Pb0 = t4(BF16, "Pb0"); Pb1 = t4(BF16, "Pb1"); Qb = t4(BF16, "Qb"); Gtb = t4(BF16, "Gtb")
                    bXA, bXB, bYA, bYB, bPA, bPB = [Buf(x) for x in ("XA", "XB", "YA", "YB", "PA", "PB")]
                    bE1, bE2, bPb0, bPb1, bQb, bGtb = [Buf(x) for x in ("E1", "E2", "Pb0", "Pb1", "Qb", "Gtb")]
                    Ttb = t4(BF16, "Ttb"); bTtb = Buf("Ttb")
                    rw = t4(BF16, "rw"); brw = Buf("rw")
                    kd = t4(BF16, "kd"); bkd = Buf("kd")
                    uf = t4(F32, "uf"); buf_ = Buf("uf")
                    wbt = t4(BF16, "wbt"); bwbt = Buf("wbt")
                    wT = t4(BF16, "wT"); bwT = Buf("wT")
                    vnew = t4(BF16, "vnew"); bvnew = Buf("vnew")
                    tB = t4(F32, "tB"); btB = Buf("tB")
                    tC = t4(F32, "tC"); btC = Buf("tC")
                    Sf = t4(F32, "Sf"); bSf = Buf("Sf")
                    Sbf = t4(BF16, "Sbf"); bSbf = Buf("Sbf")
                    esel = self.sb(stk3, [48, 4, 128], F32, "esel")
                    besel = Buf("esel")
                    for d in range(2):
                        self.dma(esel[:, 2 * d:2 * d + 2, :], self.gdn_esel[:, d * 16 + 2 * hk:d * 16 + 2 * hk + 2, :], [], [besel])
                    self.mset("dve", Sf[:], 0.0, [bSf])
                    self.mset("dve", Sbf[:], 0.0, [bSbf])
                    self.mset("pool", O[:], 0.0, [bO])

                    def f4(t):
                        return t[:, :, :].rearrange("p u d -> p (u d)")

                    yield
                    for st in range(NCH):
                        if STOP < 3:
                            break
                        chs = [st, st, NCH - 1 - st, NCH - 1 - st]
                        dirs = [0, 0, 1, 1]
                        vls = [0, 1, 0, 1]
                        hds = [2 * hk + v for v in vls]

                        def sc(t, u):
                            return t[:, dirs[u], chs[u], hds[u]:hds[u] + 1]
                        pk, bpk = self.next_psf()
                        for ci, n in enumerate((chs[0], chs[2])):
                            ksl = kT[:, n * 128:(n + 1) * 128]
                            self.mm(pk[:, (2 * ci) * 128:(2 * ci + 1) * 128], ksl, ksl, True, True, [bqk], [bpk])
                            self.mm(pk[:, (2 * ci + 1) * 128:(2 * ci + 2) * 128], ksl, qT[:, n * 128:(n + 1) * 128],
                                    True, True, [bqk], [bpk])
                        self.tt("dve", f4(kkq), pk[:, :], f4(M4), ALU.mult, [bpk, bk], [bkkq])
                        yield
                        pdf, bpdf = self.next_psf()
                        for u in range(4):
                            m = dirs[u] * 16 + hds[u]
                            cs = chs[u] * 128
                            self.mm(pdf[:, u * 128:(u + 1) * 128], esel[:, u, :], gcT[:, cs:cs + 128], True, False,
                                    [besel, bg], [bpdf])
                            self.mm(pdf[:, u * 128:(u + 1) * 128], self.identf[:], MB[dirs[u]], False, True,
                                    [bk, self.b_const], [bpdf])
                        for u in range(4):
                            self.actf(dTi[:, u, :], pdf[:, u * 128:(u + 1) * 128], AF.Exp, [bpdf, bg], [bdTi],
                                      bias=sc(ngc, u))
                        yield
                        if STOP < 4:
                            continue
                        Y0, bY0, X0, bX0 = YB, bYB, XB, bXB
                        for u in range(4):
                            self.stt("dve", Y0[:, u, :], kkq[:, 2 * (u // 2), :], sc(nBe, u), dTi[:, u, :],
                                     ALU.mult, ALU.mult, [bkkq, bg, bdTi], [bY0])
                            self.tt("pool", intraT[:, u, :], kkq[:, 2 * (u // 2) + 1, :], dTi[:, u, :], ALU.mult,
                                    [bkkq, bdTi], [bintra])
                        if STOP < 5:
                            continue
                        px, bpx = self.next_psf()
                        for u in range(4):
                            self.tr(px[:, u * 128:(u + 1) * 128], Y0[:, u, :], self.identf[:], [bY0, self.b_const], [bpx])
                        self.cp("act", f4(X0), px[:, :], [bpx], [bX0])
                        yield
                        self.tt("dve", f4(YA), f4(Y0), f4(D32_4), ALU.mult, [bY0, bk], [bYA])
                        self.tt("dve", f4(XA), f4(X0), f4(D32_4), ALU.mult, [bX0, bk], [bXA])
                        self.tt("pool", f4(E1b), f4(X0), f4(M64_4), ALU.mult, [bX0, bk], [bE1])
                        self.tt("pool", f4(E2b), f4(X0), f4(M128_4), ALU.mult, [bX0, bk], [bE2])
                        Y, bY, X, bX = YA, bYA, XA, bXA
                        Pm, bPm = PA, bPA
                        self.tt("dve", f4(Pm), f4(Y), f4(I4), ALU.add, [bY, bk], [bPm])
                        yield
                        for lev in range(1, 5):
                            Xn, bXn = (XB, bXB) if X is XA else (XA, bXA)
                            Yn, bYn = (YB, bYB) if Y is YA else (YA, bYA)
                            Pn, bPn = (PB, bPB) if Pm is PA else (PA, bPA)
                            pxn, bpxn = self.next_psf()
                            for u in range(4):
                                self.mm(pxn[:, u * 128:(u + 1) * 128], Y[:, u, :], X[:, u, :], True, True, [bY, bX], [bpxn])
                            if lev < 4:
                                pyn, bpyn = self.next_psf()
                                for u in range(4):
                                    self.mm(pyn[:, u * 128:(u + 1) * 128], X[:, u, :], Y[:, u, :], True, True, [bY, bX], [bpyn])
                            self.cp("act", f4(Xn), pxn[:, :], [bpxn], [bXn])
                            if lev < 4:
                                self.cp("act", f4(Yn), pyn[:, :], [bpyn], [bYn])
                            yield
                            ppn, bppn = self.next_psf()
                            for u in range(4):
                                self.mm(ppn[:, u * 128:(u + 1) * 128], Xn[:, u, :], Pm[:, u, :], True, True,
                                        [bXn, bPm], [bppn])
                            if lev < 4:
                                self.tt("dve", f4(Pn), ppn[:, :], f4(Pm), ALU.add, [bppn, bPm], [bPn])
                            else:
                                self.tt("dve", f4(Pb0), ppn[:, :], f4(Pm), ALU.add, [bppn, bPm], [bPb0])
                            yield
                            X, bX, Pm, bPm = Xn, bXn, Pn, bPn
                            if lev < 4:
                                Y, bY = Yn, bYn
                        pq0, bpq0 = self.next_psb()
                        for u in range(4):
                            self.tr(pq0[:, u * 128:(u + 1) * 128], Pb0[:, u, :], self.ident[:], [bPb0, self.b_const], [bpq0])
                        self.cp("act", f4(Qb), pq0[:, 0:512], [bpq0], [bQb])
                        pg2, bpg2 = self.next_psf()
                        for u in range(4):
                            self.mm(pg2[:, u * 128:(u + 1) * 128], E1b[:, u, :], Pb0[:, u, :], True, True, [bE1, bPb0], [bpg2])
                        self.cp("act", f4(Gtb), pg2[:, :], [bpg2], [bGtb])
                        yield
                        pp1, bpp1 = self.next_psf()
                        for u in range(4):
                            self.mm(pp1[:, u * 128:(u + 1) * 128], Qb[:, u, :], Gtb[:, u, :], True, True, [bQb, bGtb], [bpp1])
                        self.tt("dve", f4(Pb1), pp1[:, :], f4(Pb0), ALU.add, [bpp1, bPb0], [bPb1])
                        yield
                        pq1, bpq1 = self.next_psb()
                        for u in range(4):
                            self.tr(pq1[:, u * 128:(u + 1) * 128], Pb1[:, u, :], self.ident[:], [bPb1, self.b_const], [bpq1])
                        self.cp("act", f4(Qb), pq1[:, 0:512], [bpq1], [bQb])
                        pg3, bpg3 = self.next_psf()
                        for u in range(4):
                            self.mm(pg3[:, u * 128:(u + 1) * 128], E2b[:, u, :], Pb1[:, u, :], True, True, [bE2, bPb1], [bpg3])
                        self.cp("act", f4(Gtb), pg3[:, :], [bpg3], [bGtb])
                        yield
                        pp2, bpp2 = self.next_psf()
                        for u in range(4):
                            self.mm(pp2[:, u * 128:(u + 1) * 128], Qb[:, u, :], Gtb[:, u, :], True, True, [bQb, bGtb], [bpp2])
                        self.tt("dve", f4(Ttb), pp2[:, :], f4(Pb1), ALU.add, [bpp2, bPb1], [bTtb])
                        yield
                        if STOP < 6:
                            continue
                        for u in range(4):
                            self.actf(rw[:, u, :], k_tok[:, chs[u], :], AF.Identity, [bqk, bg], [brw], scale=sc(egc, u))
                            self.actf(kd[:, u, :], k_tok[:, chs[u], :], AF.Identity, [bqk, bg], [bkd], scale=sc(ekd, u))
                        pz0, bpz0 = self.next_psf()
                        pz1, bpz1 = self.next_psf()
                        for u in range(4):
                            pz, bpz = (pz0, bpz0) if u < 2 else (pz1, bpz1)
                            uu = u % 2
                            self.mm(pz[:, uu * 256:uu * 256 + 128], Ttb[:, u, :],
                                    v_tok[:, chs[u], vls[u] * 128:(vls[u] + 1) * 128], True, True, [bTtb, bqk], [bpz])
                            self.mm(pz[:, uu * 256 + 128:uu * 256 + 256], Ttb[:, u, :], rw[:, u, :], True, True,
                                    [bTtb, brw], [bpz])
                        for u in range(4):
                            pz, bpz = (pz0, bpz0) if u < 2 else (pz1, bpz1)
                            uu = u % 2
                            self.ts("dve", uf[:, u, :], pz[:, uu * 256:uu * 256 + 128], sc(Be, u), None, ALU.mult, None,
                                    [bpz, bg], [buf_])
                            self.ts("dve", wbt[:, u, :], pz[:, uu * 256 + 128:uu * 256 + 256], sc(Be, u), None, ALU.mult,
                                    None, [bpz, bg], [bwbt])
                        pw, bpw = self.next_psb()
                        for u in range(4):
                            self.tr(pw[:, u * 128:(u + 1) * 128], wbt[:, u, :], self.ident[:], [bwbt, self.b_const], [bpw])
                        self.cp("act", f4(wT), pw[:, 0:512], [bpw], [bwT])
                        yield
                        if STOP < 7:
                            continue
                        p1, bp1 = self.next_psf()
                        for u in range(4):
                            self.mm(p1[:, u * 128:(u + 1) * 128], wT[:, u, :], Sbf[:, u, :], True, True, [bwT, bSbf], [bp1])
                        self.tt("dve", f4(vnew), f4(uf), p1[:, :], ALU.subtract, [buf_, bp1], [bvnew])
                        yield
                        pa, bpa = self.next_psf()
                        for u in range(4):
                            self.mm(pa[:, u * 128:(u + 1) * 128], qT[:, chs[u] * 128:(chs[u] + 1) * 128], Sbf[:, u, :],
                                    True, True, [bqk, bSbf], [bpa])
                        pbm, bpbm = self.next_psf()
                        for u in range(4):
                            self.mm(pbm[:, u * 128:(u + 1) * 128], intraT[:, u, :], vnew[:, u, :], True, True,
                                    [bintra, bvnew], [bpbm])
                        self.cp("act", f4(tB), pbm[:, :], [bpbm], [btB])
                        yield
                        for u in range(4):
                            self.stt("dve", tC[:, u, :], pa[:, u * 128:(u + 1) * 128], sc(egc, u), tB[:, u, :],
                                     ALU.mult, ALU.add, [bpa, bg, btB], [btC])
                        for d in range(2):
                            n = chs[2 * d]
                            ov_ = O[:, n, :, :].rearrange("p v d -> p (v d)")
                            self.tt("pool", ov_, ov_, tC[:, 2 * d:2 * d + 2, :].rearrange("p u d -> p (u d)"), ALU.add,
                                    [btC, bO], [bO])
                        pS, bpS = self.next_psf()
                        for u in range(4):
                            self.mm(pS[:, u * 128:(u + 1) * 128], kd[:, u, :], vnew[:, u, :], True, True, [bkd, bvnew], [bpS])
                        for u in range(4):
                            self.stt("dve", Sf[:, u, :], Sf[:, u, :], sc(ege, u), pS[:, u * 128:(u + 1) * 128],
                                     ALU.mult, ALU.add, [bSf, bg, bpS], [bSf])
                        self.cp("act", f4(Sbf), f4(Sf), [bSf], [bSbf])
                        yield

            def output(hk, LB):
                qT, kT, k_tok, v_tok, O, bqk, bO = LB
                if STOP < 8:
                    return
                with ExitStack() as stk4:
                    Wz = self.sb(stk4, [128, 8, 256], BF16, "Wz")
                    bWz = Buf("Wz")
                    zc = 4096 + 2 * hk * 128
                    self.dma(Wz[:], win[:, :, zc:zc + 256], [self.dbuf["wb_gdn_w_in"]], [bWz])
                    hts = [(self.sb(stk4, [128, 8, 512], BF16, "hTo"), Buf("hTo%d" % i)) for i in range(2)]
                    sz = self.sb(stk4, [128, 256], F32, "sz"); bsz = Buf("sz")
                    sm = self.sb(stk4, [128, 8], F32, "sm"); bsm = Buf("sm")
                    junk = self.sb(stk4, [128, 128], BF16, "junk")
                    on = self.sb(stk4, [128, 2, 128], F32, "on"); bon = Buf("on")
                    ob = self.sb(stk4, [128, 256], BF16, "ob"); bob = Buf("ob")
                    oTs = [(self.sb(stk4, [128, 2, 512], BF16, "oTs"), Buf("oTs%d" % i)) for i in range(2)]
                    ovw = self.oTd[2 * hk * 128:(2 * hk + 2) * 128, :].rearrange("(c p) t -> p c t", p=128)
                    t0 = 0
                    i = 0
                    while t0 < L:
                        n = min(512, L - t0)
                        nsub = n // 128
                        n0 = t0 // 128
                        hT, bh = hts[i % 2]
                        oTt, boTt = oTs[i % 2]
                        i += 1
                        self.dma(hT[:, :, 0:n], hv[:, :, 1 + t0:1 + t0 + n], [self.dbuf["hTd"]], [bh])
                        pb, bpb = self.next_psb()
                        for j in range(nsub):
                            pz, bpz = self.next_psf()
                            for k in range(8):
                                self.mm(pz[:, 0:256], hT[:, k, j * 128:(j + 1) * 128], Wz[:, k, :], k == 0, k == 7,
                                        [bh, bWz], [bpz])
                            self.actf(sz[:], pz[:, 0:256], AF.Silu, [bpz], [bsz])
                            for v in range(2):
                                self.actf(junk[:], O[:, n0 + j, v, :], AF.Square, [bO], [bsm], accum_out=sm[:, v:v + 1])
                            self.ts("dve", sm[:, 2:4], sm[:, 0:2], 1.0 / 128, EPS, ALU.mult, ALU.add, [bsm], [bsm])
                            P.op("act", lambda g: g.sqrt(out=sm[:, 2:4], in_=sm[:, 2:4]), reads=[bsm], writes=[bsm])
                            self.recip(sm[:, 4:6], sm[:, 2:4], [bsm], [bsm])
                            for v in range(2):
                                self.stt("dve", on[:, v, :], O[:, n0 + j, v, :], sm[:, 4 + v:5 + v], nwv[:, v, :],
                                         ALU.mult, ALU.mult, [bO, bsm, bk], [bon])
                            self.tt("pool", ob[:], on[:, :, :].rearrange("p v d -> p (v d)"), sz[:], ALU.mult,
                                    [bon, bsz], [bob])
                            for v in range(2):
                                self.tr(pb[:, v * 512 + j * 128:v * 512 + (j + 1) * 128], ob[:, v * 128:(v + 1) * 128],
                                        self.ident[:], [bob, self.b_const], [bpb])
                        for v in range(2):
                            self.cp("act", oTt[:, v, 0:n], pb[:, v * 512:v * 512 + n], [bpb], [boTt])
                        self.dma(ovw[:, :, t0:t0 + n], oTt[:, :, 0:n], [boTt], [self.dbuf["oTd"]])
                        t0 += n
                P.barrier()

            for hk0 in range(0, 8, lanes):
                if STOP < 2:
                    break
                for li in range(lanes):
                    project(hk0 + li, LBs[li])
                with ExitStack() as stk3:
                    active = [steps(hk0 + li, LBs[li], stk3) for li in range(lanes)]
                    while active:
                        for gen in list(active):
                            try:
                                next(gen)
                            except StopIteration:
                                active.remove(gen)
                P.barrier()
                for li in range(lanes):
                    output(hk0 + li, LBs[li])
        P.barrier()


    def hy_dims(self, L):
        TCN = L // 128
        FC = TCN + 1
        return TCN, FC, FC * 128

    def range_reduce(self, a, kk, b):
        C = 12582912.0
        self.ts("dve", kk, a, 1.0 / (2.0 * math.pi), C, ALU.mult, ALU.add, [b], [b])
        self.ts("dve", kk, kk, C, None, ALU.subtract, None, [b], [b])
        self.stt("dve", a, kk, -2.0 * math.pi, a, ALU.mult, ALU.add, [b], [b])

    def hyena_filter(self, L):
        P = self.P
        TCN, FC, F = self.hy_dims(L)
        cons = self.hyc[L]
        hs_d, hd_d = self.hy_hs[L], self.hy_hd[L]
        bhs = self.dbuf["hy_hs%d" % L]
        bkk = self.dbuf["hy_K%d" % L]
        Kre_d, Kim_d = self.hy_kre[L], self.hy_kim[L]
        TWO_PI = 2.0 * math.pi
        with ExitStack() as stk:
            bk = Buf("fconst")
            fw1 = self.sb(stk, [33, 64], F32, "fw1")
            fw2 = self.sb(stk, [64, 64], F32, "fw2")
            fw3 = self.sb(stk, [64, 2048], F32, "fw3")
            fv = self.sb(stk, [64, 4], F32, "fv")
            self.dma(fw1[:], self.hy_fw1[:, :], [], [bk])
            self.dma(fw2[:], self.hy_fw2[:, :], [], [bk])
            self.dma(fw3[:], self.hy_fw3[:, :], [], [bk])
            for i, src in enumerate((self.hy_fb1, self.hy_ff1, self.hy_fb2, self.hy_ff2)):
                self.dma(fv[:, i:i + 1], src[0:1, :].rearrange("o p -> p o"), [], [bk], slow=True)
            rates = self.sb(stk, [128, 1024], F32, "rates")
            self.dma(rates[:], self.hy_rates[0:1, :].partition_broadcast(128), [], [bk])
            ntn = self.sb(stk, [128, TCN], F32, "ntn")
            self.dma(ntn[:], cons["tn"][:, :], [], [bk])
            self.amul(ntn[:], ntn[:], -1.0, [bk], [bk])
            negpi = self.sb(stk, [128, 1], F32, "negpi")
            self.mset("dve", negpi[:], -math.pi, [bk])
            featsT = self.sb(stk, [33, L], F32, "featsT")
            self.dma(featsT[:], cons["feats"][:, :], [], [bk])
            z2T = self.sb(stk, [64, L], F32, "z2T")
            bz = Buf("z2T")
            a1 = self.sb(stk, [64, 512], F32, "a1")
            z1 = self.sb(stk, [64, 512], F32, "z1")
            kk = self.sb(stk, [64, 512], F32, "kk")
            ba = Buf("a1")
            t0 = 0
            while t0 < L:
                n = min(512, L - t0)
                ps, bp = self.next_psf()
                self.mm(ps[0:64, 0:n], fw1[:, :], featsT[:, t0:t0 + n], True, True, [bk], [bp])
                self.ts("dve", a1[:, 0:n], ps[0:64, 0:n], fv[:, 0:1], fv[:, 1:2], ALU.add, ALU.mult, [bp, bk], [ba])
                self.range_reduce(a1[:, 0:n], kk[:, 0:n], ba)
                self.actf(z1[:, 0:n], a1[:, 0:n], AF.Sin, [ba, bk], [ba], scale=0.999999)
                ps2, bp2 = self.next_psf()
                self.mm(ps2[0:64, 0:n], fw2[:, :], z1[:, 0:n], True, True, [bk, ba], [bp2])
                self.ts("dve", a1[:, 0:n], ps2[0:64, 0:n], fv[:, 2:3], fv[:, 3:4], ALU.add, ALU.mult, [bp2, bk], [ba])
                self.range_reduce(a1[:, 0:n], kk[:, 0:n], ba)
                self.actf(z2T[:, t0:t0 + n], a1[:, 0:n], AF.Sin, [ba, bk], [bz], scale=0.999999)
                t0 += n
            wnd = self.sb(stk, [128, 1024], F32, "wnd")
            bwn = Buf("wnd")
            hfb = self.sb(stk, [128, 2048], F32, "hfb")
            bhf = Buf("hfb")
            hst = [(self.sb(stk, [128, 1024], BF16, "hst"), Buf("hst%d" % i)) for i in range(2)]
            hdt = [(self.sb(stk, [128, 1024], BF16, "hdt"), Buf("hdt%d" % i)) for i in range(2)]
            for tc in range(TCN):
                self.actf(wnd[:], rates[:], AF.Exp, [bk], [bwn], scale=ntn[:, tc:tc + 1])
                for q4 in range(4):
                    ps, bp = self.next_psf()
                    self.mm(ps[:, :], z2T[:, tc * 128:(tc + 1) * 128], fw3[:, q4 * 512:(q4 + 1) * 512], True, True,
                            [bz, bk], [bp])
                    self.tt("dve", hfb[:, q4 * 512:(q4 + 1) * 512], ps[:, :], wnd[:, (q4 % 2) * 512:(q4 % 2 + 1) * 512],
                            ALU.mult, [bp, bwn], [bhf])
                if tc == 0:
                    self.mset("dve", hfb[0:1, 1024:2048], 0.0, [bhf])
                hs_t, bhs_t = hst[tc % 2]
                hd_t, bhd_t = hdt[tc % 2]
                self.tt("pool", hs_t[:], hfb[:, 0:1024], hfb[:, 1024:2048], ALU.add, [bhf], [bhs_t])
                self.tt("pool", hd_t[:], hfb[:, 0:1024], hfb[:, 1024:2048], ALU.subtract, [bhf], [bhd_t])
                self.dma(hs_d[tc * 128:(tc + 1) * 128, :], hs_t[:], [bhs_t], [bhs])
                self.dma(hd_d[tc * 128:(tc + 1) * 128, :], hd_t[:], [bhd_t], [bhs])
        P.barrier()
        with ExitStack() as stk:
            bk = Buf("sconst")
            wN = self.sb(stk, [128, FC], F32, "wN")
            nwN = self.sb(stk, [128, FC], F32, "nwN")
            self.dma(wN[:], cons["wN"][:, :], [], [bk])
            self.amul(nwN[:], wN[:], -1.0, [bk], [bk])
            HS = self.sb(stk, [128, TCN, 512], BF16, "HS")
            HD = self.sb(stk, [128, TCN, 512], BF16, "HD")
            bH = Buf("HS")
            tabs = [(self.sb(stk, [128, TCN, 256], BF16, "tc"), self.sb(stk, [128, TCN, 256], BF16, "tsn"), Buf("tab%d" % i))
                    for i in range(2)]
            kts = [(self.sb(stk, [128, 512], F32, "kre"), self.sb(stk, [128, 512], F32, "kim"), Buf("kt%d" % i))
                   for i in range(2)]
            cv = self.hy_tc[L].rearrange("b p (r c) -> b p r c", r=FC)
            sv = self.hy_ts[L].rearrange("b p (r c) -> b p r c", r=FC)
            ti = 0
            ki = 0
            for chh in range(2):
                self.dma(HS[:], hs_d[:, chh * 512:(chh + 1) * 512].rearrange("(c p) n -> p c n", p=128), [bhs], [bH])
                self.dma(HD[:], hd_d[:, chh * 512:(chh + 1) * 512].rearrange("(c p) n -> p c n", p=128), [bhs], [bH])
                for fb in range(0, FC, 2):
                    nf = min(2, FC - fb)
                    TCb, TSb, btab = tabs[ti % 2]
                    ti += 1
                    self.dma(TCb[:, :, 0:nf * 128], cv[fb // 2][:, 0:TCN, 0:nf * 128], [], [btab])
                    self.dma(TSb[:, :, 0:nf * 128], sv[fb // 2][:, 0:TCN, 0:nf * 128], [], [btab])
                    for fi in range(nf):
                        fc = fb + fi
                        pc, bpc = self.next_psf()
                        for tc in range(TCN):
                            self.mm(pc[:, :], TCb[:, tc, fi * 128:(fi + 1) * 128], HS[:, tc, :], tc == 0, tc == TCN - 1,
                                    [btab, bH], [bpc])
                        pq, bpq = self.next_psf()
                        for tc in range(TCN):
                            self.mm(pq[:, :], TSb[:, tc, fi * 128:(fi + 1) * 128], HD[:, tc, :], tc == 0, tc == TCN - 1,
                                    [btab, bH], [bpq])
                        kre, kim, bkt = kts[ki % 2]
                        ki += 1
                        self.ts("dve", kre[:], pc[:, :], wN[:, fc:fc + 1], None, ALU.mult, None, [bpc, bk], [bkt])
                        self.ts("dve", kim[:], pq[:, :], nwN[:, fc:fc + 1], None, ALU.mult, None, [bpq, bk], [bkt])
                        self.dma(Kre_d[fc * 128:(fc + 1) * 128, chh * 512:(chh + 1) * 512], kre[:], [bkt], [bkk])
                        self.dma(Kim_d[fc * 128:(fc + 1) * 128, chh * 512:(chh + 1) * 512], kim[:], [bkt], [bkk])
        P.barrier()

    def hyena(self, l, s):
        P = self.P
        cfg = self.cfg
        L = cfg.seqs[s]
        TCN, FC, F = self.hy_dims(L)
        CG = 512 if L <= 2048 else 256
        CGC = CG // 128
        TT = min(L, 512 if L <= 2048 else 256)
        hv = self.hTd.rearrange("(c p) t -> p c t", p=128)
        win = self.wb["hy_w_in"].rearrange("(k p) c -> p k c", p=128)
        Kre_d, Kim_d = self.hy_kre[L], self.hy_kim[L]
        bkk = self.dbuf["hy_K%d" % L]
        btabd = self.dbuf["hy_tab%d" % L]
        cv = self.hy_tc[L].rearrange("b p (r c) -> b p r c", r=FC)
        sv = self.hy_ts[L].rearrange("b p (r c) -> b p r c", r=FC)
        with ExitStack() as stk:
            skp = self.sb(stk, [128, 8], F32, "skip")
            cbp = self.sb(stk, [128, 24], F32, "cbp")
            bk = Buf("hconst")
            self.dma(skp[:], self.hy_skip[0:1, :].rearrange("o (c p) -> p (o c)", p=128), [], [bk], slow=True)
            self.dma(cbp[:], self.hy_conv_b[0:1, :].rearrange("o (c p) -> p (o c)", p=128), [], [bk], slow=True)
            vg = self.sb(stk, [128, TCN, CG], BF16, "vg")
            vgT = self.sb(stk, [128, CGC, L], BF16, "vgT")
            x0T = self.sb(stk, [128, CGC, L], BF16, "x0T")
            Yr = self.sb(stk, [128, FC, CG], BF16, "Yr")
            Yi = self.sb(stk, [128, FC, CG], BF16, "Yi")
            bvg = Buf("vg")
            bY = Buf("Y")
            for g in range(1024 // CG):
                c0 = g * CG
                with ExitStack() as stk2:
                    Wraw = self.sb(stk2, [128, 8, 128], BF16, "Wraw")
                    cwb = self.sb(stk2, [128, 3, 128], F32, "cwb")
                    Wts = [self.sb(stk2, [128, 8, 3, 128], BF16, "Wt%d" % i) for i in range(3)]
                    bW = Buf("W")
                    hts = [(self.sb(stk2, [128, 8, 514], BF16, "hTh"), Buf("hTh%d" % i)) for i in range(2)]
                    t1 = self.sb(stk2, [128, 512], F32, "t1")
                    bt1 = Buf("t1")
                    hi = 0
                    for cc in range(CGC):
                        ch0 = c0 + cc * 128
                        cols = [ch0, 1024 + ch0, 2048 + ch0]
                        for pi in range(3):
                            self.dma(Wraw[:], win[:, :, cols[pi]:cols[pi] + 128], [self.dbuf["wb_hy_w_in"]], [bW])
                            for tap in range(3):
                                self.dma(cwb[:, tap, :], self.hy_conv_w[tap:tap + 1, cols[pi]:cols[pi] + 128].partition_broadcast(128),
                                         [], [bW])
                            for k in range(8):
                                for tap in range(3):
                                    self.tt("dve", Wts[pi][:, k, tap, :], Wraw[:, k, :], cwb[:, tap, :], ALU.mult, [bW], [bW])
                        t0 = 0
                        while t0 < L:
                            n = min(512, L - t0)
                            nsub = n // 128
                            n0 = t0 // 128
                            hT, bh = hts[hi % 2]
                            hi += 1
                            self.dma(hT[:, :, 0:n + 2], hv[:, :, t0:t0 + n + 2], [self.dbuf["hTd"]], [bh])
                            pss = []
                            for pi in (1, 2, 0):
                                pt, bp = self.next_psf()
                                idx = 0
                                for tap in range(3):
                                    for k in range(8):
                                        self.mm(pt[:, 0:n], Wts[pi][:, k, tap, :], hT[:, k, tap:tap + n], idx == 0, idx == 23,
                                                [bW, bh], [bp])
                                        idx += 1
                                pss.append((pt, bp))
                            (p1, bp1), (p2, bp2), (p0, bp0) = pss
                            cb = lambda pi: cbp[:, (cols[pi] // 128):(cols[pi] // 128) + 1]
                            self.actf(t1[:, 0:n], p1[:, 0:n], AF.Identity, [bp1, bk], [bt1], bias=cb(1))
                            self.stt("dve", vgT[:, cc, t0:t0 + n], p2[:, 0:n], cb(2), t1[:, 0:n], ALU.add, ALU.mult,
                                     [bp2, bk, bt1], [bvg])
                            self.actf(x0T[:, cc, t0:t0 + n], p0[:, 0:n], AF.Identity, [bp0, bk], [bvg], bias=cb(0))
                            pb, bpb = self.next_psb()
                            for j in range(nsub):
                                self.tr(pb[:, j * 128:(j + 1) * 128], vgT[:, cc, t0 + j * 128:t0 + (j + 1) * 128], self.ident[:],
                                        [bvg, self.b_const], [bpb])
                            self.cp("act", vg[:, n0:n0 + nsub, cc * 128:(cc + 1) * 128],
                                    pb[:, 0:nsub * 128].rearrange("p (j d) -> p j d", d=128), [bpb], [bvg])
                            t0 += n
                P.barrier()
                with ExitStack() as stk2:
                    tabs = [(self.sb(stk2, [128, TCN, 256], BF16, "tc"), self.sb(stk2, [128, TCN, 256], BF16, "tsn"),
                             Buf("tab%d" % i)) for i in range(2)]
                    kts = [(self.sb(stk2, [128, CG], F32, "kre"), self.sb(stk2, [128, CG], F32, "kim"), Buf("kt%d" % i))
                           for i in range(2)]
                    tms = [[self.sb(stk2, [128, CG], F32, "tm%d" % j) for j in range(4)] for i in range(2)]
                    btms = [Buf("tm%d" % i) for i in range(2)]
                    ti = 0
                    ki = 0
                    for fb in range(0, FC, 2):
                        nf = min(2, FC - fb)
                        TCb, TSb, btab = tabs[ti % 2]
                        ti += 1
                        self.dma(TCb[:, :, 0:nf * 128], cv[fb // 2][:, 0:TCN, 0:nf * 128], [], [btab])
                        self.dma(TSb[:, :, 0:nf * 128], sv[fb // 2][:, 0:TCN, 0:nf * 128], [], [btab])
                        for fi in range(nf):
                            fc = fb + fi
                            kre, kim, bkt = kts[ki % 2]
                            tm = tms[ki % 2]
                            btm = btms[ki % 2]
                            ki += 1
                            self.dma(kre[:], Kre_d[fc * 128:(fc + 1) * 128, c0:c0 + CG], [bkk], [bkt])
                            self.dma(kim[:], Kim_d[fc * 128:(fc + 1) * 128, c0:c0 + CG], [bkk], [bkt])
                            pa, bpa = self.next_psf()
                            for tc in range(TCN):
                                self.mm(pa[:, 0:CG], TCb[:, tc, fi * 128:(fi + 1) * 128], vg[:, tc, :], tc == 0, tc == TCN - 1,
                                        [btab, bvg], [bpa])
                            pbq, bpbq = self.next_psf()
                            for tc in range(TCN):
                                self.mm(pbq[:, 0:CG], TSb[:, tc, fi * 128:(fi + 1) * 128], vg[:, tc, :], tc == 0, tc == TCN - 1,
                                        [btab, bvg], [bpbq])
                            self.tt("dve", tm[0][:], pa[:, 0:CG], kre[:], ALU.mult, [bpa, bkt], [btm])
                            self.tt("dve", tm[1][:], pbq[:, 0:CG], kim[:], ALU.mult, [bpbq, bkt], [btm])
                            self.tt("dve", tm[2][:], pbq[:, 0:CG], kre[:], ALU.mult, [bpbq, bkt], [btm])
                            self.tt("dve", tm[3][:], pa[:, 0:CG], kim[:], ALU.mult, [bpa, bkt], [btm])
                            self.tt("pool", Yr[:, fc, :], tm[0][:], tm[1][:], ALU.add, [btm], [bY])
                            self.tt("pool", Yi[:, fc, :], tm[2][:], tm[3][:], ALU.subtract, [btm], [bY])
                P.barrier()
                with ExitStack() as stk2:
                    tabs = [(self.sb(stk2, [128, FC, TT], BF16, "ic"), self.sb(stk2, [128, FC, TT], BF16, "isn"),
                             Buf("itab%d" % i)) for i in range(2)]
                    yts = [(self.sb(stk2, [128, TT], F32, "yt"), Buf("yt%d" % i)) for i in range(2)]
                    ots = [(self.sb(stk2, [128, CGC, TT], BF16, "ot"), Buf("ot%d" % i)) for i in range(2)]
                    ovw = self.oTd[c0:c0 + CG, :].rearrange("(c p) t -> p c t", p=128)
                    ti = 0
                    yi = 0
                    for t0 in range(0, L, TT):
                        TCb, TSb, btab = tabs[ti % 2]
                        ot, bot = ots[ti % 2]
                        ti += 1
                        for jb in range(TT // 256):
                            self.dma(TCb[:, :, jb * 256:(jb + 1) * 256], cv[t0 // 256 + jb][:, 0:FC, :], [], [btab])
                            self.dma(TSb[:, :, jb * 256:(jb + 1) * 256], sv[t0 // 256 + jb][:, 0:FC, :], [], [btab])
                        for cc in range(CGC):
                            py, bpy = self.next_psf()
                            for fc in range(FC):
                                self.mm(py[:, 0:TT], Yr[:, fc, cc * 128:(cc + 1) * 128], TCb[:, fc, :], fc == 0, False,
                                        [bY, btab], [bpy])
                                self.mm(py[:, 0:TT], Yi[:, fc, cc * 128:(cc + 1) * 128], TSb[:, fc, :], False, fc == FC - 1,
                                        [bY, btab], [bpy])
                            yt, byt = yts[yi % 2]
                            yi += 1
                            chn = (c0 // 128) + cc
                            self.stt("dve", yt[:], vgT[:, cc, t0:t0 + TT], skp[:, chn:chn + 1], py[:, 0:TT], ALU.mult, ALU.add,
                                     [bvg, bk, bpy], [byt])
                            self.tt("pool", ot[:, cc, :], yt[:], x0T[:, cc, t0:t0 + TT], ALU.mult, [byt, bvg], [bot])
                        self.dma(ovw[:, :, t0:t0 + TT], ot[:], [bot], [self.dbuf["oTd"]])
                P.barrier()
        P.barrier()


def swa_bias_table():
    W = 128
    out = np.zeros((4, 3, 128, 512), np.float32)
    slopes = (2.0 ** (-8.0 * np.arange(1, 17, dtype=np.float32) / 16)).reshape(4, 4)
    k = np.arange(128)[:, None]
    q = np.arange(128)[None, :]
    for c in range(3):
        rel = (k + (c - 1) * W) - q
        dist = np.abs(rel).astype(np.float32)
        for hk in range(4):
            for g in range(4):
                b = -slopes[hk, g] * dist
                b = np.where(np.abs(rel) <= W, b, -30000.0)
                out[hk, c, :, g * 128:(g + 1) * 128] = b
    return out


def hy_rates():
    r = np.abs(np.linspace(math.log(1e-2) / 1.5, math.log(1e-2) / 0.3, 1024, dtype=np.float32))
    return r.reshape(1, 1024).astype(np.float32)


_HY_CACHE = {}


def hy_tables(L):
    if L in _HY_CACHE:
        return _HY_CACHE[L]
    TCN = L // 128
    FC = TCN + 1
    F = FC * 128
    N = 2 * L
    pos = np.arange(L, dtype=np.float32)[:, None]
    t = pos / np.float32(max(L - 1, 1))
    freqs = np.linspace(1e-4, 15, 16, dtype=np.float32)[None, :]
    ang = freqs * np.float32(2.0 * math.pi / L) * pos
    feats = np.concatenate([t, np.cos(ang), -np.sin(ang)], axis=-1).astype(np.float32)
    tn = (np.arange(L, dtype=np.float32) / np.float32(max(L - 1, 1))).reshape(TCN, 128).T
    a = np.arange(F, dtype=np.int64)
    prod = (a[:, None] * a[None, :]) % N
    base = 2.0 * np.pi * np.arange(N, dtype=np.float64) / N
    cosv = np.cos(base).astype(np.float32)
    sinv = np.sin(base).astype(np.float32)
    wf = np.full(F, 2.0 / N, np.float64)
    wf[0] = 1.0 / N
    wf[L] = 1.0 / N
    wf[L + 1:] = 0.0
    out = {"hy_tn%d" % L: np.ascontiguousarray(tn, np.float32), "hy_feats%d" % L: np.ascontiguousarray(feats.T),
           "hy_wN%d" % L: np.ascontiguousarray(wf.astype(np.float32).reshape(FC, 128).T),
           "hy_cos%d" % L: cosv[prod], "hy_sin%d" % L: sinv[prod]}
    _HY_CACHE[L] = out
    return out


def gdn_consts():
    j = np.arange(128)[:, None]
    i = np.arange(128)[None, :]
    c = np.zeros((9, 128, 128), np.float32)
    c[6] = (j // 32 == i // 32)
    c[7] = (j // 64 == i // 64) & (j // 32 != i // 32)
    c[8] = (j // 64 != i // 64)
    c[0] = (j <= i)
    c[1] = (j >= i)
    c[2] = np.where(i >= j, 0.0, -30000.0)
    c[3] = np.where(i <= j, 0.0, -30000.0)
    c[4] = (i > j)
    c[5] = (i < j)
    e = np.zeros((48, 32, 128), np.float32)
    for u in range(32):
        r = u if u < 16 else 32 + (u - 16)
        e[r, u, :] = 1.0
    return c, e


def rope_table(L):
    inv = 10000.0 ** (-np.arange(0, 32, 2, dtype=np.float32) / 32)
    ang = np.arange(L, dtype=np.float32)[None, :] * inv[:, None]
    cs = np.zeros((2, 96, L), np.float32)
    cs[0, 0:64] = 1.0
    cs[0, 64:80] = np.cos(ang)
    cs[0, 80:96] = np.cos(ang)
    cs[1, 64:80] = np.sin(ang)
    cs[1, 80:96] = np.sin(ang)
    return cs


def make_inputs(cfg, core_x, core_c, weights):
    m = {"xin": np.ascontiguousarray(core_x, np.float32), "cin": np.ascontiguousarray(core_c, np.float32),
         "c_ident": np.eye(128, dtype=np.float32)}
    for n in ("ada_w", "ada_b", "norm_w", "ffn_w_gu", "ffn_w_down"):
        m[n] = np.ascontiguousarray(weights[n][:cfg.depth], np.float32)
    m["final_norm_w"] = np.ascontiguousarray(weights["final_norm_w"].reshape(1, D), np.float32)
    kinds = set(cfg.kinds)
    if 2 in kinds:
        m["swa_w_qkv"] = np.ascontiguousarray(weights["swa_w_qkv"][0])
        m["swa_w_out"] = np.ascontiguousarray(weights["swa_w_out"][0])
        m["swa_sink"] = np.ascontiguousarray(weights["swa_sink"][0:1])
        m["swa_bias"] = swa_bias_table()
    if 0 in kinds:
        for n in ("hy_w_in", "hy_w_out", "hy_conv_w", "hy_filt_w1", "hy_filt_w2", "hy_filt_w3"):
            m[n] = np.ascontiguousarray(weights[n][0])
        for n in ("hy_conv_b", "hy_filt_b1", "hy_filt_freq1", "hy_filt_b2", "hy_filt_freq2", "hy_skip"):
            m[n] = np.ascontiguousarray(weights[n][0:1])
        m["hy_rates"] = hy_rates()
        for L in sorted(set(cfg.seqs)):
            for k_, v_ in hy_tables(L).items():
                m[k_] = v_
    if 1 in kinds:
        for n in ("gdn_w_in", "gdn_w_ab", "gdn_w_out", "gdn_conv_w"):
            m[n] = np.ascontiguousarray(weights[n][0])
        m["gdn_conv_b"] = np.ascontiguousarray(weights["gdn_conv_b"][0:1])
        m["gdn_a_log"] = np.ascontiguousarray(weights["gdn_a_log"][0].reshape(1, 32))
        m["gdn_dt_bias"] = np.ascontiguousarray(weights["gdn_dt_bias"][0].reshape(1, 32))
        m["gdn_norm_w"] = np.ascontiguousarray(weights["gdn_norm_w"][0:1])
        m["gdn_c"], m["gdn_esel"] = gdn_consts()
    if 3 in kinds:
        for n in ("mla_w_down", "mla_w_uq", "mla_w_ukv", "mla_w_out"):
            m[n] = np.ascontiguousarray(weights[n][0])
        m["mla_q_norm_w"] = np.ascontiguousarray(weights["mla_q_norm_w"][0:1])
        m["mla_kv_norm_w"] = np.ascontiguousarray(weights["mla_kv_norm_w"][0:1])
        m["rope_cs"] = rope_table(cfg.lmax)
    return m


_NC_CACHE = {}


def get_nc(cfg):
    key = (tuple(cfg.seqs), tuple(cfg.kinds))
    if key not in _NC_CACHE:
        _NC_CACHE[key] = Builder(cfg).build()
    return _NC_CACHE[key]


def kernel(**inputs):
    x_prompt = np.asarray(inputs["x_prompt"], np.float32)
    x_sample = np.asarray(inputs["x_sample"], np.float32)
    c_prompt = np.asarray(inputs["c_prompt"], np.float32)
    c_sample = np.asarray(inputs["c_sample"], np.float32)
    B, Lp, _ = x_prompt.shape
    Bs, Ls, _ = x_sample.shape
    NP = B // NCORES
    cfg = Cfg([Lp] * NP + [Ls], [0, 1, 2, 3])
    nc = get_nc(cfg)
    in_maps = []
    for c in range(NCORES):
        xs = [x_prompt[c * NP + i] for i in range(NP)] + [x_sample[c % Bs]]
        cs = [c_prompt[c * NP + i] for i in range(NP)] + [c_sample[c % Bs]]
        in_maps.append(make_inputs(cfg, np.concatenate(xs, 0), np.stack(cs, 0), inputs))
    res = run_bass_kernel_spmd(nc, in_maps, core_ids=list(range(NCORES)))
    y_prompt = np.zeros_like(x_prompt)
    y_sample = np.zeros_like(x_sample)
    for c in range(NCORES):
        y = res.results[c]["yout"]
        for i in range(NP):
            y_prompt[c * NP + i] = y[i * Lp:(i + 1) * Lp]
        if c < Bs:
            y_sample[c] = y[NP * Lp:NP * Lp + Ls]
    return (y_prompt, y_sample)
```
